# Optimizing a Trainium2 kernel written in Bass

```python
import math
import jax
import jax.numpy as jnp
from jax import lax
import numpy as np

D_MODEL = 1024
BATCH = 32
SEQ = 2048
DEPTH = 2

CTX_LEN = 256
GRID_W = 64
BLOCK_Q = 128
ROPE_THETA = 10000.0
EPS = 1e-6

GROUP_W = D_MODEL // 4
D_MIX = 4 * GROUP_W

POOL_WINDOWS = (2, 4, 8, 16)
POOL_GC = GROUP_W // len(POOL_WINDOWS)

GQA_HEADS = 4
GQA_KV_HEADS = 2
GQA_HD = GROUP_W // GQA_HEADS

HY_ORDER = 2
HY_C = GROUP_W
HY_BANDS = 8
HY_EMB = 1 + 2 * HY_BANDS
HY_FW = 64
HY_DIRS = 2
HY_DECAY_TARGET = 1e-2
HY_FAST_PCT = 0.3
HY_SLOW_PCT = 1.5

MLA_HEADS = 4
MLA_Q_RANK = D_MODEL // 4
MLA_KV_RANK = D_MODEL // 8
MLA_NOPE = GROUP_W // MLA_HEADS
MLA_ROPE = MLA_NOPE // 2
MLA_V = GROUP_W // MLA_HEADS

N_EXPERTS = 16
EC_CAPACITY = 2
EXPERT_FF = D_MODEL // 2

COL_SIZES = (GQA_KV_HEADS * GQA_HD, GQA_KV_HEADS * GQA_HD, MLA_KV_RANK, MLA_ROPE,
             GQA_HEADS * GQA_HD, MLA_Q_RANK, GROUP_W, (HY_ORDER + 1) * HY_C)
COL_SPLITS = tuple(sum(COL_SIZES[:i + 1]) for i in range(len(COL_SIZES) - 1))
IN_COLS = sum(COL_SIZES)
CTX_KV_COLS = sum(COL_SIZES[:4])

kernel_name = 'hybrid_parallel_group_dit_block'


def rms_norm(x, g):
    xf = x.astype(jnp.float32)
    y = xf * lax.rsqrt(jnp.mean(xf * xf, axis=-1, keepdims=True) + EPS)
    return (y * g.astype(jnp.float32)).astype(x.dtype)


def modulate(h, shift, scale):
    return h * (1 + scale) + shift


def _rotate(x, pos):
    m = x.shape[-1] // 2
    inv = ROPE_THETA ** (-jnp.arange(m, dtype=jnp.float32) / m)
    ang = pos[:, None] * inv[None, :]
    cos = jnp.cos(ang)[None, :, None, :].astype(x.dtype)
    sin = jnp.sin(ang)[None, :, None, :].astype(x.dtype)
    x1, x2 = x[..., :m], x[..., m:]
    return jnp.concatenate([x1 * cos - x2 * sin, x1 * sin + x2 * cos], axis=-1)


def axial_rope(x, row, col):
    half = x.shape[-1] // 2
    return jnp.concatenate([_rotate(x[..., :half], row), _rotate(x[..., half:], col)], axis=-1)


def attend(q, k, v, scale):
    s = jnp.einsum('bqhgd,bkhd->bhgqk', q, k).astype(jnp.float32) * scale
    p = jax.nn.softmax(s, axis=-1).astype(v.dtype)
    return jnp.einsum('bhgqk,bkhd->bqhgd', p, v)


def block_sweep_attention(q, k, v, scale):
    b, n, kvh, g, dk = q.shape
    nb = n // BLOCK_Q
    qb = jnp.moveaxis(q.reshape(b, nb, BLOCK_Q, kvh, g, dk), 1, 0)
    ob = lax.map(lambda qi: attend(qi, k, v, scale), qb)
    return jnp.moveaxis(ob, 0, 1).reshape(b, n, -1)


def gqa_q(q_raw, g_q, pos):
    b, n, _ = q_raw.shape
    q = rms_norm(q_raw.reshape(b, n, GQA_HEADS, GQA_HD), g_q)
    if pos is not None:
        q = axial_rope(q, *pos)
    return q.reshape(b, n, GQA_KV_HEADS, GQA_HEADS // GQA_KV_HEADS, GQA_HD)


def gqa_kv(k_raw, v_raw, g_k, pos):
    b, n, _ = k_raw.shape
    k = rms_norm(k_raw.reshape(b, n, GQA_KV_HEADS, GQA_HD), g_k)
    if pos is not None:
        k = axial_rope(k, *pos)
    return k, v_raw.reshape(b, n, GQA_KV_HEADS, GQA_HD)


def mla_q(cq, g_cq, w_uq, g_mq, pos):
    b, n, _ = cq.shape
    q = (rms_norm(cq, g_cq) @ w_uq).reshape(b, n, MLA_HEADS, MLA_NOPE + MLA_ROPE)
    q = rms_norm(q, g_mq)
    if pos is not None:
        q = jnp.concatenate([q[..., :MLA_NOPE], axial_rope(q[..., MLA_NOPE:], *pos)], axis=-1)
    return q[:, :, :, None, :]


def mla_kv(ckv, kr, g_ckv, w_ukv, g_mk, pos):
    b, n, _ = ckv.shape
    kv = (rms_norm(ckv, g_ckv) @ w_ukv).reshape(b, n, MLA_HEADS, MLA_NOPE + MLA_V)
    k_nope, v = kv[..., :MLA_NOPE], kv[..., MLA_NOPE:]
    k_rope = jnp.broadcast_to(kr[:, :, None, :], (b, n, MLA_HEADS, MLA_ROPE))
    k = rms_norm(jnp.concatenate([k_nope, k_rope], axis=-1), g_mk)
    if pos is not None:
        k = jnp.concatenate([k[..., :MLA_NOPE], axial_rope(k[..., MLA_NOPE:], *pos)], axis=-1)
    return k, v


def multiscale_pool(u, w_pool, scale):
    b, n, _ = u.shape
    uf = u.astype(jnp.float32)
    cs = jnp.concatenate([jnp.zeros((b, 1, GROUP_W), jnp.float32), jnp.cumsum(uf, axis=1)], axis=1)
    t = jnp.arange(n)
    diffs = []
    for gi, w in enumerate(POOL_WINDOWS):
        left = w // 2
        right = w - 1 - left
        lo = jnp.clip(t - left, 0, n)
        hi = jnp.clip(t + right + 1, 0, n)
        sl = slice(gi * POOL_GC, (gi + 1) * POOL_GC)
        csg = cs[:, :, sl]
        cnt = (hi - lo).astype(jnp.float32)[None, :, None]
        mean = (jnp.take(csg, hi, axis=1) - jnp.take(csg, lo, axis=1)) / cnt
        diffs.append(mean - uf[:, :, sl])
    d = jnp.stack(diffs, axis=2).astype(u.dtype)
    y = jnp.einsum('bngc,gce->bnge', d, w_pool).reshape(b, n, GROUP_W)
    return y * scale


def short_conv3(u, w, bias):
    up = jnp.pad(u, ((0, 0), (1, 1), (0, 0)))
    return up[:, :-2] * w[0] + up[:, 1:-1] * w[1] + up[:, 2:] * w[2] + bias


def hyena_filters(n, w1, b1, w2, b2, w3, freq):
    f32 = jnp.float32
    t = jnp.linspace(0.0, 1.0, n, dtype=f32)[:, None]
    lag = jnp.arange(n, dtype=f32)[:, None]
    bands = jnp.linspace(1e-4, HY_BANDS - 1, HY_BANDS, dtype=f32)[None, :]
    ang = (2.0 * math.pi / n) * lag * bands
    feats = jnp.concatenate([t, jnp.cos(ang), -jnp.sin(ang)], axis=-1)
    fr = freq.astype(f32)
    hid = jnp.sin(fr * (feats @ w1.astype(f32) + b1.astype(f32)))
    hid = jnp.sin(fr * (hid @ w2.astype(f32) + b2.astype(f32)))
    h = (hid @ w3.astype(f32)).reshape(n, HY_DIRS, HY_ORDER, HY_C)
    deltas = jnp.abs(jnp.linspace(math.log(HY_DECAY_TARGET) / HY_SLOW_PCT,
                                  math.log(HY_DECAY_TARGET) / HY_FAST_PCT, HY_C, dtype=f32))
    h = h * jnp.exp(-t * deltas[None, :])[:, None, None, :]
    fwd, bwd = h[:, 0], h[:, 1]
    filt = jnp.concatenate([fwd, jnp.zeros((1, HY_ORDER, HY_C), f32), bwd[:0:-1]], axis=0)
    return filt * lax.rsqrt(jnp.sum(filt * filt, axis=0, keepdims=True) + EPS)


def long_conv(z, filt, bias):
    n = z.shape[1]
    zf = jnp.fft.rfft(z.astype(jnp.float32), n=2 * n, axis=1)
    hf = jnp.fft.rfft(filt, n=2 * n, axis=0)
    y = jnp.fft.irfft(zf * hf[None], n=2 * n, axis=1)[:, :n]
    return (y + z.astype(jnp.float32) * bias.astype(jnp.float32)).astype(z.dtype)


def hyena(u, conv_w, conv_b, filt, bias):
    v, x1, x2 = jnp.split(short_conv3(u, conv_w, conv_b), 3, axis=-1)
    z = x1 * long_conv(v, filt[:, 0], bias[0])
    return x2 * long_conv(z, filt[:, 1], bias[1])


def expert_choice_ffn(h, w_router, w_gate, w_up, w_down):
    b, n, d = h.shape
    cap = EC_CAPACITY * n // N_EXPERTS
    aff = jax.nn.softmax((h @ w_router).astype(jnp.float32), axis=-1)
    gate, idx = lax.top_k(jnp.swapaxes(aff, 1, 2), cap)
    xg = jax.vmap(lambda hb, ib: hb[ib])(h, idx)
    a = jnp.einsum('becd,edf->becf', xg, w_gate)
    u = jnp.einsum('becd,edf->becf', xg, w_up)
    y = jnp.einsum('becf,efd->becd', jax.nn.silu(a) * u, w_down) * gate[..., None].astype(h.dtype)
    scatter = lambda ib, yb: jnp.zeros((n, d), h.dtype).at[ib.reshape(-1)].add(yb.reshape(-1, d))
    return jax.vmap(scatter)(idx, y)


def setup_inputs(seed: int = 0) -> dict:
    key = jax.random.key(seed)
    keys = iter(jax.random.split(key, 40))
    f32 = jnp.float32

    def nrm(shape, scale):
        return scale * jax.random.normal(next(keys), shape, f32)

    def gain(shape):
        return 1.0 + nrm(shape, 0.05)

    L = DEPTH
    return {
        'x': nrm((BATCH, SEQ, D_MODEL), 1.0),
        'c': nrm((BATCH, D_MODEL), 1.0),
        'ctx': nrm((BATCH, CTX_LEN, D_MODEL), 1.0),
        'c_ctx': nrm((D_MODEL,), 1.0),
        'norm1_g': gain((L, D_MODEL)),
        'norm2_g': gain((L, D_MODEL)),
        'w_mod': nrm((L, D_MODEL, 6 * D_MODEL), 0.5 * D_MODEL ** -0.5),
        'b_mod': nrm((L, 6 * D_MODEL), 0.02),
        'w_in': nrm((L, D_MODEL, IN_COLS), D_MODEL ** -0.5),
        'w_out': nrm((L, D_MIX, D_MODEL), D_MIX ** -0.5),
        'pool_w': nrm((L, len(POOL_WINDOWS), POOL_GC, POOL_GC), POOL_GC ** -0.5),
        'pool_scale': gain((L, GROUP_W)),
        'gqa_qnorm_g': gain((L, GQA_HD)),
        'gqa_knorm_g': gain((L, GQA_HD)),
        'hy_conv_w': nrm((L, 3, (HY_ORDER + 1) * HY_C), 3 ** -0.5),
        'hy_conv_b': nrm((L, (HY_ORDER + 1) * HY_C), 0.02),
        'hy_f_w1': nrm((L, HY_EMB, HY_FW), 1.0),
        'hy_f_b1': nrm((L, HY_FW), 0.02),
        'hy_f_w2': nrm((L, HY_FW, HY_FW), HY_FW ** -0.5),
        'hy_f_b2': nrm((L, HY_FW), 0.02),
        'hy_f_w3': nrm((L, HY_FW, HY_DIRS * HY_ORDER * HY_C), HY_FW ** -0.5),
        'hy_freq': gain((L, HY_FW)),
        'hy_bias': nrm((L, HY_ORDER, HY_C), 0.5),
        'mla_cq_g': gain((L, MLA_Q_RANK)),
        'mla_w_uq': nrm((L, MLA_Q_RANK, MLA_HEADS * (MLA_NOPE + MLA_ROPE)), MLA_Q_RANK ** -0.5),
        'mla_ckv_g': gain((L, MLA_KV_RANK)),
        'mla_w_ukv': nrm((L, MLA_KV_RANK, MLA_HEADS * (MLA_NOPE + MLA_V)), MLA_KV_RANK ** -0.5),
        'mla_qnorm_g': gain((L, MLA_NOPE + MLA_ROPE)),
        'mla_knorm_g': gain((L, MLA_NOPE + MLA_ROPE)),
        'router_w': nrm((L, D_MODEL, N_EXPERTS), D_MODEL ** -0.5),
        'exp_w_gate': nrm((L, N_EXPERTS, D_MODEL, EXPERT_FF), D_MODEL ** -0.5),
        'exp_w_up': nrm((L, N_EXPERTS, D_MODEL, EXPERT_FF), D_MODEL ** -0.5),
        'exp_w_down': nrm((L, N_EXPERTS, EXPERT_FF, D_MODEL), EXPERT_FF ** -0.5),
    }


def reference(x, c, ctx, c_ctx, norm1_g, norm2_g, w_mod, b_mod, w_in, w_out, pool_w, pool_scale,
              gqa_qnorm_g, gqa_knorm_g, hy_conv_w, hy_conv_b, hy_f_w1, hy_f_b1, hy_f_w2, hy_f_b2,
              hy_f_w3, hy_freq, hy_bias, mla_cq_g, mla_w_uq, mla_ckv_g, mla_w_ukv, mla_qnorm_g,
              mla_knorm_g, router_w, exp_w_gate, exp_w_up, exp_w_down):
    b, n, _ = x.shape
    n_ctx = ctx.shape[1]
    rows = n // GRID_W
    row = jnp.repeat(jnp.arange(rows, dtype=jnp.float32), GRID_W)
    col = jnp.tile(jnp.arange(GRID_W, dtype=jnp.float32), rows)
    pos = (row, col)
    gqa_scale = GQA_HD ** -0.5
    mla_scale = (MLA_NOPE + MLA_ROPE) ** -0.5
    xc = ctx
    for l in range(DEPTH):
        last = l == DEPTH - 1
        sh1, sc1, g1, sh2, sc2, g2 = jnp.split((jax.nn.silu(c) @ w_mod[l] + b_mod[l])[:, None, :], 6, axis=-1)
        modc = jnp.split(jax.nn.silu(c_ctx) @ w_mod[l] + b_mod[l], 6)
        h = modulate(rms_norm(x, norm1_g[l]), sh1, sc1)
        hc = modulate(rms_norm(xc, norm1_g[l]), modc[0], modc[1])
        p_k, p_v, p_ckv, p_kr, p_q, p_cq, p_pool, p_hy = jnp.split(h @ w_in[l], COL_SPLITS, axis=-1)
        if last:
            pc = jnp.split(hc @ w_in[l][:, :CTX_KV_COLS], COL_SPLITS[:3], axis=-1)
        else:
            pc = jnp.split(hc @ w_in[l], COL_SPLITS, axis=-1)
        kc, vc = gqa_kv(pc[0], pc[1], gqa_knorm_g[l], None)
        mkc, mvc = mla_kv(pc[2], pc[3], mla_ckv_g[l], mla_w_ukv[l], mla_knorm_g[l], None)
        kl, vl = gqa_kv(p_k, p_v, gqa_knorm_g[l], pos)
        o_gqa = block_sweep_attention(gqa_q(p_q, gqa_qnorm_g[l], pos),
                                      jnp.concatenate([kl, kc], axis=1), jnp.concatenate([vl, vc], axis=1), gqa_scale)
        mkl, mvl = mla_kv(p_ckv, p_kr, mla_ckv_g[l], mla_w_ukv[l], mla_knorm_g[l], pos)
        o_mla = block_sweep_attention(mla_q(p_cq, mla_cq_g[l], mla_w_uq[l], mla_qnorm_g[l], pos),
                                      jnp.concatenate([mkl, mkc], axis=1), jnp.concatenate([mvl, mvc], axis=1), mla_scale)
        o_pool = multiscale_pool(p_pool, pool_w[l], pool_scale[l])
        filt = hyena_filters(n, hy_f_w1[l], hy_f_b1[l], hy_f_w2[l], hy_f_b2[l], hy_f_w3[l], hy_freq[l])
        o_hy = hyena(p_hy, hy_conv_w[l], hy_conv_b[l], filt, hy_bias[l])
        mixed = jnp.concatenate([o_pool, o_gqa, o_hy, o_mla], axis=-1) @ w_out[l]
        x_new = x + g1 * mixed
        x_new = x_new + g2 * expert_choice_ffn(modulate(rms_norm(x_new, norm2_g[l]), sh2, sc2),
                                               router_w[l], exp_w_gate[l], exp_w_up[l], exp_w_down[l])
        if not last:
            oc_gqa = attend(gqa_q(pc[4], gqa_qnorm_g[l], None), kc, vc, gqa_scale).reshape(b, n_ctx, -1)
            oc_mla = attend(mla_q(pc[5], mla_cq_g[l], mla_w_uq[l], mla_qnorm_g[l], None),
                            mkc, mvc, mla_scale).reshape(b, n_ctx, -1)
            oc_pool = multiscale_pool(pc[6], pool_w[l], pool_scale[l])
            filt_c = hyena_filters(n_ctx, hy_f_w1[l], hy_f_b1[l], hy_f_w2[l], hy_f_b2[l], hy_f_w3[l], hy_freq[l])
            oc_hy = hyena(pc[7], hy_conv_w[l], hy_conv_b[l], filt_c, hy_bias[l])
            mixed_c = jnp.concatenate([oc_pool, oc_gqa, oc_hy, oc_mla], axis=-1) @ w_out[l]
            xc = xc + modc[2] * mixed_c
            xc = xc + modc[5] * expert_choice_ffn(modulate(rms_norm(xc, norm2_g[l]), modc[3], modc[4]),
                                                  router_w[l], exp_w_gate[l], exp_w_up[l], exp_w_down[l])
        x = x_new
    return x
```

```python
import math
import numpy as np
import ml_dtypes
import concourse.bass as bass
import concourse.mybir as mybir
from concourse.bass_utils import run_bass_kernel_spmd

F32 = mybir.dt.float32
BF16 = mybir.dt.bfloat16
ALU = mybir.AluOpType
AF = mybir.ActivationFunctionType
AX = mybir.AxisListType
NPBF = ml_dtypes.bfloat16

ENGS = ("pe", "act", "dve", "pool", "sp")
N_DMA_SEMS = 48

D = 1024
SEQ = 2048
NCTX = 256
DEPTH = 2
NB = 4
NEXP = 16
EPS = 1e-6
MAGIC = 12582912.0
TWO_PI = 2.0 * math.pi


class Tile:
    def __init__(self, name, handle, space, const=False):
        self.name = name
        self.h = handle
        self.space = space
        self.const = const
        self.last_write = None
        self.reads = []
        self.sem_idx = None

    def __getitem__(self, idx):
        return V(self, self.h[idx])


class V:
    def __init__(self, tile, ap):
        self.tile = tile
        self.ap = ap

    def __getitem__(self, idx):
        return V(self.tile, self.ap[idx])

    def re(self, pat, **kw):
        return V(self.tile, self.ap.rearrange(pat, **kw))

    def bc(self, shape):
        return V(self.tile, self.ap.to_broadcast(list(shape)))


class Rec:
    __slots__ = ("eng", "fn", "deps", "signaled", "count", "is_dma", "sem_idx")

    def __init__(self, eng, fn, deps, is_dma=False):
        self.eng = eng
        self.fn = fn
        self.deps = deps
        self.signaled = False
        self.count = None
        self.is_dma = is_dma
        self.sem_idx = None


class StopBuild(Exception):
    pass


class Scope:
    def __init__(self, P):
        self.P = P

    def __enter__(self):
        self.P.scope_stack.append([])
        return self

    def __exit__(self, *a):
        self.P.barrier()
        for cm in reversed(self.P.scope_stack.pop()):
            cm.__exit__(None, None, None)
        return False


class Prog:
    def __init__(self, nc, same_engine_sync=True):
        self.nc = nc
        self.recs = []
        self.same_engine_sync = same_engine_sync
        self.scope_stack = [[]]
        self.bar_deps = {e: [] for e in ENGS}
        self.bar_start = 0
        self.next_sem = 0
        self.uid = 0
        self.stop_at = None
        self.ckpts = []

    def checkpoint(self, name):
        self.ckpts.append((name, len(self.recs)))
        if self.stop_at is not None and name == self.stop_at:
            raise StopBuild()

    def scope(self):
        return Scope(self)

    def _enter(self, cm):
        v = cm.__enter__()
        self.scope_stack[-1].append(cm)
        return v

    def _name(self, name):
        self.uid += 1
        return "%s_%d" % (name, self.uid)

    def sbuf(self, name, shape, dt):
        h = self._enter(self.nc.sbuf_tensor(self._name(name), list(shape), dt))
        return Tile(name, h, "sbuf")

    def psum(self, name, shape, dt=F32):
        h = self._enter(self.nc.psum_tensor(self._name(name), list(shape), dt))
        return Tile(name, h, "psum")

    def dram(self, name, shape, dt, kind="Internal"):
        h = self.nc.dram_tensor(name, list(shape), dt, kind=kind)
        return Tile(name, h, "dram", const=(kind == "ExternalInput"))

    def barrier(self):
        last = {}
        dmas = []
        for r in self.recs[self.bar_start:]:
            if r.is_dma:
                dmas.append(r)
            else:
                last[r.eng] = r
        self.bar_start = len(self.recs)
        new = list(last.values()) + dmas
        for e in ENGS:
            self.bar_deps[e] = self.bar_deps[e] + new

    def _deps(self, eng, reads, writes):
        deps = list(self.bar_deps[eng])
        self.bar_deps[eng] = []
        for t in reads:
            if t.const:
                continue
            if t.last_write is not None:
                deps.append(t.last_write)
        for t in writes:
            if t.last_write is not None:
                deps.append(t.last_write)
            deps.extend(t.reads)
        return deps

    def _commit(self, rec, reads, writes):
        for t in writes:
            t.last_write = rec
            t.reads = []
        for t in reads:
            if t.const or t in writes:
                continue
            t.reads.append(rec)
            if len(t.reads) > 48:
                keep = {}
                rest = {}
                for r in t.reads:
                    if r.is_dma:
                        rest[r.sem_idx] = r
                    else:
                        keep[r.eng] = r
                t.reads = list(rest.values()) + list(keep.values())
        self.recs.append(rec)

    def op(self, eng, fn, reads=(), writes=()):
        reads = [v.tile for v in reads]
        writes = [v.tile for v in writes]
        rec = Rec(eng, fn, self._deps(eng, reads, writes))
        self._commit(rec, reads, writes)
        return rec

    def dma(self, out, in_, eng="sp", **kw):
        reads = [in_.tile]
        writes = [out.tile]
        o_ap, i_ap = out.ap, in_.ap
        rec = Rec(eng, lambda e: e.dma_start(out=o_ap, in_=i_ap, **kw), self._deps(eng, reads, writes), is_dma=True)
        t = out.tile
        if t.sem_idx is None:
            t.sem_idx = self.next_sem % N_DMA_SEMS
            self.next_sem += 1
        rec.sem_idx = t.sem_idx
        self._commit(rec, reads, writes)
        return rec

    def mm(self, out, lhsT, rhs, start=True, stop=True):
        o, l, r = out.ap, lhsT.ap, rhs.ap
        return self.op("pe", lambda e: e.matmul(o, l, r, start=start, stop=stop),
                       reads=[lhsT, rhs], writes=[out])

    def transpose(self, out, in_, ident):
        o, i, d = out.ap, in_.ap, ident.ap
        return self.op("pe", lambda e: e.transpose(o, i, d), reads=[in_, ident], writes=[out])

    def act(self, out, in_, func, bias=None, scale=1.0):
        o, i = out.ap, in_.ap
        reads = [in_]
        kw = {}
        if bias is not None:
            if isinstance(bias, V):
                reads.append(bias)
                kw["bias"] = bias.ap
            else:
                kw["bias"] = bias
        if isinstance(scale, V):
            reads.append(scale)
            kw["scale"] = scale.ap
        else:
            kw["scale"] = scale
        return self.op("act", lambda e: e.activation(o, i, func, **kw), reads=reads, writes=[out])

    def tt(self, out, in0, in1, op, eng="dve"):
        o, a, b = out.ap, in0.ap, in1.ap
        return self.op(eng, lambda e: e.tensor_tensor(o, a, b, op), reads=[in0, in1], writes=[out])

    def ts(self, out, in0, s1, op0, s2=None, op1=None, eng="dve"):
        o, a = out.ap, in0.ap
        reads = [in0]
        if isinstance(s1, V):
            reads.append(s1)
            s1 = s1.ap
        if isinstance(s2, V):
            reads.append(s2)
            s2 = s2.ap
        kw = {}
        if op1 is not None:
            kw["op1"] = op1
        return self.op(eng, lambda e: e.tensor_scalar(o, a, s1, s2, op0, **kw), reads=reads, writes=[out])

    def stt(self, out, in0, scalar, in1, op0, op1, eng="dve"):
        o, a, b = out.ap, in0.ap, in1.ap
        reads = [in0, in1]
        if isinstance(scalar, V):
            reads.append(scalar)
            scalar = scalar.ap
        return self.op(eng, lambda e: e.scalar_tensor_tensor(o, a, scalar, b, op0, op1), reads=reads, writes=[out])

    def copy(self, out, in_, eng="dve"):
        o, i = out.ap, in_.ap
        if eng == "act":
            return self.op("act", lambda e: e.copy(o, i), reads=[in_], writes=[out])
        return self.op(eng, lambda e: e.tensor_copy(o, i), reads=[in_], writes=[out])

    def memset(self, out, val, eng="dve"):
        o = out.ap
        return self.op(eng, lambda e: e.memset(o, val), reads=[], writes=[out])

    def reduce(self, out, in_, op, axis=AX.X):
        o, i = out.ap, in_.ap
        return self.op("dve", lambda e: e.tensor_reduce(o, i, axis, op), reads=[in_], writes=[out])

    def recip(self, out, in_):
        o, i = out.ap, in_.ap
        return self.op("dve", lambda e: e.reciprocal(o, i), reads=[in_], writes=[out])

    def _skip(self, d, r):
        return (not r.is_dma) and d.eng == r.eng and (d.eng == "pe" or not self.same_engine_sync)

    def finish(self, final_waits=()):
        nc = self.nc
        for r in self.recs:
            for d in r.deps:
                if d.is_dma or self._skip(d, r):
                    continue
                d.signaled = True
        for r in final_waits:
            if not r.is_dma:
                r.signaled = True
        cnt = {e: 0 for e in ENGS}
        for r in self.recs:
            if r.is_dma or not r.signaled:
                continue
            cnt[r.eng] += 1
            r.count = cnt[r.eng]
        cms = []
        sems = {}
        for e in ENGS:
            cm = nc.semaphore("s_" + e)
            sems[e] = cm.__enter__()
            cms.append(cm)
        dsems = []
        for i in range(min(N_DMA_SEMS, max(1, self.next_sem))):
            cm = nc.semaphore("d_%d" % i)
            dsems.append(cm.__enter__())
            cms.append(cm)
        streams = {e: [] for e in ENGS}
        seen = {e: {} for e in ENGS}
        dma_emitted = {}
        for r in self.recs:
            waits = {}
            for d in r.deps:
                if d.is_dma:
                    key = ("d", d.sem_idx)
                    val = 16 * dma_emitted[d.sem_idx]
                    sem = dsems[d.sem_idx]
                else:
                    if self._skip(d, r):
                        continue
                    key = ("c", d.eng)
                    val = d.count
                    sem = sems[d.eng]
                if seen[r.eng].get(key, 0) >= val:
                    continue
                if key not in waits or waits[key][1] < val:
                    waits[key] = (sem, val)
            for key, (sem, val) in waits.items():
                seen[r.eng][key] = val
            if r.is_dma:
                dma_emitted[r.sem_idx] = dma_emitted.get(r.sem_idx, 0) + 1
            streams[r.eng].append((list(waits.values()), r))
            r.deps = None
        fw = {}
        for r in final_waits:
            if r.is_dma:
                fw[("d", r.sem_idx)] = (dsems[r.sem_idx], 16 * dma_emitted[r.sem_idx])
            else:
                key = ("c", r.eng)
                if key not in fw or fw[key][1] < r.count:
                    fw[key] = (sems[r.eng], r.count)
        self.stats = {e: len(streams[e]) for e in ENGS}
        self.stats["sem_counts"] = dict(cnt)

        def make(ename):
            def body(e):
                for waits, r in streams[ename]:
                    for sem, val in waits:
                        e.wait_ge(sem, val)
                    ins = r.fn(e)
                    if r.is_dma:
                        ins.then_inc(dsems[r.sem_idx], 16)
                    elif r.signaled:
                        ins.then_inc(sems[r.eng], 1)
                if ename == "sp":
                    for sem, val in fw.values():
                        e.wait_ge(sem, val)
            return body

        with nc.Block() as block:
            block.tensor(make("pe"))
            block.scalar(make("act"))
            block.vector(make("dve"))
            block.gpsimd(make("pool"))
            block.sync(make("sp"))
        for cm in reversed(cms):
            cm.__exit__(None, None, None)
        while self.scope_stack:
            for cm in reversed(self.scope_stack.pop()):
                cm.__exit__(None, None, None)


VEC_LAYOUT = [("n1g", 8), ("n2g", 8), ("bmod", 48), ("pscale", 2), ("gq", 1), ("gk", 1), ("hcw", 18),
              ("hcb", 6), ("fb1", 1), ("fb2", 1), ("ffr", 1), ("hbias", 4), ("cqg", 2), ("ckvg", 1),
              ("mqg", 1), ("mkg", 1)]
W_IN_GROUPS = [(0, 256), (256, 160), (416, 256), (672, 256)] + [(928 + 128 * j, 128) for j in range(8)]
VEC_OFF = {}
_o = 0
for _n, _k in VEC_LAYOUT:
    VEC_OFF[_n] = (_o, _k)
    _o += _k
NV = _o


def _rot_tables(m, p):
    inv = (10000.0 ** (-np.arange(m, dtype=np.float32) / m)).astype(np.float32)
    ang = (p[None, :].astype(np.float32) * inv[:, None]).astype(np.float32)
    c = np.concatenate([np.cos(ang), np.cos(ang)], 0)
    s = np.concatenate([np.sin(ang), np.sin(ang)], 0)
    return c.astype(np.float32), s.astype(np.float32)


def _rot_matrix(dim, blocks):
    R = np.zeros((dim, dim), np.float32)
    for st, m in blocks:
        for d in range(m):
            R[st + d, st + d + m] = -1.0
            R[st + d + m, st + d] = 1.0
    return np.ascontiguousarray(R.T)


def build_consts():
    C = {}
    pos = np.arange(SEQ)
    row = (pos // 64).astype(np.float32)
    col = (pos % 64).astype(np.float32)
    c1, s1 = _rot_tables(16, row)
    c2, s2 = _rot_tables(16, col)
    c64 = np.concatenate([c1, c2], 0)
    s64 = np.concatenate([s1, s2], 0)
    C["ropeg"] = np.ascontiguousarray(np.stack([np.concatenate([c64, c64], 0), np.concatenate([s64, s64], 0)], 1)).astype(NPBF)
    c1, s1 = _rot_tables(8, row)
    c2, s2 = _rot_tables(8, col)
    c96 = np.concatenate([np.ones((64, SEQ), np.float32), c1, c2], 0)
    s96 = np.concatenate([np.zeros((64, SEQ), np.float32), s1, s2], 0)
    rm = np.zeros((2, 128, SEQ), np.float32)
    rm[0, :96] = c96
    rm[1, :96] = s96
    C["ropem"] = np.ascontiguousarray(rm.transpose(1, 0, 2)).astype(NPBF)
    rg = _rot_matrix(128, [(0, 16), (32, 16), (64, 16), (96, 16)])
    rmm = np.zeros((128, 128), np.float32)
    rmm[:96, :96] = _rot_matrix(96, [(64, 8), (80, 8)])
    mats = np.zeros((6, 128, 128), np.float32)
    mats[0] = np.eye(128)
    mats[1] = 1.0
    mats[2, :64, :64] = 1.0
    mats[2, 64:, 64:] = 1.0
    mats[3] = rg
    mats[4] = rmm
    mats[5, :32, 64:96] = np.eye(32)
    C["mats"] = mats
    sel = np.zeros((16, 16, 128), np.float32)
    for e in range(16):
        sel[e, e, :] = 1.0
    C["sel"] = sel.astype(NPBF)
    for n, tag in ((SEQ, "L"), (NCTX, "C")):
        N = 2 * n
        th = 2.0 * np.pi / N
        nt = n // 128
        cw = min(512, n)
        ncw = n // cw
        t = np.linspace(0.0, 1.0, n, dtype=np.float32)[:, None]
        lag = np.arange(n, dtype=np.float32)[:, None]
        bands = np.linspace(1e-4, 7.0, 8, dtype=np.float32)[None, :]
        ang = (np.float32(2.0 * math.pi / n) * lag * bands).astype(np.float32)
        feats = np.concatenate([t, np.cos(ang), -np.sin(ang)], -1).astype(np.float32)
        C["feats" + tag] = np.ascontiguousarray(feats.T)
        deltas = np.abs(np.linspace(math.log(1e-2) / 1.5, math.log(1e-2) / 0.3, 256, dtype=np.float32))
        dec = np.exp(-t * deltas[None, :]).astype(np.float32)
        C["decay" + tag] = np.ascontiguousarray(dec.T.reshape(2, 128, n))
        ic = np.zeros((2, 128, n), np.float32)
        tt = np.arange(n)
        for gi, w in enumerate((2, 4, 8, 16)):
            left = w // 2
            right = w - 1 - left
            lo = np.clip(tt - left, 0, n)
            hi = np.clip(tt + right + 1, 0, n)
            ic[gi // 2, (gi % 2) * 64:(gi % 2) * 64 + 64, :] = (1.0 / (hi - lo).astype(np.float32))[None, :]
        C["invcnt" + tag] = ic
        f = np.arange(n, dtype=np.float64)
        p = np.arange(n, dtype=np.float64)
        A = th * np.outer(p, f)
        Fc = np.cos(A)
        Fs = -np.sin(A)
        Fh = np.zeros((nt, 128, nt, 256), np.float32)
        Fc4 = Fc.reshape(nt, 128, nt, 128)
        Fs4 = Fs.reshape(nt, 128, nt, 128)
        Fh[:, :, :, 0:128] = Fc4.transpose(2, 1, 0, 3)
        Fh[:, :, :, 128:256] = Fs4.transpose(2, 1, 0, 3)
        C["F" + tag] = Fh.astype(NPBF)
        wf = np.full(n, 2.0)
        wf[0] = 1.0
        Gr = (wf[:, None] / N) * np.cos(A.T)
        Gi = -(2.0 / N) * np.sin(A.T)
        G = np.concatenate([Gr, Gi], 0)
        cwi = 128
        G4 = G.reshape(2 * nt, 128, n // cwi, cwi)
        C["G" + tag] = np.ascontiguousarray(G4.transpose(2, 1, 0, 3)).astype(NPBF)
        ny = np.zeros((128, 1 + n), np.float32)
        ny[:, 0] = (-1.0) ** np.arange(128)
        ny[0, 1:] = ((-1.0) ** np.arange(n)) / N
        C["ny" + tag] = ny.astype(NPBF)
    return C


_CONSTS = None


def get_consts():
    global _CONSTS
    if _CONSTS is None:
        _CONSTS = build_consts()
    return _CONSTS


def fm(v, rows=128):
    v = np.asarray(v, np.float32)
    return np.ascontiguousarray(v.reshape(-1, rows).T)


def pack_vecs(inp, l):
    out = np.zeros((128, NV), np.float32)

    def put(name, arr):
        o, k = VEC_OFF[name]
        arr = np.asarray(arr, np.float32)
        out[:arr.shape[0], o:o + k] = arr.reshape(arr.shape[0], k)

    put("n1g", fm(inp["norm1_g"][l]))
    put("n2g", fm(inp["norm2_g"][l]))
    put("bmod", fm(inp["b_mod"][l]))
    put("pscale", fm(inp["pool_scale"][l]))
    put("gq", np.tile(inp["gqa_qnorm_g"][l], 2)[:, None])
    put("gk", np.tile(inp["gqa_knorm_g"][l], 2)[:, None])
    hw = np.asarray(inp["hy_conv_w"][l], np.float32)
    put("hcw", hw.reshape(3, 6, 128).transpose(2, 1, 0).reshape(128, 18))
    put("hcb", fm(inp["hy_conv_b"][l]))
    put("fb1", np.asarray(inp["hy_f_b1"][l])[:, None])
    put("fb2", np.asarray(inp["hy_f_b2"][l])[:, None])
    put("ffr", np.asarray(inp["hy_freq"][l])[:, None])
    hb = np.asarray(inp["hy_bias"][l], np.float32)
    put("hbias", hb.reshape(2, 2, 128).transpose(2, 0, 1).reshape(128, 4))
    put("cqg", fm(inp["mla_cq_g"][l]))
    put("ckvg", fm(inp["mla_ckv_g"][l]))
    put("mqg", np.asarray(inp["mla_qnorm_g"][l])[:, None])
    put("mkg", np.asarray(inp["mla_knorm_g"][l])[:, None])
    return out


def build_program(nb=NB, depth=DEPTH, debug=False, stop_at=None, nexp=NEXP):
    nc = bass.Bass("TRN2", target_bir_lowering=False)
    P = Prog(nc)
    P.stop_at = stop_at
    P.used_inputs = []

    class LazyIn:
        def __init__(self, name, shape, dt):
            self.name, self.shape, self.dt, self.t = name, shape, dt, None

        def __getitem__(self, idx):
            if self.t is None:
                self.t = P.dram(self.name, self.shape, self.dt, kind="ExternalInput")
                P.used_inputs.append(self.name)
            return self.t[idx]

    def inp(name, shape, dt=F32):
        return LazyIn(name, shape, dt)

    xT_in = inp("xT", [nb, D, SEQ])
    cxT_in = inp("ctxT", [nb, D, NCTX])
    cT_in = inp("cT", [128, 8, nb + 1])
    vecs_in = inp("vecs", [DEPTH, 128, NV])
    wmod_in = inp("w_mod", [DEPTH, D, 6 * D])
    win_in = inp("w_in", [DEPTH, 128, 8 * 1952])
    wout_in = inp("w_out", [DEPTH, 128, 8, D])
    wout64_in = inp("w_out64", [DEPTH, 64, 8, D])
    poolbd_in = inp("poolbd", [DEPTH, 2, 128, 128])
    hw1_in = inp("hy_f_w1", [DEPTH, 17, 64])
    hw2_in = inp("hy_f_w2", [DEPTH, 64, 64])
    hw3_in = inp("hy_f_w3", [DEPTH, 64, 1024])
    wuq_in = inp("mla_w_uq", [DEPTH, 256, 384])
    wukvk_in = inp("wukv_k", [DEPTH, 128, 4, 96])
    wukvv_in = inp("wukv_v", [DEPTH, 128, 256])
    wr_in = inp("router_w", [DEPTH, D, 16])
    wgu_in = inp("exp_w_gu", [DEPTH, nexp, 128, 8, 1024])
    wd_in = inp("exp_w_dn", [DEPTH, nexp, 128, 4, D])
    ropeg_in = inp("ropeg", [128, 2, SEQ], BF16)
    ropem_in = inp("ropem", [128, 2, SEQ], BF16)
    mats_in = inp("mats", [6, 128, 128])
    sel_in = inp("sel", [16, 16, 128], BF16)
    CI = {}
    for n, tag in ((SEQ, "L"), (NCTX, "C")):
        nt = n // 128
        cw = min(512, n)
        CI["feats" + tag] = inp("feats" + tag, [17, n])
        CI["decay" + tag] = inp("decay" + tag, [2, 128, n])
        CI["invcnt" + tag] = inp("invcnt" + tag, [2, 128, n])
        CI["F" + tag] = inp("F" + tag, [nt, 128, nt, 256], BF16)
        cwi = 128
        CI["G" + tag] = inp("G" + tag, [n // cwi, 128, 2 * nt, cwi], BF16)
        CI["ny" + tag] = inp("ny" + tag, [128, 1 + n], BF16)

    if debug == "force_decl":
        for t_ in (wgu_in, wd_in):
            t_[0, 0, 0:1, :, :]
    outT = P.dram("outT", [nb, D, SEQ], F32, kind="ExternalOutput")
    dbg = {}

    dk = "ExternalOutput" if debug else "Internal"
    xs_d = P.dram("xs_d", [nb, D, SEQ], F32, kind=dk)
    xc_d = P.dram("xc_d", [nb, D, NCTX], F32, kind=dk)
    SCR = {}
    for tag, n in (("L", SEQ), ("C", NCTX)):
        SCR[tag] = dict(
            xmid=P.dram("xmid_" + tag, [D, n], F32, kind=dk),
            mix=P.dram("mix_" + tag, [D, n], BF16, kind=dk),
            pu=P.dram("pu_" + tag, [256, n], F32, kind=dk),
            hu=P.dram("hu_" + tag, [768, n], F32, kind=dk),
        )
    hf_d = {"L": P.dram("hf_L", [2, SEQ // 128, 128, 2, 256], F32, kind=dk),
            "C": P.dram("hf_C", [2, NCTX // 128, 128, 2, 256], F32, kind=dk)}

    wgu_bf = [P.dram("wgu_bf%d" % i, [NEXP // 2, 128, 8, 1024], BF16) for i in range(2)]
    wdn_bf = [P.dram("wdn_bf%d" % i, [NEXP // 2, 128, 4, D], BF16) for i in range(2)]

    win_bf = P.dram("win_bf", [128, 8 * 1952], BF16)
    wout_bf = P.dram("wout_bf", [128, 8, D], BF16)
    wout64_bf = P.dram("wout64_bf", [64, 8, D], BF16)
    LW = {}

    def precast_dense(l):
        P.dma(win_bf[:, :], win_in[l, :, :], eng="pool")
        P.dma(wout_bf[:, :, :], wout_in[l, :, :, :], eng="pool")
        P.dma(wout64_bf[:, :, :], wout64_in[l, :, :, :], eng="pool")
        P.dma(LW["wk"][:], wukvk_in[l, :, :, :], eng="pool")
        P.dma(LW["wv"][:], wukvv_in[l, :, :], eng="pool")
        P.dma(LW["wuq"][:], wuq_in[l, :, :].re("(k p) n -> p k n", p=128), eng="pool")
        P.dma(LW["wbd"][:], poolbd_in[l, :, :, :].re("c p e -> p c e"), eng="pool")
        P.dma(LW["wr"][:], wr_in[l, :, :].re("(k p) e -> p k e", p=128), eng="pool")

    def precast_experts(l):
        for e in range(nexp):
            P.dma(wgu_bf[e % 2][e // 2, :, :, :], wgu_in[l, e, :, :, :], eng="pool")
            P.dma(wdn_bf[e % 2][e // 2, :, :, :], wd_in[l, e, :, :, :], eng="pool")

    dbg_gm = {}
    if debug:
        dbg_gm = {"L": P.dram("gm_L", [16, SEQ], F32, kind="ExternalOutput"),
                  "C": P.dram("gm_C", [16, NCTX], F32, kind="ExternalOutput")}

    mats = P.sbuf("mats", [128, 6, 128], F32)
    matsb = P.sbuf("matsb", [128, 6, 128], BF16)
    P.dma(mats[:], mats_in[:, :, :].re("m p c -> p m c"))
    P.dma(matsb[:], mats_in[:, :, :].re("m p c -> p m c"), eng="pool")
    IDF, ONF, BLKF, RG, RM, SELKR = [mats[:, i, :] for i in range(6)]
    IDB, ONB, BLKB = [matsb[:, i, :] for i in range(3)]
    SELKRB = matsb[:, 5, :]
    vecs = P.sbuf("vecs", [128, NV], F32)
    modT = P.sbuf("modT", [128, 48, nb + 1], F32)
    epsb = P.sbuf("epsb", [128, 1], F32)
    P.memset(epsb[:], EPS)
    hn = {"L": P.sbuf("hnL", [1, 512], F32), "C": P.sbuf("hnC", [1, 512], F32)}
    LW["wk"] = P.sbuf("wk", [128, 4, 96], BF16)
    LW["wv"] = P.sbuf("wv", [128, 256], BF16)
    LW["wuq"] = P.sbuf("wuq", [128, 2, 384], BF16)
    LW["wbd"] = P.sbuf("wbd", [128, 2, 128], BF16)
    LW["wr"] = P.sbuf("wr", [128, 8, 16], BF16)
    ps = [P.psum("ps%d" % i, [128, 512], F32) for i in range(7)]
    psb = P.psum("psb", [128, 1024], BF16)
    KV = {}

    def vec(name, j=0, rows=128):
        o, k = VEC_OFF[name]
        return vecs[0:rows, o + j:o + j + 1]

    evac_flip = [0]

    def evac(out, in_):
        evac_flip[0] ^= 1
        if evac_flip[0]:
            P.copy(out, in_, eng="act")
        else:
            P.copy(out, in_, eng="dve")

    def q2(i):
        return "sp"

    def layer_prologue(l):
        P.dma(vecs[:], vecs_in[l, :, :])
        with P.scope():
            sil = P.sbuf("sil", [128, 8, nb + 1], F32)
            P.dma(sil[:], cT_in[:, :, :])
            P.act(sil[:], sil[:], AF.Silu)
            wbuf = [P.sbuf("wmod%d" % i, [128, 8, 512], F32) for i in range(2)]
            for piece in range(12):
                wb = wbuf[piece % 2]
                P.dma(wb[:], wmod_in[l, :, piece * 512:(piece + 1) * 512].re("(k p) n -> p k n", p=128),
                      eng=q2(piece))
                for jj in range(4):
                    j = piece * 4 + jj
                    pt = ps[j % 2]
                    for k in range(8):
                        P.mm(pt[:, 0:nb + 1], wb[:, k, jj * 128:(jj + 1) * 128], sil[:, k, :],
                             start=(k == 0), stop=(k == 7))
                    P.ts(modT[:, j, :], pt[:, 0:nb + 1], vec("bmod", j), ALU.add)

    def filter_gen(l, n, tag):
        nt = n // 128
        cw = min(512, n)
        ncw = n // cw
        with P.scope():
            edT = P.sbuf("fedT", [128, 2, nt, 512], BF16)
            with P.scope():
                ed = P.sbuf("fed", [128, 2, 4, n], BF16)
                with P.scope():
                    hid2 = P.sbuf("hid2", [64, n], F32)
                    w3 = P.sbuf("fw3", [64, 1024], F32)
                    P.dma(w3[:], hw3_in[l, :, :])
                    with P.scope():
                        feats = P.sbuf("feats", [17, n], F32)
                        P.dma(feats[:], CI["feats" + tag][:, :])
                        w1 = P.sbuf("fw1", [17, 64], F32)
                        w2 = P.sbuf("fw2", [64, 64], F32)
                        P.dma(w1[:], hw1_in[l, :, :])
                        P.dma(w2[:], hw2_in[l, :, :])
                        frb = P.sbuf("frb", [64, 2], F32)
                        P.tt(frb[:, 0:1], vec("ffr", 0, 64), vec("fb1", 0, 64), ALU.mult)
                        P.tt(frb[:, 1:2], vec("ffr", 0, 64), vec("fb2", 0, 64), ALU.mult)
                        hid1 = P.sbuf("hid1", [64, n], F32)
                        arg = P.sbuf("farg", [64, cw], F32)
                        kk = P.sbuf("fkk", [64, cw], F32)

                        def sin_layer(dst, w, src, bcol):
                            for c in range(ncw):
                                sl = slice(c * cw, (c + 1) * cw)
                                P.mm(ps[0][0:64, 0:cw], w, src[:, sl])
                                P.ts(arg[:], ps[0][0:64, 0:cw], vec("ffr", 0, 64), ALU.mult,
                                     frb[:, bcol:bcol + 1], ALU.add)
                                P.ts(kk[:], arg[:], 1.0 / TWO_PI, ALU.mult, MAGIC, ALU.add)
                                P.ts(kk[:], kk[:], -MAGIC, ALU.add, TWO_PI, ALU.mult)
                                P.tt(arg[:], arg[:], kk[:], ALU.subtract)
                                P.ts(arg[:], arg[:], -3.141592, ALU.max, 3.141592, ALU.min)
                                P.act(dst[:, sl], arg[:], AF.Sin)

                        sin_layer(hid1, w1[:], feats, 0)
                        sin_layer(hid2, w2[:], hid1, 1)
                    decay = P.sbuf("decay", [128, 2, n], F32)
                    P.dma(decay[:], CI["decay" + tag][:, :, :].re("c p n -> p c n"))
                    hh = P.sbuf("hh", [128, 2, n], F32)
                    sq = P.sbuf("fsq", [128, n], F32)
                    tmp = P.sbuf("ftmp", [128, n], F32)
                    ssq = P.sbuf("ssq", [128, 2], F32)
                    rn = P.sbuf("frn", [128, 1], F32)
                    for q in range(4):
                        for d_ in range(2):
                            j = d_ * 4 + q
                            for c in range(ncw):
                                sl = slice(c * cw, (c + 1) * cw)
                                pt = ps[c % 2]
                                P.mm(pt[:, 0:cw], w3[:, j * 128:(j + 1) * 128], hid2[:, sl])
                                P.tt(hh[:, d_, sl], pt[:, 0:cw], decay[:, q % 2, sl], ALU.mult)
                        P.memset(hh[:, 1, 0:1], 0.0)
                        for d_ in range(2):
                            P.tt(sq[:], hh[:, d_, :], hh[:, d_, :], ALU.mult)
                            P.reduce(ssq[:, d_:d_ + 1], sq[:], ALU.add)
                        P.tt(rn[:], ssq[:, 0:1], ssq[:, 1:2], ALU.add)
                        P.act(rn[:], rn[:], AF.Sqrt, bias=epsb[:, 0:1], scale=1.0)
                        P.recip(rn[:], rn[:])
                        P.tt(tmp[:], hh[:, 0, :], hh[:, 1, :], ALU.add)
                        P.ts(ed[:, 0, q, :], tmp[:], rn[:, 0:1], ALU.mult)
                        P.tt(tmp[:], hh[:, 0, :], hh[:, 1, :], ALU.subtract)
                        P.ts(ed[:, 1, q, :], tmp[:], rn[:, 0:1], ALU.mult)
                for w_ in range(2):
                    for s in range(nt):
                        for q in range(4):
                            P.transpose(psb[:, q * 128:(q + 1) * 128], ed[:, w_, q, s * 128:(s + 1) * 128], IDB)
                        evac(edT[:, w_, s, :], psb[:, 0:512])
            ny = P.sbuf("fny", [128, 1 + n], BF16)
            P.dma(ny[:], CI["ny" + tag][:, :])
            for s in range(nt):
                P.mm(ps[2][0:1, :], ny[:, 0:1], edT[:, 0, s, :], start=(s == 0), stop=(s == nt - 1))
            P.copy(hn[tag][:], ps[2][0:1, :])
            fbuf = [P.sbuf("fF%d" % i, [128, nt, 256], BF16) for i in range(2)]
            hfs = [P.sbuf("fhfs%d" % i, [128, 2, 2, 256], F32) for i in range(2)]
            for fk in range(nt):
                fb = fbuf[fk % 2]
                P.dma(fb[:], CI["F" + tag][fk, :, :, :], eng=q2(fk))
                for half in range(2):
                    pt = ps[half]
                    for s in range(nt):
                        P.mm(pt[:, :], fb[:, s, half * 128:(half + 1) * 128], edT[:, half, s, :],
                             start=(s == 0), stop=(s == nt - 1))
                hs = hfs[fk % 2]
                evac(hs[:, :, 0, :], ps[0][:, :].re("p (o c) -> p o c", o=2))
                evac(hs[:, :, 1, :], ps[1][:, :].re("p (o c) -> p o c", o=2))
                for o in range(2):
                    P.dma(hf_d[tag][o, fk, :, :, :], hs[:, o, :, :], eng=q2(o))

    def norm_mod(dst_bf, x_view_fn, n, gname, sh_j, sc_j, bcol):
        cw = min(512, n)
        ncw = n // cw
        with P.scope():
            A = P.sbuf("nmA", [128, 8], F32)
            P.ts(A[:], modT[:, sc_j:sc_j + 8, bcol], 1.0, ALU.add)
            o, k = VEC_OFF[gname]
            P.tt(A[:], A[:], vecs[:, o:o + 8], ALU.mult)
            xfs = [P.sbuf("nmx%d" % i, [128, 8, cw], F32) for i in range(2)]
            sq = P.sbuf("nmsq", [128, 8, cw], BF16)
            rstd = P.sbuf("nmrstd", [128, cw], F32)
            tmp = P.sbuf("nmtmp", [128, cw], F32)
            for c in range(ncw):
                sl = slice(c * cw, (c + 1) * cw)
                xf = xfs[c % 2]
                P.dma(xf[:], x_view_fn(sl), eng=q2(c))
                P.act(sq[:, 0:4, :], xf[:, 0:4, :], AF.Square)
                P.tt(sq[:, 4:8, :], xf[:, 4:8, :], xf[:, 4:8, :], ALU.mult)
                pt = ps[c % 2]
                for k in range(8):
                    P.mm(pt[:, 0:cw], ONB, sq[:, k, :], start=(k == 0), stop=(k == 7))
                P.act(rstd[:], pt[:, 0:cw], AF.Sqrt, bias=epsb[:, 0:1], scale=1.0 / D)
                P.recip(rstd[:], rstd[:])
                for k in range(8):
                    P.tt(tmp[:], xf[:, k, :], rstd[:], ALU.mult)
                    P.ts(dst_bf[:, k, sl], tmp[:], A[:, k:k + 1], ALU.mult,
                         modT[:, sh_j + k, bcol:bcol + 1], ALU.add)

    def headnorm_rope(dst, src, rows, n, ones_blk, gcol, gscale, rmat, rope, inv_d):
        cw = min(512, n)
        ncw = n // cw
        with P.scope():
            g = P.sbuf("hg", [128, 1], F32)
            P.ts(g[0:rows, :], gcol, gscale, ALU.mult)
            sq = [P.sbuf("hsq%d" % i, [128, cw], BF16) for i in range(2)]
            rstd = [P.sbuf("hrstd%d" % i, [128, cw], F32) for i in range(2)]
            xn = P.sbuf("hxn", [128, n], F32)
            t1 = [P.sbuf("ht1%d" % i, [128, cw], F32) for i in range(2)]
            for c in range(ncw):
                sl = slice(c * cw, (c + 1) * cw)
                sq_, rs_ = sq[c % 2], rstd[c % 2]
                pt = ps[c % 4]
                P.act(sq_[0:rows, :], src[0:rows, sl], AF.Square)
                P.mm(pt[0:rows, 0:cw], ones_blk, sq_[0:rows, :])
                P.act(rs_[0:rows, :], pt[0:rows, 0:cw], AF.Sqrt, bias=epsb[0:rows, 0:1], scale=inv_d)
                P.recip(rs_[0:rows, :], rs_[0:rows, :])
                if rope is None:
                    P.stt(dst[0:rows, sl], src[0:rows, sl], g[0:rows, 0:1], rs_[0:rows, :], ALU.mult, ALU.mult)
                else:
                    P.stt(xn[0:rows, sl], src[0:rows, sl], g[0:rows, 0:1], rs_[0:rows, :], ALU.mult, ALU.mult)
            if rope is not None:
                for c in range(ncw):
                    sl = slice(c * cw, (c + 1) * cw)
                    pt = ps[4 + c % 2]
                    t1_ = t1[c % 2]
                    P.mm(pt[0:rows, 0:cw], rmat, xn[0:rows, sl])
                    P.tt(t1_[0:rows, :], pt[0:rows, 0:cw], rope[0:rows, 1, sl], ALU.mult)
                    P.tt(xn[0:rows, sl], xn[0:rows, sl], rope[0:rows, 0, sl], ALU.mult)
                    P.tt(dst[0:rows, sl], xn[0:rows, sl], t1_[0:rows, :], ALU.add)

    def seq_params(l, b, n, is_ctx, last):
        tag = "C" if is_ctx else "L"
        if l == 0:
            x_src = (cxT_in if is_ctx else xT_in)
        else:
            x_src = (xc_d if is_ctx else xs_d)
        if is_ctx:
            x_dst = xc_d
        else:
            x_dst = outT if last else xs_d
        return tag, n // 128, min(512, n), n // min(512, n), (nb if is_ctx else b), x_src, x_dst

    def seq_front(l, b, n, is_ctx, last):
        tag, nt, cw, ncw, bcol, x_src, x_dst = seq_params(l, b, n, is_ctx, last)
        S = SCR[tag]
        kg_all, vg_all, km_all, vm_all = KV["kg"], KV["vg"], KV["km"], KV["vm"]
        koff = SEQ if is_ctx else 0
        ktoff = koff // 128
        kv_only = is_ctx and last
        with P.scope():
            qg = P.sbuf("qg", [128, 2, n], BF16)
            qm = P.sbuf("qm", [96, 4, n], BF16)
            with P.scope():
                hT = P.sbuf("hT", [128, 8, n], BF16)
                norm_mod(hT, lambda sl: x_src[b, :, sl].re("(k p) n -> p k n", p=128), n, "n1g", 0, 8, bcol)

                lw_i = [0]

                def load_w(c0, m):
                    w = P.sbuf("wsub", [128, 8, m], BF16)
                    lw_i[0] += 1
                    P.dma(w[:], win_bf[:, 8 * c0:8 * (c0 + m)].re("p (k m) -> p k m", k=8), eng=q2(lw_i[0]))
                    return w

                def proj_fm(w, w0, m, dst_fn):
                    for c in range(ncw):
                        sl = slice(c * cw, (c + 1) * cw)
                        pt = ps[2 + c % 2]
                        for k in range(8):
                            P.mm(pt[0:m, 0:cw], w[:, k, w0:w0 + m], hT[:, k, sl], start=(k == 0), stop=(k == 7))
                        dst_fn(sl, pt[0:m, 0:cw])

                with P.scope():
                    rope_g = None
                    if not is_ctx:
                        rope_g = P.sbuf("ropeg", [128, 2, SEQ], BF16)
                        P.dma(rope_g[:], ropeg_in[:, :, :], eng="sp")
                    w = load_w(0, 256)
                    src = P.sbuf("pj", [128, n], F32)
                    proj_fm(w, 0, 128, lambda sl, pv: evac(src[:, sl], pv))
                    headnorm_rope(kg_all[:, koff:koff + n], src, 128, n, BLKB, vec("gk"), 1.0, RG, rope_g, 1.0 / 64)
                    for t in range(nt):
                        pt = ps[4 + t % 2]
                        for k in range(8):
                            P.mm(pt[:, 0:128], hT[:, k, t * 128:(t + 1) * 128], w[:, k, 128:256],
                                 start=(k == 0), stop=(k == 7))
                        evac(vg_all[:, ktoff + t, :, 0:64], pt[:, 0:128].re("p (h d) -> p h d", h=2))
                    if not kv_only:
                        wq = load_w(416, 256)
                        for j in range(2):
                            proj_fm(wq, j * 128, 128, lambda sl, pv: evac(src[:, sl], pv))
                            headnorm_rope(qg[:, j, :], src, 128, n, BLKB, vec("gq"), 64 ** -0.5, RG, rope_g, 1.0 / 64)
                rope_m = None
                with P.scope():
                    if not is_ctx:
                        rope_m = P.sbuf("ropem", [128, 2, SEQ], BF16)
                        P.dma(rope_m[:], ropem_in[:, :, :], eng="sp")
                    w = load_w(256, 160)
                    src = P.sbuf("pj", [128, n], F32)
                    proj_fm(w, 0, 128, lambda sl, pv: evac(src[:, sl], pv))
                    ckvn = P.sbuf("ckvn", [128, n], BF16)
                    headnorm_rope(ckvn[:, :], src, 128, n, ONB, vec("ckvg"), 1.0, None, None, 1.0 / 128)
                    kr = P.sbuf("kr", [32, n], BF16)
                    proj_fm(w, 128, 32, lambda sl, pv: evac(kr[:, sl], pv))
                    wk, wv = LW["wk"], LW["wv"]
                    for h in range(4):
                        for c in range(ncw):
                            sl = slice(c * cw, (c + 1) * cw)
                            pt = ps[2 + c % 2]
                            P.mm(pt[0:96, 0:cw], wk[:, h, :], ckvn[:, sl], start=True, stop=False)
                            P.mm(pt[0:96, 0:cw], SELKRB[0:32, 0:96], kr[:, sl], start=False, stop=True)
                            evac(src[0:96, sl], pt[0:96, 0:cw])
                        headnorm_rope(km_all[:, h, koff:koff + n], src, 96, n, ONB[0:96, 0:96], vec("mkg", 0, 96),
                                      1.0, RM[0:96, 0:96], rope_m, 1.0 / 96)
                    for t in range(nt):
                        pt = ps[4 + t % 2]
                        P.mm(pt[:, 0:256], ckvn[:, t * 128:(t + 1) * 128], wv[:, :])
                        evac(vm_all[:, ktoff + t, :, 0:64], pt[:, 0:256].re("p (h d) -> p h d", h=4))
                if not kv_only:
                    with P.scope():
                        if not is_ctx:
                            rope_m = P.sbuf("ropem", [128, 2, SEQ], BF16)
                            P.dma(rope_m[:], ropem_in[:, :, :], eng="sp")
                        w = load_w(672, 256)
                        cqn = P.sbuf("cqn", [128, 2, n], BF16)
                        wuq = LW["wuq"]
                        with P.scope():
                            cq = P.sbuf("cq", [128, 2, cw], F32)
                            sqc = P.sbuf("sqc", [128, 2, cw], BF16)
                            rstd = P.sbuf("cqrstd", [128, cw], F32)
                            for c in range(ncw):
                                sl = slice(c * cw, (c + 1) * cw)
                                for j in range(2):
                                    pt = ps[2 + j]
                                    for k in range(8):
                                        P.mm(pt[:, 0:cw], w[:, k, j * 128:(j + 1) * 128], hT[:, k, sl],
                                             start=(k == 0), stop=(k == 7))
                                    evac(cq[:, j, :], pt[:, 0:cw])
                                P.act(sqc[:, :, :], cq[:, :, :], AF.Square)
                                for j in range(2):
                                    P.mm(ps[0][:, 0:cw], ONB, sqc[:, j, :], start=(j == 0), stop=(j == 1))
                                P.act(rstd[:], ps[0][:, 0:cw], AF.Sqrt, bias=epsb[:, 0:1], scale=1.0 / 256)
                                P.recip(rstd[:], rstd[:])
                                for j in range(2):
                                    P.stt(cqn[:, j, sl], cq[:, j, :], vec("cqg", j), rstd[:], ALU.mult, ALU.mult)
                        src = P.sbuf("pj", [128, n], F32)
                        for h in range(4):
                            for c in range(ncw):
                                sl = slice(c * cw, (c + 1) * cw)
                                pt = ps[2 + c % 2]
                                for j in range(2):
                                    P.mm(pt[0:96, 0:cw], wuq[:, j, h * 96:(h + 1) * 96], cqn[:, j, sl],
                                         start=(j == 0), stop=(j == 1))
                                evac(src[0:96, sl], pt[0:96, 0:cw])
                            headnorm_rope(qm[:, h, :], src, 96, n, ONB[0:96, 0:96], vec("mqg", 0, 96),
                                          96 ** -0.5, RM[0:96, 0:96], rope_m, 1.0 / 96)
                    with P.scope():
                        stg = [P.sbuf("stg%d" % i, [128, n], F32) for i in range(2)]
                        for j in range(8):
                            st = stg[j % 2]
                            w = load_w(928 + j * 128, 128)
                            proj_fm(w, 0, 128, lambda sl, pv: evac(st[:, sl], pv))
                            if j < 2:
                                P.dma(S["pu"][j * 128:(j + 1) * 128, :], st[:, :], eng=q2(j))
                            else:
                                P.dma(S["hu"][(j - 2) * 128:(j - 1) * 128, :], st[:, :], eng=q2(j))
            P.checkpoint("A" + tag)
            if kv_only:
                return
            nk = n if is_ctx else SEQ + NCTX
            k0 = SEQ if is_ctx else 0
            nkt = nk // 128
            kt0 = k0 // 128
            with P.scope():
                pT = [P.sbuf("pT%d" % i, [128, cw], BF16) for i in range(3)]
                rs = P.sbuf("rs", [128, cw], F32)
                bcs = P.sbuf("bcs", [64, cw], F32)
                ob = [P.sbuf("ob%d" % i, [64, n], BF16) for i in range(2)]
                pend = [None]

                def flush():
                    if pend[0] is not None:
                        pend[0]()
                        pend[0] = None

                for hh_ in range(8):
                    o_t = ob[hh_ % 2]
                    if hh_ < 4:
                        row0 = 256 + hh_ * 64
                    else:
                        row0 = 768 + (hh_ - 4) * 64
                    for c in range(ncw):
                        sl = slice(c * cw, (c + 1) * cw)
                        acc = ps[4 + (hh_ * ncw + c) % 2]

                        def mm1(kt):
                            st_ = ps[kt % 3]
                            ks = slice(k0 + kt * 128, k0 + (kt + 1) * 128)
                            if hh_ < 4:
                                chunk, half = hh_ % 2, hh_ // 2
                                P.mm(st_[:, 0:cw], kg_all[half * 64:(half + 1) * 64, ks],
                                     qg[half * 64:(half + 1) * 64, chunk, sl])
                            else:
                                P.mm(st_[:, 0:cw], km_all[:, hh_ - 4, ks], qm[:, hh_ - 4, sl])

                        def fin(acc=acc, o_t=o_t, sl=sl, last_c=(c == ncw - 1), row0=row0, hh_=hh_):
                            P.recip(rs[64:65, :], acc[64:65, 0:cw])
                            P.mm(ps[3][0:64, 0:cw], ONF[64:65, 0:64], rs[64:65, :])
                            P.copy(bcs[:], ps[3][0:64, 0:cw], eng="act")
                            P.tt(o_t[:, sl], acc[0:64, 0:cw], bcs[:], ALU.mult)
                            if last_c:
                                P.dma(S["mix"][row0:row0 + 64, :], o_t[:, :], eng=q2(hh_))

                        mm1(0)
                        for kt in range(nkt):
                            if kt + 1 < nkt:
                                mm1(kt + 1)
                            if hh_ < 4:
                                vv = vg_all[:, kt0 + kt, hh_ // 2, :]
                            else:
                                vv = vm_all[:, kt0 + kt, hh_ - 4, :]
                            p_t = pT[kt % 3]
                            P.act(p_t[:], ps[kt % 3][:, 0:cw], AF.Exp)
                            P.mm(acc[0:65, 0:cw], vv, p_t[:], start=(kt == 0), stop=(kt == nkt - 1))
                            if kt == 1:
                                flush()
                        pend[0] = fin
                flush()

    def seq_back(l, b, n, is_ctx, last):
        tag, nt, cw, ncw, bcol, x_src, x_dst = seq_params(l, b, n, is_ctx, last)
        S = SCR[tag]
        with P.scope():
            rights = (0, 1, 3, 7)
            wbd = LW["wbd"]
            U = P.sbuf("pU", [128, n + 32], F32)
            A = [P.sbuf("pA%d" % i, [128, n + 32], F32) for i in range(2)]
            ic = P.sbuf("pic", [128, n], F32)
            dd = P.sbuf("pdd", [128, n], F32)
            db = P.sbuf("pdb", [128, n], BF16)
            ob = P.sbuf("pob", [128, n], BF16)
            P.memset(U[:], 0.0)
            P.memset(A[0][:], 0.0)
            P.memset(A[1][:], 0.0)
            ext = n + 8
            for ch in range(2):
                P.dma(U[:, 16:16 + n], S["pu"][ch * 128:(ch + 1) * 128, :])
                P.dma(ic[:], CI["invcnt" + tag][ch, :, :], eng="sp")
                cur = U
                for wi, w in enumerate((2, 4, 8, 16)):
                    nxt = A[wi % 2]
                    sh = w // 2
                    P.tt(nxt[:, 16:16 + ext], cur[:, 16:16 + ext], cur[:, 16 - sh:16 - sh + ext], ALU.add)
                    cur = nxt
                    g = wi - 2 * ch
                    if g in (0, 1):
                        r_ = rights[wi]
                        psl = slice(g * 64, g * 64 + 64)
                        P.tt(dd[psl, :], cur[psl, 16 + r_:16 + r_ + n], ic[psl, :], ALU.mult)
                        P.tt(db[psl, :], dd[psl, :], U[psl, 16:16 + n], ALU.subtract)
                for c in range(ncw):
                    sl = slice(c * cw, (c + 1) * cw)
                    pt = ps[c % 2]
                    P.mm(pt[:, 0:cw], wbd[:, ch, :], db[:, sl])
                    P.ts(ob[:, sl], pt[:, 0:cw], vec("pscale", ch), ALU.mult)
                P.dma(S["mix"][ch * 128:(ch + 1) * 128, :], ob[:, :])

        P.checkpoint("C" + tag)
        with P.scope():
            NF = nt
            cwi = 128
            ncwi = n // cwi
            vx = P.sbuf("hvx", [128, 6, n], F32)
            with P.scope():
                up = P.sbuf("hup", [128, n + 2], F32)
                P.memset(up[:], 0.0)
                o_, _k = VEC_OFF["hcw"]
                for j in range(6):
                    P.dma(up[:, 1:n + 1], S["hu"][j * 128:(j + 1) * 128, :], eng=q2(j))
                    w0 = vecs[:, o_ + j * 3:o_ + j * 3 + 1]
                    w1_ = vecs[:, o_ + j * 3 + 1:o_ + j * 3 + 2]
                    w2_ = vecs[:, o_ + j * 3 + 2:o_ + j * 3 + 3]
                    P.ts(vx[:, j, :], up[:, 0:n], w0, ALU.mult, vec("hcb", j), ALU.add)
                    P.stt(vx[:, j, :], up[:, 1:n + 1], w1_, vx[:, j, :], ALU.mult, ALU.add)
                    P.stt(vx[:, j, :], up[:, 2:n + 2], w2_, vx[:, j, :], ALU.mult, ALU.add)
            ny = P.sbuf("hny", [128, 1 + n], BF16)
            P.dma(ny[:], CI["ny" + tag][:, :])
            zin = P.sbuf("hzin", [128, 2, n], F32)
            zb = P.sbuf("hzb", [128, 2, n], BF16)
            ztok = P.sbuf("hztok", [128, nt, 256], BF16)
            Y = P.sbuf("hY", [128, 2 * NF, 256], BF16)
            yn = P.sbuf("hyn", [1, 256], BF16)
            NFB = 3 if n > 256 else 2
            fbuf = [P.sbuf("hF%d" % i, [128, nt, 256], BF16) for i in range(NFB)]
            hfb = [P.sbuf("hhf%d" % i, [128, 2, 256], F32) for i in range(NFB)]
            gbuf = [P.sbuf("hG%d" % i, [128, 2 * NF, cwi], BF16) for i in range(2)]
            ta = P.sbuf("hta", [128, 2, 256], F32)
            tb = P.sbuf("htb", [128, 2, 256], F32)
            oh = P.sbuf("hoh", [128, 2, n], BF16)
            taf = ta[:, :, :].re("p h c -> p (h c)")
            for order in range(2):
                src = vx[:, 0:2, :] if order == 0 else zin[:, :, :]
                P.copy(zb[:, 0, :], src[:, 0, :], eng="act")
                P.copy(zb[:, 1, :], src[:, 1, :], eng="dve")
                for s in range(nt):
                    for ch in range(2):
                        P.transpose(psb[:, ch * 128:(ch + 1) * 128], zb[:, ch, s * 128:(s + 1) * 128], IDB)
                    evac(ztok[:, s, :], psb[:, 0:256])
                for s in range(nt):
                    P.mm(ps[2][0:1, 0:256], ny[:, 0:1], ztok[:, s, :], start=(s == 0), stop=(s == nt - 1))
                P.tt(yn[:, :], ps[2][0:1, 0:256], hn[tag][:, order * 256:(order + 1) * 256], ALU.mult)
                for fk in range(NF):
                    fb = fbuf[fk % NFB]
                    P.dma(fb[:], CI["F" + tag][fk, :, :, :], eng=q2(fk))
                    hb = hfb[fk % NFB]
                    P.dma(hb[:], hf_d[tag][order, fk, :, :, :], eng=q2(fk + 1))
                    pz = ps[fk % 2]
                    for half in range(2):
                        for s in range(nt):
                            P.mm(pz[:, half * 256:(half + 1) * 256], fb[:, s, half * 128:(half + 1) * 128],
                                 ztok[:, s, :], start=(s == 0), stop=(s == nt - 1))
                    zv = pz[:, :].re("p (h c) -> p h c", h=2)
                    P.tt(ta[:], zv, hb[:, 0:1, :].bc([128, 2, 256]), ALU.mult)
                    P.tt(tb[:], zv, hb[:, 1:2, :].bc([128, 2, 256]), ALU.mult)
                    P.tt(Y[:, fk, :], ta[:, 0, :], tb[:, 1, :], ALU.subtract)
                    P.tt(Y[:, NF + fk, :], tb[:, 0, :], ta[:, 1, :], ALU.add)
                for c in range(ncwi):
                    sl = slice(c * cwi, (c + 1) * cwi)
                    gb = gbuf[c % 2]
                    P.dma(gb[:], CI["G" + tag][c, :, :, :], eng=q2(c))
                    for ch in range(2):
                        pt = ps[2 + (2 * c + ch) % 4]
                        for r in range(2 * NF):
                            P.mm(pt[:, 0:cwi], Y[:, r, ch * 128:(ch + 1) * 128], gb[:, r, :],
                                 start=(r == 0), stop=False)
                        P.mm(pt[:, 0:cwi], yn[0:1, ch * 128:(ch + 1) * 128], ny[0:1, 1 + c * cwi:1 + (c + 1) * cwi],
                             start=False, stop=True)
                        bias = vec("hbias", order * 2 + ch)
                        if order == 0:
                            P.stt(zin[:, ch, sl], vx[:, ch, sl], bias, pt[:, 0:cwi], ALU.mult, ALU.add)
                            P.tt(zin[:, ch, sl], zin[:, ch, sl], vx[:, 2 + ch, sl], ALU.mult)
                        else:
                            P.stt(taf[:, 0:cwi], zin[:, ch, sl], bias, pt[:, 0:cwi], ALU.mult, ALU.add)
                            P.tt(oh[:, ch, sl], taf[:, 0:cwi], vx[:, 4 + ch, sl], ALU.mult)
            for ch in range(2):
                P.dma(S["mix"][512 + ch * 128:512 + (ch + 1) * 128, :], oh[:, ch, :], eng=q2(ch))

        P.checkpoint("D" + tag)
        with P.scope():
            wo = P.sbuf("wo", [128, 8, D], BF16)
            wo64 = P.sbuf("wo64", [64, 8, D], BF16)
            P.dma(wo[:], wout_bf[:, :, :], eng="sp")
            P.dma(wo64[:], wout64_bf[:, :, :], eng="sp")
            m128 = P.sbuf("m128", [128, 4, n], BF16)
            m64 = P.sbuf("m64", [64, 8, n], BF16)
            for j, r0 in enumerate((0, 128, 512, 640)):
                P.dma(m128[:, j, :], S["mix"][r0:r0 + 128, :], eng=q2(j))
            for j in range(8):
                r0 = (256 + j * 64) if j < 4 else (768 + (j - 4) * 64)
                P.dma(m64[:, j, :], S["mix"][r0:r0 + 64, :], eng=q2(j))
            xin = [P.sbuf("exin%d" % i, [128, n], F32) for i in range(2)]
            for i in range(8):
                xi = xin[i % 2]
                P.dma(xi[:, :], x_src[b, i * 128:(i + 1) * 128, :], eng=q2(i))
                for c in range(ncw):
                    sl = slice(c * cw, (c + 1) * cw)
                    pt = ps[(i * ncw + c) % 2]
                    osl = slice(i * 128, (i + 1) * 128)
                    mlist = []
                    for j, kc in enumerate((0, 1, 4, 5)):
                        mlist.append((wo[:, kc, osl], m128[:, j, sl]))
                    for j in range(8):
                        mlist.append((wo64[:, j, osl], m64[:, j, sl]))
                    for mi, (lh, rh) in enumerate(mlist):
                        P.mm(pt[:, 0:cw], lh, rh, start=(mi == 0), stop=(mi == len(mlist) - 1))
                    P.stt(xi[:, sl], pt[:, 0:cw], modT[:, 16 + i, bcol:bcol + 1], xi[:, sl], ALU.mult, ALU.add)
                P.dma(S["xmid"][i * 128:(i + 1) * 128, :], xi[:, :], eng=q2(i))

        P.checkpoint("E" + tag)
        with P.scope():
            h2 = P.sbuf("h2", [128, 8, n], BF16)
            norm_mod(h2, lambda sl: S["xmid"][:, sl].re("(k p) n -> p k n", p=128), n, "n2g", 24, 32, bcol)
            gm = P.sbuf("gm", [16, n], F32)
            P.checkpoint("F1" + tag)
            with P.scope():
                wr = LW["wr"]
                lg = ps[0]
                for t in range(nt):
                    for k in range(8):
                        P.mm(lg[:, t * 16:(t + 1) * 16], h2[:, k, t * 128:(t + 1) * 128], wr[:, k, :],
                             start=(k == 0), stop=(k == 7))
                aff = P.sbuf("aff", [128, nt, 16], F32)
                mx = P.sbuf("affmx", [128, nt], F32)
                lv = lg[:, 0:nt * 16].re("p (t e) -> p t e", e=16)
                P.reduce(mx[:], lv, ALU.max)
                P.tt(aff[:], lv, mx[:].re("p (t o) -> p t o", o=1).bc([128, nt, 16]), ALU.subtract)
                P.act(aff[:], aff[:], AF.Exp)
                P.reduce(mx[:], aff[:], ALU.add)
                P.recip(mx[:], mx[:])
                P.tt(aff[:], aff[:], mx[:].re("p (t o) -> p t o", o=1).bc([128, nt, 16]), ALU.mult)
                affT = P.sbuf("affT", [16, n], F32)
                for t in range(nt):
                    pt = ps[1 + (t // 4) % 2]
                    P.transpose(pt[0:16, (t % 4) * 128:(t % 4 + 1) * 128], aff[:, t, :], IDF)
                    if t % 4 == 3 or t == nt - 1:
                        t0 = (t // 4) * 4
                        wdt = (t - t0 + 1) * 128
                        evac(affT[:, t0 * 128:t0 * 128 + wdt], pt[0:16, 0:wdt])
                work = P.sbuf("tkwork", [16, n], F32)
                mx8 = P.sbuf("tkmx8", [16, 8], F32)
                cap = n // 8
                cur = affT
                for it in range(cap // 8):
                    P.op("dve", (lambda c_: (lambda e: e.max(out=mx8.h[:], in_=c_.h[:])))(cur),
                         reads=[cur[:]], writes=[mx8[:]])
                    P.op("dve", (lambda c_: (lambda e: e.match_replace(out=work.h[:], in_to_replace=mx8.h[:],
                                                                        in_values=c_.h[:], imm_value=-1.0)))(cur),
                         reads=[cur[:], mx8[:]], writes=[work[:]])
                    cur = work
                P.ts(work[:], work[:], 0.0, ALU.is_lt)
                P.tt(gm[:], work[:], affT[:], ALU.mult)
            if debug:
                P.dma(dbg_gm[tag][:, :], gm[:, :])
            P.checkpoint("F2" + tag)
            gmb = P.sbuf("gmb", [16, n], BF16)
            P.copy(gmb[:], gm[:])
            P.checkpoint("F2b" + tag)
            with P.scope():
                yacc = P.sbuf("yacc", [128, 8, n], F32)
                with P.scope():
                    sel = P.sbuf("sel", [16, 16, 128], BF16)
                    P.dma(sel[:], sel_in[:, :, :])
                    wgu = [P.sbuf("wgu%d" % i, [128, 8, 1024], BF16) for i in range(2)]
                    wdn = [P.sbuf("wdn%d" % i, [128, 4, D], BF16) for i in range(2)]
                    gbc = [P.sbuf("gbc%d" % i, [128, cw], F32) for i in range(2)]
                    sa = P.sbuf("sa", [128, cw], F32)
                    hid = [P.sbuf("hid%d" % i, [128, 4, cw], BF16) for i in range(2)]
                    for e in range(nexp):
                        wg_ = wgu[e % 2]
                        wd_ = wdn[e % 2]
                        P.dma(wg_[:], wgu_bf[e % 2][e // 2, :, :, :], eng=q2(e))
                        P.dma(wd_[:], wdn_bf[e % 2][e // 2, :, :, :], eng=q2(e + 1))
                        for c in range(ncw):
                            sl = slice(c * cw, (c + 1) * cw)
                            gb = gbc[c % 2]
                            P.mm(ps[6][:, 0:cw], sel[:, e, :], gmb[:, sl])
                            P.copy(gb[:], ps[6][:, 0:cw], eng="act")
                            hd = hid[c % 2]
                            for j in range(4):
                                pa = ps[(j % 2) * 2]
                                pu = ps[(j % 2) * 2 + 1]
                                for k in range(8):
                                    P.mm(pa[:, 0:cw], wg_[:, k, j * 128:(j + 1) * 128], h2[:, k, sl],
                                         start=(k == 0), stop=(k == 7))
                                for k in range(8):
                                    P.mm(pu[:, 0:cw], wg_[:, k, 512 + j * 128:512 + (j + 1) * 128], h2[:, k, sl],
                                         start=(k == 0), stop=(k == 7))
                                P.act(sa[:], pa[:, 0:cw], AF.Silu)
                                P.tt(sa[:], sa[:], gb[:], ALU.mult)
                                P.tt(hd[:, j, :], pu[:, 0:cw], sa[:], ALU.mult)
                            for i in range(8):
                                py = ps[4 + i % 2]
                                for j in range(4):
                                    P.mm(py[:, 0:cw], wd_[:, j, i * 128:(i + 1) * 128], hd[:, j, :],
                                         start=(j == 0), stop=(j == 3))
                                if e == 0:
                                    evac(yacc[:, i, sl], py[:, 0:cw])
                                else:
                                    P.tt(yacc[:, i, sl], yacc[:, i, sl], py[:, 0:cw], ALU.add)
                xin = [P.sbuf("fxin%d" % i, [128, n], F32) for i in range(2)]
                for i in range(8):
                    xi = xin[i % 2]
                    P.dma(xi[:, :], S["xmid"][i * 128:(i + 1) * 128, :], eng=q2(i))
                    P.stt(xi[:, :], yacc[:, i, :], modT[:, 40 + i, bcol:bcol + 1], xi[:, :], ALU.mult, ALU.add)
                    r = P.dma(x_dst[b, i * 128:(i + 1) * 128, :], xi[:, :], eng=q2(i))
                    if last and not is_ctx:
                        final.append(r)

    final = []
    try:
        for l in range(depth):
            last = (l == DEPTH - 1)
            precast_dense(l)
            precast_experts(l)
            layer_prologue(l)
            P.checkpoint("prologue")
            filter_gen(l, SEQ, "L")
            P.checkpoint("filtL")
            if not last:
                filter_gen(l, NCTX, "C")
                P.checkpoint("filtC")
            for b in range(nb):
                with P.scope():
                    KV["kg"] = P.sbuf("kg_all", [128, SEQ + NCTX], BF16)
                    KV["vg"] = P.sbuf("vg_all", [128, 18, 2, 65], BF16)
                    KV["km"] = P.sbuf("km_all", [96, 4, SEQ + NCTX], BF16)
                    KV["vm"] = P.sbuf("vm_all", [128, 18, 4, 65], BF16)
                    P.memset(KV["vg"][:, :, :, 64:65], 1.0)
                    P.memset(KV["vm"][:, :, :, 64:65], 1.0)
                    seq_front(l, b, NCTX, True, last)
                    P.checkpoint("frontC")
                    seq_front(l, b, SEQ, False, last)
                    P.checkpoint("frontL")
                seq_back(l, b, SEQ, False, last)
                P.checkpoint("backL")
                if not last:
                    seq_back(l, b, NCTX, True, last)
                    P.checkpoint("backC")
    except StopBuild:
        lastrec = {}
        for r in P.recs:
            if r.is_dma:
                lastrec[("d", r.sem_idx)] = r
            else:
                lastrec[r.eng] = r
        final = list(lastrec.values())
    P.finish(final_waits=final)
    return nc, P


def prep_shared(inp):
    C = get_consts()
    sh = dict(C)
    sh["vecs"] = np.stack([pack_vecs(inp, l) for l in range(DEPTH)], 0)
    sh["w_mod"] = np.ascontiguousarray(inp["w_mod"], np.float32)
    w_in = np.asarray(inp["w_in"], np.float32)
    perm = np.arange(1952)
    q0 = 416
    perm[q0:q0 + 256] = np.concatenate([q0 + np.arange(0, 64), q0 + np.arange(128, 192),
                                        q0 + np.arange(64, 128), q0 + np.arange(192, 256)])
    w_in = w_in[:, :, perm]
    wflat = np.zeros((DEPTH, 128, 8 * 1952), np.float32)
    for l in range(DEPTH):
        wp = w_in[l].reshape(8, 128, 1952).transpose(1, 0, 2)
        for c0, m in W_IN_GROUPS:
            wflat[l, :, 8 * c0:8 * (c0 + m)] = wp[:, :, c0:c0 + m].reshape(128, 8 * m)
    sh["w_in"] = wflat
    w_out = np.asarray(inp["w_out"], np.float32)
    sh["w_out"] = np.ascontiguousarray(w_out.reshape(DEPTH, 8, 128, D).transpose(0, 2, 1, 3))
    wo64 = np.concatenate([w_out[:, 256:512, :], w_out[:, 768:1024, :]], 1)
    sh["w_out64"] = np.ascontiguousarray(wo64.reshape(DEPTH, 8, 64, D).transpose(0, 2, 1, 3))
    pw = np.asarray(inp["pool_w"], np.float32)
    bd = np.zeros((DEPTH, 2, 128, 128), np.float32)
    for l in range(DEPTH):
        for g in range(4):
            o = (g % 2) * 64
            bd[l, g // 2, o:o + 64, o:o + 64] = pw[l, g]
    sh["poolbd"] = bd
    for k in ("hy_f_w1", "hy_f_w2", "hy_f_w3", "mla_w_uq", "router_w"):
        sh[k] = np.ascontiguousarray(inp[k], np.float32)
    wg = np.asarray(inp["exp_w_gate"], np.float32).reshape(DEPTH, NEXP, 8, 128, 512)
    wu = np.asarray(inp["exp_w_up"], np.float32).reshape(DEPTH, NEXP, 8, 128, 512)
    sh["exp_w_gu"] = np.ascontiguousarray(np.concatenate([wg, wu], -1).transpose(0, 1, 3, 2, 4))
    wd = np.asarray(inp["exp_w_down"], np.float32).reshape(DEPTH, NEXP, 4, 128, D)
    sh["exp_w_dn"] = np.ascontiguousarray(wd.transpose(0, 1, 3, 2, 4))
    wukv = np.asarray(inp["mla_w_ukv"], np.float32).reshape(DEPTH, 128, 4, 128)
    wk = np.zeros((DEPTH, 128, 4, 96), np.float32)
    wk[:, :, :, 0:64] = wukv[:, :, :, 0:64]
    sh["wukv_k"] = wk
    sh["wukv_v"] = np.ascontiguousarray(wukv[:, :, :, 64:128].reshape(DEPTH, 128, 256))
    return sh


def make_in_maps(inp, nb, n_cores=8):
    sh = prep_shared(inp)
    x = np.asarray(inp["x"], np.float32)
    ctx = np.asarray(inp["ctx"], np.float32)
    c = np.asarray(inp["c"], np.float32)
    c_ctx = np.asarray(inp["c_ctx"], np.float32)
    in_maps = []
    for core in range(n_cores):
        bs = slice(core * nb, (core + 1) * nb)
        m = dict(sh)
        m["xT"] = np.ascontiguousarray(x[bs].transpose(0, 2, 1))
        m["ctxT"] = np.ascontiguousarray(ctx[bs].transpose(0, 2, 1))
        cc = np.concatenate([c[bs], c_ctx[None, :]], 0)
        m["cT"] = np.ascontiguousarray(cc.reshape(nb + 1, 8, 128).transpose(2, 1, 0))
        in_maps.append(m)
    return in_maps


def kernel(**inp):
    inp = {k: np.asarray(v) for k, v in inp.items()}
    n_cores = 8
    in_maps = make_in_maps(inp, NB, n_cores)
    nc, _ = build_program()
    res = run_bass_kernel_spmd(nc, in_maps, core_ids=list(range(n_cores)))
    out = np.empty((32, SEQ, D), np.float32)
    for core in range(n_cores):
        oT = np.asarray(res.results[core]["outT"], np.float32)
        out[core * NB:(core + 1) * NB] = oT.transpose(0, 2, 1)
    return out
```

```python
import math
import numpy as np
import ml_dtypes
import concourse.bass as bass
import concourse.mybir as mybir
from concourse.bass_utils import run_bass_kernel_spmd

F32 = mybir.dt.float32
BF16 = mybir.dt.bfloat16
ALU = mybir.AluOpType
AF = mybir.ActivationFunctionType
AX = mybir.AxisListType
NPBF = ml_dtypes.bfloat16

ENGS = ("pe", "act", "dve", "pool", "sp")
N_DMA_SEMS = 48

D = 1024
SEQ = 2048
NCTX = 256
DEPTH = 2
NB = 4
NEXP = 16
EPS = 1e-6
MAGIC = 12582912.0
TWO_PI = 2.0 * math.pi


class Tile:
    def __init__(self, name, handle, space, const=False):
        self.name = name
        self.h = handle
        self.space = space
        self.const = const
        self.last_write = None
        self.reads = []
        self.sem_idx = None

    def __getitem__(self, idx):
        return V(self, self.h[idx])


class V:
    def __init__(self, tile, ap):
        self.tile = tile
        self.ap = ap

    def __getitem__(self, idx):
        return V(self.tile, self.ap[idx])

    def re(self, pat, **kw):
        return V(self.tile, self.ap.rearrange(pat, **kw))

    def bc(self, shape):
        return V(self.tile, self.ap.to_broadcast(list(shape)))


class Rec:
    __slots__ = ("eng", "fn", "deps", "signaled", "count", "is_dma", "sem_idx")

    def __init__(self, eng, fn, deps, is_dma=False):
        self.eng = eng
        self.fn = fn
        self.deps = deps
        self.signaled = False
        self.count = None
        self.is_dma = is_dma
        self.sem_idx = None


class StopBuild(Exception):
    pass


class Scope:
    def __init__(self, P):
        self.P = P

    def __enter__(self):
        self.P.scope_stack.append([])
        return self

    def __exit__(self, *a):
        self.P.barrier()
        for cm in reversed(self.P.scope_stack.pop()):
            cm.__exit__(None, None, None)
        return False


class Prog:
    def __init__(self, nc, same_engine_sync=True):
        self.nc = nc
        self.recs = []
        self.same_engine_sync = same_engine_sync
        self.scope_stack = [[]]
        self.bar_deps = {e: [] for e in ENGS}
        self.bar_start = 0
        self.next_sem = 0
        self.uid = 0
        self.stop_at = None
        self.ckpts = []

    def checkpoint(self, name):
        self.ckpts.append((name, len(self.recs)))
        if self.stop_at is not None and name == self.stop_at:
            raise StopBuild()

    def scope(self):
        return Scope(self)

    def _enter(self, cm):
        v = cm.__enter__()
        self.scope_stack[-1].append(cm)
        return v

    def _name(self, name):
        self.uid += 1
        return "%s_%d" % (name, self.uid)

    def sbuf(self, name, shape, dt):
        h = self._enter(self.nc.sbuf_tensor(self._name(name), list(shape), dt))
        return Tile(name, h, "sbuf")

    def psum(self, name, shape, dt=F32):
        h = self._enter(self.nc.psum_tensor(self._name(name), list(shape), dt))
        return Tile(name, h, "psum")

    def dram(self, name, shape, dt, kind="Internal"):
        h = self.nc.dram_tensor(name, list(shape), dt, kind=kind)
        return Tile(name, h, "dram", const=(kind == "ExternalInput"))

    def barrier(self):
        last = {}
        dmas = []
        for r in self.recs[self.bar_start:]:
            if r.is_dma:
                dmas.append(r)
            else:
                last[r.eng] = r
        self.bar_start = len(self.recs)
        new = list(last.values()) + dmas
        for e in ENGS:
            self.bar_deps[e] = self.bar_deps[e] + new

    def _deps(self, eng, reads, writes):
        deps = list(self.bar_deps[eng])
        self.bar_deps[eng] = []
        for t in reads:
            if t.const:
                continue
            if t.last_write is not None:
                deps.append(t.last_write)
        for t in writes:
            if t.last_write is not None:
                deps.append(t.last_write)
            deps.extend(t.reads)
        return deps

    def _commit(self, rec, reads, writes):
        for t in writes:
            t.last_write = rec
            t.reads = []
        for t in reads:
            if t.const or t in writes:
                continue
            t.reads.append(rec)
            if len(t.reads) > 48:
                keep = {}
                rest = {}
                for r in t.reads:
                    if r.is_dma:
                        rest[r.sem_idx] = r
                    else:
                        keep[r.eng] = r
                t.reads = list(rest.values()) + list(keep.values())
        self.recs.append(rec)

    def op(self, eng, fn, reads=(), writes=()):
        reads = [v.tile for v in reads]
        writes = [v.tile for v in writes]
        rec = Rec(eng, fn, self._deps(eng, reads, writes))
        self._commit(rec, reads, writes)
        return rec

    def dma(self, out, in_, eng="sp", **kw):
        reads = [in_.tile]
        writes = [out.tile]
        o_ap, i_ap = out.ap, in_.ap
        rec = Rec(eng, lambda e: e.dma_start(out=o_ap, in_=i_ap, **kw), self._deps(eng, reads, writes), is_dma=True)
        t = out.tile
        if t.sem_idx is None:
            t.sem_idx = self.next_sem % N_DMA_SEMS
            self.next_sem += 1
        rec.sem_idx = t.sem_idx
        self._commit(rec, reads, writes)
        return rec

    def mm(self, out, lhsT, rhs, start=True, stop=True):
        o, l, r = out.ap, lhsT.ap, rhs.ap
        return self.op("pe", lambda e: e.matmul(o, l, r, start=start, stop=stop),
                       reads=[lhsT, rhs], writes=[out])

    def transpose(self, out, in_, ident):
        o, i, d = out.ap, in_.ap, ident.ap
        return self.op("pe", lambda e: e.transpose(o, i, d), reads=[in_, ident], writes=[out])

    def act(self, out, in_, func, bias=None, scale=1.0):
        o, i = out.ap, in_.ap
        reads = [in_]
        kw = {}
        if bias is not None:
            if isinstance(bias, V):
                reads.append(bias)
                kw["bias"] = bias.ap
            else:
                kw["bias"] = bias
        if isinstance(scale, V):
            reads.append(scale)
            kw["scale"] = scale.ap
        else:
            kw["scale"] = scale
        return self.op("act", lambda e: e.activation(o, i, func, **kw), reads=reads, writes=[out])

    def tt(self, out, in0, in1, op, eng="dve"):
        o, a, b = out.ap, in0.ap, in1.ap
        return self.op(eng, lambda e: e.tensor_tensor(o, a, b, op), reads=[in0, in1], writes=[out])

    def ts(self, out, in0, s1, op0, s2=None, op1=None, eng="dve"):
        o, a = out.ap, in0.ap
        reads = [in0]
        if isinstance(s1, V):
            reads.append(s1)
            s1 = s1.ap
        if isinstance(s2, V):
            reads.append(s2)
            s2 = s2.ap
        kw = {}
        if op1 is not None:
            kw["op1"] = op1
        return self.op(eng, lambda e: e.tensor_scalar(o, a, s1, s2, op0, **kw), reads=reads, writes=[out])

    def stt(self, out, in0, scalar, in1, op0, op1, eng="dve"):
        o, a, b = out.ap, in0.ap, in1.ap
        reads = [in0, in1]
        if isinstance(scalar, V):
            reads.append(scalar)
            scalar = scalar.ap
        return self.op(eng, lambda e: e.scalar_tensor_tensor(o, a, scalar, b, op0, op1), reads=reads, writes=[out])

    def copy(self, out, in_, eng="dve"):
        o, i = out.ap, in_.ap
        if eng == "act":
            return self.op("act", lambda e: e.copy(o, i), reads=[in_], writes=[out])
        return self.op(eng, lambda e: e.tensor_copy(o, i), reads=[in_], writes=[out])

    def memset(self, out, val, eng="dve"):
        o = out.ap
        return self.op(eng, lambda e: e.memset(o, val), reads=[], writes=[out])

    def reduce(self, out, in_, op, axis=AX.X):
        o, i = out.ap, in_.ap
        return self.op("dve", lambda e: e.tensor_reduce(o, i, axis, op), reads=[in_], writes=[out])

    def recip(self, out, in_):
        o, i = out.ap, in_.ap
        return self.op("dve", lambda e: e.reciprocal(o, i), reads=[in_], writes=[out])

    def _skip(self, d, r):
        return (not r.is_dma) and d.eng == r.eng and (d.eng == "pe" or not self.same_engine_sync)

    def finish(self, final_waits=()):
        nc = self.nc
        for r in self.recs:
            for d in r.deps:
                if d.is_dma or self._skip(d, r):
                    continue
                d.signaled = True
        for r in final_waits:
            if not r.is_dma:
                r.signaled = True
        cnt = {e: 0 for e in ENGS}
        for r in self.recs:
            if r.is_dma or not r.signaled:
                continue
            cnt[r.eng] += 1
            r.count = cnt[r.eng]
        cms = []
        sems = {}
        for e in ENGS:
            cm = nc.semaphore("s_" + e)
            sems[e] = cm.__enter__()
            cms.append(cm)
        dsems = []
        for i in range(min(N_DMA_SEMS, max(1, self.next_sem))):
            cm = nc.semaphore("d_%d" % i)
            dsems.append(cm.__enter__())
            cms.append(cm)
        streams = {e: [] for e in ENGS}
        seen = {e: {} for e in ENGS}
        dma_emitted = {}
        for r in self.recs:
            waits = {}
            for d in r.deps:
                if d.is_dma:
                    key = ("d", d.sem_idx)
                    val = 16 * dma_emitted[d.sem_idx]
                    sem = dsems[d.sem_idx]
                else:
                    if self._skip(d, r):
                        continue
                    key = ("c", d.eng)
                    val = d.count
                    sem = sems[d.eng]
                if seen[r.eng].get(key, 0) >= val:
                    continue
                if key not in waits or waits[key][1] < val:
                    waits[key] = (sem, val)
            for key, (sem, val) in waits.items():
                seen[r.eng][key] = val
            if r.is_dma:
                dma_emitted[r.sem_idx] = dma_emitted.get(r.sem_idx, 0) + 1
            streams[r.eng].append((list(waits.values()), r))
            r.deps = None
        fw = {}
        for r in final_waits:
            if r.is_dma:
                fw[("d", r.sem_idx)] = (dsems[r.sem_idx], 16 * dma_emitted[r.sem_idx])
            else:
                key = ("c", r.eng)
                if key not in fw or fw[key][1] < r.count:
                    fw[key] = (sems[r.eng], r.count)
        self.stats = {e: len(streams[e]) for e in ENGS}
        self.stats["sem_counts"] = dict(cnt)

        def make(ename):
            def body(e):
                for waits, r in streams[ename]:
                    for sem, val in waits:
                        e.wait_ge(sem, val)
                    ins = r.fn(e)
                    if r.is_dma:
                        ins.then_inc(dsems[r.sem_idx], 16)
                    elif r.signaled:
                        ins.then_inc(sems[r.eng], 1)
                if ename == "sp":
                    for sem, val in fw.values():
                        e.wait_ge(sem, val)
            return body

        with nc.Block() as block:
            block.tensor(make("pe"))
            block.scalar(make("act"))
            block.vector(make("dve"))
            block.gpsimd(make("pool"))
            block.sync(make("sp"))
        for cm in reversed(cms):
            cm.__exit__(None, None, None)
        while self.scope_stack:
            for cm in reversed(self.scope_stack.pop()):
                cm.__exit__(None, None, None)


VEC_LAYOUT = [("n1g", 8), ("n2g", 8), ("bmod", 48), ("pscale", 2), ("gq", 1), ("gk", 1), ("hcw", 18),
              ("hcb", 6), ("fb1", 1), ("fb2", 1), ("ffr", 1), ("hbias", 4), ("cqg", 2), ("ckvg", 1),
              ("mqg", 1), ("mkg", 1)]
W_IN_GROUPS = [(0, 256), (256, 160), (416, 256), (672, 256)] + [(928 + 128 * j, 128) for j in range(8)]
VEC_OFF = {}
_o = 0
for _n, _k in VEC_LAYOUT:
    VEC_OFF[_n] = (_o, _k)
    _o += _k
NV = _o


def _rot_tables(m, p):
    inv = (10000.0 ** (-np.arange(m, dtype=np.float32) / m)).astype(np.float32)
    ang = (p[None, :].astype(np.float32) * inv[:, None]).astype(np.float32)
    c = np.concatenate([np.cos(ang), np.cos(ang)], 0)
    s = np.concatenate([np.sin(ang), np.sin(ang)], 0)
    return c.astype(np.float32), s.astype(np.float32)


def _rot_matrix(dim, blocks):
    R = np.zeros((dim, dim), np.float32)
    for st, m in blocks:
        for d in range(m):
            R[st + d, st + d + m] = -1.0
            R[st + d + m, st + d] = 1.0
    return np.ascontiguousarray(R.T)


def build_consts():
    C = {}
    pos = np.arange(SEQ)
    row = (pos // 64).astype(np.float32)
    col = (pos % 64).astype(np.float32)
    c1, s1 = _rot_tables(16, row)
    c2, s2 = _rot_tables(16, col)
    c64 = np.concatenate([c1, c2], 0)
    s64 = np.concatenate([s1, s2], 0)
    C["ropeg"] = np.ascontiguousarray(np.stack([np.concatenate([c64, c64], 0), np.concatenate([s64, s64], 0)], 1)).astype(NPBF)
    c1, s1 = _rot_tables(8, row)
    c2, s2 = _rot_tables(8, col)
    c96 = np.concatenate([np.ones((64, SEQ), np.float32), c1, c2], 0)
    s96 = np.concatenate([np.zeros((64, SEQ), np.float32), s1, s2], 0)
    rm = np.zeros((2, 128, SEQ), np.float32)
    rm[0, :96] = c96
    rm[1, :96] = s96
    C["ropem"] = np.ascontiguousarray(rm.transpose(1, 0, 2)).astype(NPBF)
    rg = _rot_matrix(128, [(0, 16), (32, 16), (64, 16), (96, 16)])
    rmm = np.zeros((128, 128), np.float32)
    rmm[:96, :96] = _rot_matrix(96, [(64, 8), (80, 8)])
    mats = np.zeros((6, 128, 128), np.float32)
    mats[0] = np.eye(128)
    mats[1] = 1.0
    mats[2, :64, :64] = 1.0
    mats[2, 64:, 64:] = 1.0
    mats[3] = rg
    mats[4] = rmm
    mats[5, :32, 64:96] = np.eye(32)
    C["mats"] = mats
    sel = np.zeros((16, 16, 128), np.float32)
    for e in range(16):
        sel[e, e, :] = 1.0
    C["sel"] = sel.astype(NPBF)
    for n, tag in ((SEQ, "L"), (NCTX, "C")):
        N = 2 * n
        th = 2.0 * np.pi / N
        nt = n // 128
        cw = min(512, n)
        ncw = n // cw
        t = np.linspace(0.0, 1.0, n, dtype=np.float32)[:, None]
        lag = np.arange(n, dtype=np.float32)[:, None]
        bands = np.linspace(1e-4, 7.0, 8, dtype=np.float32)[None, :]
        ang = (np.float32(2.0 * math.pi / n) * lag * bands).astype(np.float32)
        feats = np.concatenate([t, np.cos(ang), -np.sin(ang)], -1).astype(np.float32)
        C["feats" + tag] = np.ascontiguousarray(feats.T)
        deltas = np.abs(np.linspace(math.log(1e-2) / 1.5, math.log(1e-2) / 0.3, 256, dtype=np.float32))
        dec = np.exp(-t * deltas[None, :]).astype(np.float32)
        C["decay" + tag] = np.ascontiguousarray(dec.T.reshape(2, 128, n))
        ic = np.zeros((2, 128, n), np.float32)
        tt = np.arange(n)
        for gi, w in enumerate((2, 4, 8, 16)):
            left = w // 2
            right = w - 1 - left
            lo = np.clip(tt - left, 0, n)
            hi = np.clip(tt + right + 1, 0, n)
            ic[gi // 2, (gi % 2) * 64:(gi % 2) * 64 + 64, :] = (1.0 / (hi - lo).astype(np.float32))[None, :]
        C["invcnt" + tag] = ic
        f = np.arange(n, dtype=np.float64)
        p = np.arange(n, dtype=np.float64)
        A = th * np.outer(p, f)
        Fc = np.cos(A)
        Fs = -np.sin(A)
        Fh = np.zeros((nt, 128, nt, 256), np.float32)
        Fc4 = Fc.reshape(nt, 128, nt, 128)
        Fs4 = Fs.reshape(nt, 128, nt, 128)
        Fh[:, :, :, 0:128] = Fc4.transpose(2, 1, 0, 3)
        Fh[:, :, :, 128:256] = Fs4.transpose(2, 1, 0, 3)
        C["F" + tag] = Fh.astype(NPBF)
        wf = np.full(n, 2.0)
        wf[0] = 1.0
        Gr = (wf[:, None] / N) * np.cos(A.T)
        Gi = -(2.0 / N) * np.sin(A.T)
        G = np.concatenate([Gr, Gi], 0)
        cwi = 128
        G4 = G.reshape(2 * nt, 128, n // cwi, cwi)
        C["G" + tag] = np.ascontiguousarray(G4.transpose(2, 1, 0, 3)).astype(NPBF)
        ny = np.zeros((128, 1 + n), np.float32)
        ny[:, 0] = (-1.0) ** np.arange(128)
        ny[0, 1:] = ((-1.0) ** np.arange(n)) / N
        C["ny" + tag] = ny.astype(NPBF)
    return C


_CONSTS = None


def get_consts():
    global _CONSTS
    if _CONSTS is None:
        _CONSTS = build_consts()
    return _CONSTS


def fm(v, rows=128):
    v = np.asarray(v, np.float32)
    return np.ascontiguousarray(v.reshape(-1, rows).T)


def pack_vecs(inp, l):
    out = np.zeros((128, NV), np.float32)

    def put(name, arr):
        o, k = VEC_OFF[name]
        arr = np.asarray(arr, np.float32)
        out[:arr.shape[0], o:o + k] = arr.reshape(arr.shape[0], k)

    put("n1g", fm(inp["norm1_g"][l]))
    put("n2g", fm(inp["norm2_g"][l]))
    put("bmod", fm(inp["b_mod"][l]))
    put("pscale", fm(inp["pool_scale"][l]))
    put("gq", np.tile(inp["gqa_qnorm_g"][l], 2)[:, None])
    put("gk", np.tile(inp["gqa_knorm_g"][l], 2)[:, None])
    hw = np.asarray(inp["hy_conv_w"][l], np.float32)
    put("hcw", hw.reshape(3, 6, 128).transpose(2, 1, 0).reshape(128, 18))
    put("hcb", fm(inp["hy_conv_b"][l]))
    put("fb1", np.asarray(inp["hy_f_b1"][l])[:, None])
    put("fb2", np.asarray(inp["hy_f_b2"][l])[:, None])
    put("ffr", np.asarray(inp["hy_freq"][l])[:, None])
    hb = np.asarray(inp["hy_bias"][l], np.float32)
    put("hbias", hb.reshape(2, 2, 128).transpose(2, 0, 1).reshape(128, 4))
    put("cqg", fm(inp["mla_cq_g"][l]))
    put("ckvg", fm(inp["mla_ckv_g"][l]))
    put("mqg", np.asarray(inp["mla_qnorm_g"][l])[:, None])
    put("mkg", np.asarray(inp["mla_knorm_g"][l])[:, None])
    return out


def build_program(nb=NB, depth=DEPTH, debug=False, stop_at=None, nexp=NEXP):
    nc = bass.Bass("TRN2", target_bir_lowering=False)
    P = Prog(nc)
    P.stop_at = stop_at
    P.used_inputs = []

    class LazyIn:
        def __init__(self, name, shape, dt):
            self.name, self.shape, self.dt, self.t = name, shape, dt, None

        def __getitem__(self, idx):
            if self.t is None:
                self.t = P.dram(self.name, self.shape, self.dt, kind="ExternalInput")
                P.used_inputs.append(self.name)
            return self.t[idx]

    def inp(name, shape, dt=F32):
        return LazyIn(name, shape, dt)

    xT_in = inp("xT", [nb, D, SEQ])
    cxT_in = inp("ctxT", [nb, D, NCTX])
    cT_in = inp("cT", [128, 8, nb + 1])
    vecs_in = inp("vecs", [DEPTH, 128, NV])
    wmod_in = inp("w_mod", [DEPTH, D, 6 * D])
    win_in = inp("w_in", [DEPTH, 128, 8 * 1952])
    wout_in = inp("w_out", [DEPTH, 128, 8, D])
    wout64_in = inp("w_out64", [DEPTH, 64, 8, D])
    poolbd_in = inp("poolbd", [DEPTH, 2, 128, 128])
    hw1_in = inp("hy_f_w1", [DEPTH, 17, 64])
    hw2_in = inp("hy_f_w2", [DEPTH, 64, 64])
    hw3_in = inp("hy_f_w3", [DEPTH, 64, 1024])
    wuq_in = inp("mla_w_uq", [DEPTH, 256, 384])
    wukvk_in = inp("wukv_k", [DEPTH, 128, 4, 96])
    wukvv_in = inp("wukv_v", [DEPTH, 128, 256])
    wr_in = inp("router_w", [DEPTH, D, 16])
    wgu_in = inp("exp_w_gu", [DEPTH, nexp, 128, 8, 1024])
    wd_in = inp("exp_w_dn", [DEPTH, nexp, 128, 4, D])
    ropeg_in = inp("ropeg", [128, 2, SEQ], BF16)
    ropem_in = inp("ropem", [128, 2, SEQ], BF16)
    mats_in = inp("mats", [6, 128, 128])
    sel_in = inp("sel", [16, 16, 128], BF16)
    CI = {}
    for n, tag in ((SEQ, "L"), (NCTX, "C")):
        nt = n // 128
        cw = min(512, n)
        CI["feats" + tag] = inp("feats" + tag, [17, n])
        CI["decay" + tag] = inp("decay" + tag, [2, 128, n])
        CI["invcnt" + tag] = inp("invcnt" + tag, [2, 128, n])
        CI["F" + tag] = inp("F" + tag, [nt, 128, nt, 256], BF16)
        cwi = 128
        CI["G" + tag] = inp("G" + tag, [n // cwi, 128, 2 * nt, cwi], BF16)
        CI["ny" + tag] = inp("ny" + tag, [128, 1 + n], BF16)

    if debug == "force_decl":
        for t_ in (wgu_in, wd_in):
            t_[0, 0, 0:1, :, :]
    outT = P.dram("outT", [nb, D, SEQ], F32, kind="ExternalOutput")
    dbg = {}

    dk = "ExternalOutput" if debug else "Internal"
    xs_d = P.dram("xs_d", [nb, D, SEQ], F32, kind=dk)
    xc_d = P.dram("xc_d", [nb, D, NCTX], F32, kind=dk)
    SCR = {}
    for tag, n in (("L", SEQ), ("C", NCTX)):
        SCR[tag] = dict(
            xmid=P.dram("xmid_" + tag, [D, n], F32, kind=dk),
            mix=P.dram("mix_" + tag, [D, n], BF16, kind=dk),
            pu=P.dram("pu_" + tag, [256, n], F32, kind=dk),
            hu=P.dram("hu_" + tag, [768, n], F32, kind=dk),
        )
    hf_d = {"L": P.dram("hf_L", [2, SEQ // 128, 128, 2, 256], F32, kind=dk),
            "C": P.dram("hf_C", [2, NCTX // 128, 128, 2, 256], F32, kind=dk)}

    wgu_bf = [P.dram("wgu_bf%d" % i, [NEXP // 2, 128, 8, 1024], BF16) for i in range(2)]
    wdn_bf = [P.dram("wdn_bf%d" % i, [NEXP // 2, 128, 4, D], BF16) for i in range(2)]

    win_bf = P.dram("win_bf", [128, 8 * 1952], BF16)
    wout_bf = P.dram("wout_bf", [128, 8, D], BF16)
    wout64_bf = P.dram("wout64_bf", [64, 8, D], BF16)
    LW = {}

    def precast_dense(l):
        P.dma(win_bf[:, :], win_in[l, :, :], eng="pool")
        P.dma(wout_bf[:, :, :], wout_in[l, :, :, :], eng="pool")
        P.dma(wout64_bf[:, :, :], wout64_in[l, :, :, :], eng="pool")
        P.dma(LW["wk"][:], wukvk_in[l, :, :, :], eng="pool")
        P.dma(LW["wv"][:], wukvv_in[l, :, :], eng="pool")
        P.dma(LW["wuq"][:], wuq_in[l, :, :].re("(k p) n -> p k n", p=128), eng="pool")
        P.dma(LW["wbd"][:], poolbd_in[l, :, :, :].re("c p e -> p c e"), eng="pool")
        P.dma(LW["wr"][:], wr_in[l, :, :].re("(k p) e -> p k e", p=128), eng="pool")

    def precast_experts(l):
        for e in range(nexp):
            P.dma(wgu_bf[e % 2][e // 2, :, :, :], wgu_in[l, e, :, :, :], eng="pool")
            P.dma(wdn_bf[e % 2][e // 2, :, :, :], wd_in[l, e, :, :, :], eng="pool")

    dbg_gm = {}
    if debug:
        dbg_gm = {"L": P.dram("gm_L", [16, SEQ], F32, kind="ExternalOutput"),
                  "C": P.dram("gm_C", [16, NCTX], F32, kind="ExternalOutput")}

    mats = P.sbuf("mats", [128, 6, 128], F32)
    matsb = P.sbuf("matsb", [128, 6, 128], BF16)
    P.dma(mats[:], mats_in[:, :, :].re("m p c -> p m c"))
    P.dma(matsb[:], mats_in[:, :, :].re("m p c -> p m c"), eng="pool")
    IDF, ONF, BLKF, RG, RM, SELKR = [mats[:, i, :] for i in range(6)]
    IDB, ONB, BLKB = [matsb[:, i, :] for i in range(3)]
    SELKRB = matsb[:, 5, :]
    vecs = P.sbuf("vecs", [128, NV], F32)
    modT = P.sbuf("modT", [128, 48, nb + 1], F32)
    epsb = P.sbuf("epsb", [128, 1], F32)
    P.memset(epsb[:], EPS)
    hn = {"L": P.sbuf("hnL", [1, 512], F32), "C": P.sbuf("hnC", [1, 512], F32)}
    LW["wk"] = P.sbuf("wk", [128, 4, 96], BF16)
    LW["wv"] = P.sbuf("wv", [128, 256], BF16)
    LW["wuq"] = P.sbuf("wuq", [128, 2, 384], BF16)
    LW["wbd"] = P.sbuf("wbd", [128, 2, 128], BF16)
    LW["wr"] = P.sbuf("wr", [128, 8, 16], BF16)
    ps = [P.psum("ps%d" % i, [128, 512], F32) for i in range(7)]
    psb = P.psum("psb", [128, 1024], BF16)
    KV = {}

    def vec(name, j=0, rows=128):
        o, k = VEC_OFF[name]
        return vecs[0:rows, o + j:o + j + 1]

    evac_flip = [0]

    def evac(out, in_):
        evac_flip[0] ^= 1
        if evac_flip[0]:
            P.copy(out, in_, eng="act")
        else:
            P.copy(out, in_, eng="dve")

    def q2(i):
        return "sp"

    def layer_prologue(l):
        P.dma(vecs[:], vecs_in[l, :, :])
        with P.scope():
            sil = P.sbuf("sil", [128, 8, nb + 1], F32)
            P.dma(sil[:], cT_in[:, :, :])
            P.act(sil[:], sil[:], AF.Silu)
            wbuf = [P.sbuf("wmod%d" % i, [128, 8, 512], F32) for i in range(2)]
            for piece in range(12):
                wb = wbuf[piece % 2]
                P.dma(wb[:], wmod_in[l, :, piece * 512:(piece + 1) * 512].re("(k p) n -> p k n", p=128),
                      eng=q2(piece))
                for jj in range(4):
                    j = piece * 4 + jj
                    pt = ps[j % 2]
                    for k in range(8):
                        P.mm(pt[:, 0:nb + 1], wb[:, k, jj * 128:(jj + 1) * 128], sil[:, k, :],
                             start=(k == 0), stop=(k == 7))
                    P.ts(modT[:, j, :], pt[:, 0:nb + 1], vec("bmod", j), ALU.add)

    def filter_gen(l, n, tag):
        nt = n // 128
        cw = min(512, n)
        ncw = n // cw
        with P.scope():
            edT = P.sbuf("fedT", [128, 2, nt, 512], BF16)
            with P.scope():
                ed = P.sbuf("fed", [128, 2, 4, n], BF16)
                with P.scope():
                    hid2 = P.sbuf("hid2", [64, n], F32)
                    w3 = P.sbuf("fw3", [64, 1024], F32)
                    P.dma(w3[:], hw3_in[l, :, :])
                    with P.scope():
                        feats = P.sbuf("feats", [17, n], F32)
                        P.dma(feats[:], CI["feats" + tag][:, :])
                        w1 = P.sbuf("fw1", [17, 64], F32)
                        w2 = P.sbuf("fw2", [64, 64], F32)
                        P.dma(w1[:], hw1_in[l, :, :])
                        P.dma(w2[:], hw2_in[l, :, :])
                        frb = P.sbuf("frb", [64, 2], F32)
                        P.tt(frb[:, 0:1], vec("ffr", 0, 64), vec("fb1", 0, 64), ALU.mult)
                        P.tt(frb[:, 1:2], vec("ffr", 0, 64), vec("fb2", 0, 64), ALU.mult)
                        hid1 = P.sbuf("hid1", [64, n], F32)
                        arg = P.sbuf("farg", [64, cw], F32)
                        kk = P.sbuf("fkk", [64, cw], F32)

                        def sin_layer(dst, w, src, bcol):
                            for c in range(ncw):
                                sl = slice(c * cw, (c + 1) * cw)
                                P.mm(ps[0][0:64, 0:cw], w, src[:, sl])
                                P.ts(arg[:], ps[0][0:64, 0:cw], vec("ffr", 0, 64), ALU.mult,
                                     frb[:, bcol:bcol + 1], ALU.add)
                                P.ts(kk[:], arg[:], 1.0 / TWO_PI, ALU.mult, MAGIC, ALU.add)
                                P.ts(kk[:], kk[:], -MAGIC, ALU.add, TWO_PI, ALU.mult)
                                P.tt(arg[:], arg[:], kk[:], ALU.subtract)
                                P.ts(arg[:], arg[:], -3.141592, ALU.max, 3.141592, ALU.min)
                                P.act(dst[:, sl], arg[:], AF.Sin)

                        sin_layer(hid1, w1[:], feats, 0)
                        sin_layer(hid2, w2[:], hid1, 1)
                    decay = P.sbuf("decay", [128, 2, n], F32)
                    P.dma(decay[:], CI["decay" + tag][:, :, :].re("c p n -> p c n"))
                    hh = P.sbuf("hh", [128, 2, n], F32)
                    sq = P.sbuf("fsq", [128, n], F32)
                    tmp = P.sbuf("ftmp", [128, n], F32)
                    ssq = P.sbuf("ssq", [128, 2], F32)
                    rn = P.sbuf("frn", [128, 1], F32)
                    for q in range(4):
                        for d_ in range(2):
                            j = d_ * 4 + q
                            for c in range(ncw):
                                sl = slice(c * cw, (c + 1) * cw)
                                pt = ps[c % 2]
                                P.mm(pt[:, 0:cw], w3[:, j * 128:(j + 1) * 128], hid2[:, sl])
                                P.tt(hh[:, d_, sl], pt[:, 0:cw], decay[:, q % 2, sl], ALU.mult)
                        P.memset(hh[:, 1, 0:1], 0.0)
                        for d_ in range(2):
                            P.tt(sq[:], hh[:, d_, :], hh[:, d_, :], ALU.mult)
                            P.reduce(ssq[:, d_:d_ + 1], sq[:], ALU.add)
                        P.tt(rn[:], ssq[:, 0:1], ssq[:, 1:2], ALU.add)
                        P.act(rn[:], rn[:], AF.Sqrt, bias=epsb[:, 0:1], scale=1.0)
                        P.recip(rn[:], rn[:])
                        P.tt(tmp[:], hh[:, 0, :], hh[:, 1, :], ALU.add)
                        P.ts(ed[:, 0, q, :], tmp[:], rn[:, 0:1], ALU.mult)
                        P.tt(tmp[:], hh[:, 0, :], hh[:, 1, :], ALU.subtract)
                        P.ts(ed[:, 1, q, :], tmp[:], rn[:, 0:1], ALU.mult)
                for w_ in range(2):
                    for s in range(nt):
                        for q in range(4):
                            P.transpose(psb[:, q * 128:(q + 1) * 128], ed[:, w_, q, s * 128:(s + 1) * 128], IDB)
                        evac(edT[:, w_, s, :], psb[:, 0:512])
            ny = P.sbuf("fny", [128, 1 + n], BF16)
            P.dma(ny[:], CI["ny" + tag][:, :])
            for s in range(nt):
                P.mm(ps[2][0:1, :], ny[:, 0:1], edT[:, 0, s, :], start=(s == 0), stop=(s == nt - 1))
            P.copy(hn[tag][:], ps[2][0:1, :])
            fbuf = [P.sbuf("fF%d" % i, [128, nt, 256], BF16) for i in range(2)]
            hfs = [P.sbuf("fhfs%d" % i, [128, 2, 2, 256], F32) for i in range(2)]
            for fk in range(nt):
                fb = fbuf[fk % 2]
                P.dma(fb[:], CI["F" + tag][fk, :, :, :], eng=q2(fk))
                for half in range(2):
                    pt = ps[half]
                    for s in range(nt):
                        P.mm(pt[:, :], fb[:, s, half * 128:(half + 1) * 128], edT[:, half, s, :],
                             start=(s == 0), stop=(s == nt - 1))
                hs = hfs[fk % 2]
                evac(hs[:, :, 0, :], ps[0][:, :].re("p (o c) -> p o c", o=2))
                evac(hs[:, :, 1, :], ps[1][:, :].re("p (o c) -> p o c", o=2))
                for o in range(2):
                    P.dma(hf_d[tag][o, fk, :, :, :], hs[:, o, :, :], eng=q2(o))

    def norm_mod(dst_bf, x_view_fn, n, gname, sh_j, sc_j, bcol):
        cw = min(512, n)
        ncw = n // cw
        with P.scope():
            A = P.sbuf("nmA", [128, 8], F32)
            P.ts(A[:], modT[:, sc_j:sc_j + 8, bcol], 1.0, ALU.add)
            o, k = VEC_OFF[gname]
            P.tt(A[:], A[:], vecs[:, o:o + 8], ALU.mult)
            xfs = [P.sbuf("nmx%d" % i, [128, 8, cw], F32) for i in range(2)]
            sq = P.sbuf("nmsq", [128, 8, cw], BF16)
            rstd = P.sbuf("nmrstd", [128, cw], F32)
            tmp = P.sbuf("nmtmp", [128, cw], F32)
            for c in range(ncw):
                sl = slice(c * cw, (c + 1) * cw)
                xf = xfs[c % 2]
                P.dma(xf[:], x_view_fn(sl), eng=q2(c))
                P.act(sq[:, 0:4, :], xf[:, 0:4, :], AF.Square)
                P.tt(sq[:, 4:8, :], xf[:, 4:8, :], xf[:, 4:8, :], ALU.mult)
                pt = ps[c % 2]
                for k in range(8):
                    P.mm(pt[:, 0:cw], ONB, sq[:, k, :], start=(k == 0), stop=(k == 7))
                P.act(rstd[:], pt[:, 0:cw], AF.Sqrt, bias=epsb[:, 0:1], scale=1.0 / D)
                P.recip(rstd[:], rstd[:])
                for k in range(8):
                    P.tt(tmp[:], xf[:, k, :], rstd[:], ALU.mult)
                    P.ts(dst_bf[:, k, sl], tmp[:], A[:, k:k + 1], ALU.mult,
                         modT[:, sh_j + k, bcol:bcol + 1], ALU.add)

    HS = {}

    def headnorm_alloc(n):
        cw = min(512, n)
        HS["g"] = [P.sbuf("hg%d" % i, [128, 1], F32) for i in range(2)]
        HS["sq"] = [P.sbuf("hsq%d" % i, [128, cw], BF16) for i in range(2)]
        HS["rstd"] = [P.sbuf("hrstd%d" % i, [128, cw], F32) for i in range(2)]
        HS["xn"] = P.sbuf("hxn", [128, n], F32)
        HS["t1"] = [P.sbuf("ht1%d" % i, [128, cw], F32) for i in range(2)]
        HS["i"] = 0

    def headnorm_rope(dst, src, rows, n, ones_blk, gcol, gscale, rmat, rope, inv_d):
        cw = min(512, n)
        ncw = n // cw
        HS["i"] += 1
        g = HS["g"][HS["i"] % 2]
        sq, rstd, xn, t1 = HS["sq"], HS["rstd"], HS["xn"], HS["t1"]
        P.ts(g[0:rows, :], gcol, gscale, ALU.mult)
        for c in range(ncw):
            sl = slice(c * cw, (c + 1) * cw)
            sq_, rs_ = sq[c % 2], rstd[c % 2]
            pt = ps[c % 4]
            P.act(sq_[0:rows, :], src[0:rows, sl], AF.Square)
            P.mm(pt[0:rows, 0:cw], ones_blk, sq_[0:rows, :])
            P.act(rs_[0:rows, :], pt[0:rows, 0:cw], AF.Sqrt, bias=epsb[0:rows, 0:1], scale=inv_d)
            P.recip(rs_[0:rows, :], rs_[0:rows, :])
            if rope is None:
                P.stt(dst[0:rows, sl], src[0:rows, sl], g[0:rows, 0:1], rs_[0:rows, :], ALU.mult, ALU.mult)
            else:
                P.stt(xn[0:rows, sl], src[0:rows, sl], g[0:rows, 0:1], rs_[0:rows, :], ALU.mult, ALU.mult)
        if rope is not None:
            for c in range(ncw):
                sl = slice(c * cw, (c + 1) * cw)
                pt = ps[4 + c % 2]
                t1_ = t1[c % 2]
                P.mm(pt[0:rows, 0:cw], rmat, xn[0:rows, sl])
                P.tt(t1_[0:rows, :], pt[0:rows, 0:cw], rope[0:rows, 1, sl], ALU.mult)
                P.tt(xn[0:rows, sl], xn[0:rows, sl], rope[0:rows, 0, sl], ALU.mult)
                P.tt(dst[0:rows, sl], xn[0:rows, sl], t1_[0:rows, :], ALU.add)

    def seq_params(l, b, n, is_ctx, last):
        tag = "C" if is_ctx else "L"
        if l == 0:
            x_src = (cxT_in if is_ctx else xT_in)
        else:
            x_src = (xc_d if is_ctx else xs_d)
        if is_ctx:
            x_dst = xc_d
        else:
            x_dst = outT if last else xs_d
        return tag, n // 128, min(512, n), n // min(512, n), (nb if is_ctx else b), x_src, x_dst

    def seq_front(l, b, n, is_ctx, last):
        tag, nt, cw, ncw, bcol, x_src, x_dst = seq_params(l, b, n, is_ctx, last)
        S = SCR[tag]
        kg_all, vg_all, km_all, vm_all = KV["kg"], KV["vg"], KV["km"], KV["vm"]
        koff = SEQ if is_ctx else 0
        ktoff = koff // 128
        kv_only = is_ctx and last
        with P.scope():
            qg = P.sbuf("qg", [128, 2, n], BF16)
            qm = P.sbuf("qm", [96, 4, n], BF16)
            with P.scope():
                hT = P.sbuf("hT", [128, 8, n], BF16)
                headnorm_alloc(n)
                norm_mod(hT, lambda sl: x_src[b, :, sl].re("(k p) n -> p k n", p=128), n, "n1g", 0, 8, bcol)

                lw_i = [0]

                def load_w(c0, m):
                    w = P.sbuf("wsub", [128, 8, m], BF16)
                    lw_i[0] += 1
                    P.dma(w[:], win_bf[:, 8 * c0:8 * (c0 + m)].re("p (k m) -> p k m", k=8), eng=q2(lw_i[0]))
                    return w

                def proj_fm(w, w0, m, dst_fn):
                    for c in range(ncw):
                        sl = slice(c * cw, (c + 1) * cw)
                        pt = ps[2 + c % 2]
                        for k in range(8):
                            P.mm(pt[0:m, 0:cw], w[:, k, w0:w0 + m], hT[:, k, sl], start=(k == 0), stop=(k == 7))
                        dst_fn(sl, pt[0:m, 0:cw])

                with P.scope():
                    rope_g = None
                    if not is_ctx:
                        rope_g = P.sbuf("ropeg", [128, 2, SEQ], BF16)
                        P.dma(rope_g[:], ropeg_in[:, :, :], eng="sp")
                    w = load_w(0, 256)
                    src = P.sbuf("pj", [128, n], F32)
                    proj_fm(w, 0, 128, lambda sl, pv: evac(src[:, sl], pv))
                    headnorm_rope(kg_all[:, koff:koff + n], src, 128, n, BLKB, vec("gk"), 1.0, RG, rope_g, 1.0 / 64)
                    for t in range(nt):
                        pt = ps[4 + t % 2]
                        for k in range(8):
                            P.mm(pt[:, 0:128], hT[:, k, t * 128:(t + 1) * 128], w[:, k, 128:256],
                                 start=(k == 0), stop=(k == 7))
                        evac(vg_all[:, ktoff + t, :, 0:64], pt[:, 0:128].re("p (h d) -> p h d", h=2))
                    if not kv_only:
                        wq = load_w(416, 256)
                        for j in range(2):
                            proj_fm(wq, j * 128, 128, lambda sl, pv: evac(src[:, sl], pv))
                            headnorm_rope(qg[:, j, :], src, 128, n, BLKB, vec("gq"), 64 ** -0.5, RG, rope_g, 1.0 / 64)
                rope_m = None
                with P.scope():
                    if not is_ctx:
                        rope_m = P.sbuf("ropem", [128, 2, SEQ], BF16)
                        P.dma(rope_m[:], ropem_in[:, :, :], eng="sp")
                    w = load_w(256, 160)
                    src = P.sbuf("pj", [128, n], F32)
                    proj_fm(w, 0, 128, lambda sl, pv: evac(src[:, sl], pv))
                    ckvn = P.sbuf("ckvn", [128, n], BF16)
                    headnorm_rope(ckvn[:, :], src, 128, n, ONB, vec("ckvg"), 1.0, None, None, 1.0 / 128)
                    kr = P.sbuf("kr", [32, n], BF16)
                    proj_fm(w, 128, 32, lambda sl, pv: evac(kr[:, sl], pv))
                    wk, wv = LW["wk"], LW["wv"]
                    for h in range(4):
                        for c in range(ncw):
                            sl = slice(c * cw, (c + 1) * cw)
                            pt = ps[2 + c % 2]
                            P.mm(pt[0:96, 0:cw], wk[:, h, :], ckvn[:, sl], start=True, stop=False)
                            P.mm(pt[0:96, 0:cw], SELKRB[0:32, 0:96], kr[:, sl], start=False, stop=True)
                            evac(src[0:96, sl], pt[0:96, 0:cw])
                        headnorm_rope(km_all[:, h, koff:koff + n], src, 96, n, ONB[0:96, 0:96], vec("mkg", 0, 96),
                                      1.0, RM[0:96, 0:96], rope_m, 1.0 / 96)
                    for t in range(nt):
                        pt = ps[4 + t % 2]
                        P.mm(pt[:, 0:256], ckvn[:, t * 128:(t + 1) * 128], wv[:, :])
                        evac(vm_all[:, ktoff + t, :, 0:64], pt[:, 0:256].re("p (h d) -> p h d", h=4))
                if not kv_only:
                    with P.scope():
                        if not is_ctx:
                            rope_m = P.sbuf("ropem", [128, 2, SEQ], BF16)
                            P.dma(rope_m[:], ropem_in[:, :, :], eng="sp")
                        w = load_w(672, 256)
                        cqn = P.sbuf("cqn", [128, 2, n], BF16)
                        wuq = LW["wuq"]
                        with P.scope():
                            cq = P.sbuf("cq", [128, 2, cw], F32)
                            sqc = P.sbuf("sqc", [128, 2, cw], BF16)
                            rstd = P.sbuf("cqrstd", [128, cw], F32)
                            for c in range(ncw):
                                sl = slice(c * cw, (c + 1) * cw)
                                for j in range(2):
                                    pt = ps[2 + j]
                                    for k in range(8):
                                        P.mm(pt[:, 0:cw], w[:, k, j * 128:(j + 1) * 128], hT[:, k, sl],
                                             start=(k == 0), stop=(k == 7))
                                    evac(cq[:, j, :], pt[:, 0:cw])
                                P.act(sqc[:, :, :], cq[:, :, :], AF.Square)
                                for j in range(2):
                                    P.mm(ps[0][:, 0:cw], ONB, sqc[:, j, :], start=(j == 0), stop=(j == 1))
                                P.act(rstd[:], ps[0][:, 0:cw], AF.Sqrt, bias=epsb[:, 0:1], scale=1.0 / 256)
                                P.recip(rstd[:], rstd[:])
                                for j in range(2):
                                    P.stt(cqn[:, j, sl], cq[:, j, :], vec("cqg", j), rstd[:], ALU.mult, ALU.mult)
                        src = P.sbuf("pj", [128, n], F32)
                        for h in range(4):
                            for c in range(ncw):
                                sl = slice(c * cw, (c + 1) * cw)
                                pt = ps[2 + c % 2]
                                for j in range(2):
                                    P.mm(pt[0:96, 0:cw], wuq[:, j, h * 96:(h + 1) * 96], cqn[:, j, sl],
                                         start=(j == 0), stop=(j == 1))
                                evac(src[0:96, sl], pt[0:96, 0:cw])
                            headnorm_rope(qm[:, h, :], src, 96, n, ONB[0:96, 0:96], vec("mqg", 0, 96),
                                          96 ** -0.5, RM[0:96, 0:96], rope_m, 1.0 / 96)
                    with P.scope():
                        stg = [P.sbuf("stg%d" % i, [128, n], F32) for i in range(2)]
                        wph = [load_w(928 + j * 128, 128) for j in range(8)]
                        for j in range(8):
                            st = stg[j % 2]
                            w = wph[j]
                            proj_fm(w, 0, 128, lambda sl, pv: evac(st[:, sl], pv))
                            if j < 2:
                                P.dma(S["pu"][j * 128:(j + 1) * 128, :], st[:, :], eng=q2(j))
                            else:
                                P.dma(S["hu"][(j - 2) * 128:(j - 1) * 128, :], st[:, :], eng=q2(j))
            P.checkpoint("A" + tag)
            if kv_only:
                return
            nk = n if is_ctx else SEQ + NCTX
            k0 = SEQ if is_ctx else 0
            nkt = nk // 128
            kt0 = k0 // 128
            with P.scope():
                pT = [P.sbuf("pT%d" % i, [128, cw], BF16) for i in range(3)]
                rs = P.sbuf("rs", [128, cw], F32)
                bcs = P.sbuf("bcs", [64, cw], F32)
                ob = [P.sbuf("ob%d" % i, [64, n], BF16) for i in range(2)]
                pend = [None]

                def flush():
                    if pend[0] is not None:
                        pend[0]()
                        pend[0] = None

                for hh_ in range(8):
                    o_t = ob[hh_ % 2]
                    if hh_ < 4:
                        row0 = 256 + hh_ * 64
                    else:
                        row0 = 768 + (hh_ - 4) * 64
                    for c in range(ncw):
                        sl = slice(c * cw, (c + 1) * cw)
                        acc = ps[4 + (hh_ * ncw + c) % 2]

                        def mm1(kt):
                            st_ = ps[kt % 3]
                            ks = slice(k0 + kt * 128, k0 + (kt + 1) * 128)
                            if hh_ < 4:
                                chunk, half = hh_ % 2, hh_ // 2
                                P.mm(st_[:, 0:cw], kg_all[half * 64:(half + 1) * 64, ks],
                                     qg[half * 64:(half + 1) * 64, chunk, sl])
                            else:
                                P.mm(st_[:, 0:cw], km_all[:, hh_ - 4, ks], qm[:, hh_ - 4, sl])

                        def fin(acc=acc, o_t=o_t, sl=sl, last_c=(c == ncw - 1), row0=row0, hh_=hh_):
                            P.recip(rs[64:65, :], acc[64:65, 0:cw])
                            P.mm(ps[3][0:64, 0:cw], ONF[64:65, 0:64], rs[64:65, :])
                            P.copy(bcs[:], ps[3][0:64, 0:cw], eng="act")
                            P.tt(o_t[:, sl], acc[0:64, 0:cw], bcs[:], ALU.mult)
                            if last_c:
                                P.dma(S["mix"][row0:row0 + 64, :], o_t[:, :], eng=q2(hh_))

                        mm1(0)
                        for kt in range(nkt):
                            if kt + 1 < nkt:
                                mm1(kt + 1)
                            if hh_ < 4:
                                vv = vg_all[:, kt0 + kt, hh_ // 2, :]
                            else:
                                vv = vm_all[:, kt0 + kt, hh_ - 4, :]
                            p_t = pT[kt % 3]
                            P.act(p_t[:], ps[kt % 3][:, 0:cw], AF.Exp)
                            P.mm(acc[0:65, 0:cw], vv, p_t[:], start=(kt == 0), stop=(kt == nkt - 1))
                            if kt == 1:
                                flush()
                        pend[0] = fin
                flush()

    def seq_back(l, b, n, is_ctx, last):
        tag, nt, cw, ncw, bcol, x_src, x_dst = seq_params(l, b, n, is_ctx, last)
        S = SCR[tag]
        with P.scope():
            rights = (0, 1, 3, 7)
            wbd = LW["wbd"]
            U = P.sbuf("pU", [128, n + 32], F32)
            A = [P.sbuf("pA%d" % i, [128, n + 32], F32) for i in range(2)]
            ic = P.sbuf("pic", [128, n], F32)
            dd = P.sbuf("pdd", [128, n], F32)
            db = P.sbuf("pdb", [128, n], BF16)
            ob = P.sbuf("pob", [128, n], BF16)
            P.memset(U[:], 0.0)
            P.memset(A[0][:], 0.0)
            P.memset(A[1][:], 0.0)
            ext = n + 8
            for ch in range(2):
                P.dma(U[:, 16:16 + n], S["pu"][ch * 128:(ch + 1) * 128, :])
                P.dma(ic[:], CI["invcnt" + tag][ch, :, :], eng="sp")
                cur = U
                for wi, w in enumerate((2, 4, 8, 16)):
                    nxt = A[wi % 2]
                    sh = w // 2
                    P.tt(nxt[:, 16:16 + ext], cur[:, 16:16 + ext], cur[:, 16 - sh:16 - sh + ext], ALU.add)
                    cur = nxt
                    g = wi - 2 * ch
                    if g in (0, 1):
                        r_ = rights[wi]
                        psl = slice(g * 64, g * 64 + 64)
                        P.tt(dd[psl, :], cur[psl, 16 + r_:16 + r_ + n], ic[psl, :], ALU.mult)
                        P.tt(db[psl, :], dd[psl, :], U[psl, 16:16 + n], ALU.subtract)
                for c in range(ncw):
                    sl = slice(c * cw, (c + 1) * cw)
                    pt = ps[c % 2]
                    P.mm(pt[:, 0:cw], wbd[:, ch, :], db[:, sl])
                    P.ts(ob[:, sl], pt[:, 0:cw], vec("pscale", ch), ALU.mult)
                P.dma(S["mix"][ch * 128:(ch + 1) * 128, :], ob[:, :])

        P.checkpoint("C" + tag)
        with P.scope():
            NF = nt
            cwi = 128
            ncwi = n // cwi
            vx = P.sbuf("hvx", [128, 6, n], F32)
            with P.scope():
                up = P.sbuf("hup", [128, n + 2], F32)
                P.memset(up[:], 0.0)
                o_, _k = VEC_OFF["hcw"]
                for j in range(6):
                    P.dma(up[:, 1:n + 1], S["hu"][j * 128:(j + 1) * 128, :], eng=q2(j))
                    w0 = vecs[:, o_ + j * 3:o_ + j * 3 + 1]
                    w1_ = vecs[:, o_ + j * 3 + 1:o_ + j * 3 + 2]
                    w2_ = vecs[:, o_ + j * 3 + 2:o_ + j * 3 + 3]
                    P.ts(vx[:, j, :], up[:, 0:n], w0, ALU.mult, vec("hcb", j), ALU.add)
                    P.stt(vx[:, j, :], up[:, 1:n + 1], w1_, vx[:, j, :], ALU.mult, ALU.add)
                    P.stt(vx[:, j, :], up[:, 2:n + 2], w2_, vx[:, j, :], ALU.mult, ALU.add)
            ny = P.sbuf("hny", [128, 1 + n], BF16)
            P.dma(ny[:], CI["ny" + tag][:, :])
            zin = P.sbuf("hzin", [128, 2, n], F32)
            zb = P.sbuf("hzb", [128, 2, n], BF16)
            ztok = P.sbuf("hztok", [128, nt, 256], BF16)
            Y = P.sbuf("hY", [128, 2 * NF, 256], BF16)
            yn = P.sbuf("hyn", [1, 256], BF16)
            NFB = 3 if n > 256 else 2
            fbuf = [P.sbuf("hF%d" % i, [128, nt, 256], BF16) for i in range(NFB)]
            hfb = [P.sbuf("hhf%d" % i, [128, 2, 256], F32) for i in range(NFB)]
            gbuf = [P.sbuf("hG%d" % i, [128, 2 * NF, cwi], BF16) for i in range(2)]
            ta = P.sbuf("hta", [128, 2, 256], F32)
            tb = P.sbuf("htb", [128, 2, 256], F32)
            oh = P.sbuf("hoh", [128, 2, n], BF16)
            taf = ta[:, :, :].re("p h c -> p (h c)")
            for order in range(2):
                src = vx[:, 0:2, :] if order == 0 else zin[:, :, :]
                P.copy(zb[:, 0, :], src[:, 0, :], eng="act")
                P.copy(zb[:, 1, :], src[:, 1, :], eng="dve")
                for s in range(nt):
                    for ch in range(2):
                        P.transpose(psb[:, ch * 128:(ch + 1) * 128], zb[:, ch, s * 128:(s + 1) * 128], IDB)
                    evac(ztok[:, s, :], psb[:, 0:256])
                for s in range(nt):
                    P.mm(ps[2][0:1, 0:256], ny[:, 0:1], ztok[:, s, :], start=(s == 0), stop=(s == nt - 1))
                P.tt(yn[:, :], ps[2][0:1, 0:256], hn[tag][:, order * 256:(order + 1) * 256], ALU.mult)
                for fk in range(NF):
                    fb = fbuf[fk % NFB]
                    P.dma(fb[:], CI["F" + tag][fk, :, :, :], eng=q2(fk))
                    hb = hfb[fk % NFB]
                    P.dma(hb[:], hf_d[tag][order, fk, :, :, :], eng=q2(fk + 1))
                    pz = ps[fk % 2]
                    for half in range(2):
                        for s in range(nt):
                            P.mm(pz[:, half * 256:(half + 1) * 256], fb[:, s, half * 128:(half + 1) * 128],
                                 ztok[:, s, :], start=(s == 0), stop=(s == nt - 1))
                    zv = pz[:, :].re("p (h c) -> p h c", h=2)
                    P.tt(ta[:], zv, hb[:, 0:1, :].bc([128, 2, 256]), ALU.mult)
                    P.tt(tb[:], zv, hb[:, 1:2, :].bc([128, 2, 256]), ALU.mult)
                    P.tt(Y[:, fk, :], ta[:, 0, :], tb[:, 1, :], ALU.subtract)
                    P.tt(Y[:, NF + fk, :], tb[:, 0, :], ta[:, 1, :], ALU.add)
                for c in range(ncwi):
                    sl = slice(c * cwi, (c + 1) * cwi)
                    gb = gbuf[c % 2]
                    P.dma(gb[:], CI["G" + tag][c, :, :, :], eng=q2(c))
                    for ch in range(2):
                        pt = ps[2 + (2 * c + ch) % 4]
                        for r in range(2 * NF):
                            P.mm(pt[:, 0:cwi], Y[:, r, ch * 128:(ch + 1) * 128], gb[:, r, :],
                                 start=(r == 0), stop=False)
                        P.mm(pt[:, 0:cwi], yn[0:1, ch * 128:(ch + 1) * 128], ny[0:1, 1 + c * cwi:1 + (c + 1) * cwi],
                             start=False, stop=True)
                        bias = vec("hbias", order * 2 + ch)
                        if order == 0:
                            P.stt(zin[:, ch, sl], vx[:, ch, sl], bias, pt[:, 0:cwi], ALU.mult, ALU.add)
                            P.tt(zin[:, ch, sl], zin[:, ch, sl], vx[:, 2 + ch, sl], ALU.mult)
                        else:
                            P.stt(taf[:, 0:cwi], zin[:, ch, sl], bias, pt[:, 0:cwi], ALU.mult, ALU.add)
                            P.tt(oh[:, ch, sl], taf[:, 0:cwi], vx[:, 4 + ch, sl], ALU.mult)
            for ch in range(2):
                P.dma(S["mix"][512 + ch * 128:512 + (ch + 1) * 128, :], oh[:, ch, :], eng=q2(ch))

        P.checkpoint("D" + tag)
        with P.scope():
            wo = P.sbuf("wo", [128, 8, D], BF16)
            wo64 = P.sbuf("wo64", [64, 8, D], BF16)
            P.dma(wo[:], wout_bf[:, :, :], eng="sp")
            P.dma(wo64[:], wout64_bf[:, :, :], eng="sp")
            m128 = P.sbuf("m128", [128, 4, n], BF16)
            m64 = P.sbuf("m64", [64, 8, n], BF16)
            for j, r0 in enumerate((0, 128, 512, 640)):
                P.dma(m128[:, j, :], S["mix"][r0:r0 + 128, :], eng=q2(j))
            for j in range(8):
                r0 = (256 + j * 64) if j < 4 else (768 + (j - 4) * 64)
                P.dma(m64[:, j, :], S["mix"][r0:r0 + 64, :], eng=q2(j))
            xin = [P.sbuf("exin%d" % i, [128, n], F32) for i in range(2)]
            P.dma(xin[0][:, :], x_src[b, 0:128, :], eng=q2(0))
            for i in range(8):
                xi = xin[i % 2]
                if i + 1 < 8:
                    P.dma(xin[(i + 1) % 2][:, :], x_src[b, (i + 1) * 128:(i + 2) * 128, :], eng=q2(i + 1))
                for c in range(ncw):
                    sl = slice(c * cw, (c + 1) * cw)
                    pt = ps[(i * ncw + c) % 2]
                    osl = slice(i * 128, (i + 1) * 128)
                    mlist = []
                    for j, kc in enumerate((0, 1, 4, 5)):
                        mlist.append((wo[:, kc, osl], m128[:, j, sl]))
                    for j in range(8):
                        mlist.append((wo64[:, j, osl], m64[:, j, sl]))
                    for mi, (lh, rh) in enumerate(mlist):
                        P.mm(pt[:, 0:cw], lh, rh, start=(mi == 0), stop=(mi == len(mlist) - 1))
                    P.stt(xi[:, sl], pt[:, 0:cw], modT[:, 16 + i, bcol:bcol + 1], xi[:, sl], ALU.mult, ALU.add)
                P.dma(S["xmid"][i * 128:(i + 1) * 128, :], xi[:, :], eng=q2(i))

        P.checkpoint("E" + tag)
        with P.scope():
            h2 = P.sbuf("h2", [128, 8, n], BF16)
            norm_mod(h2, lambda sl: S["xmid"][:, sl].re("(k p) n -> p k n", p=128), n, "n2g", 24, 32, bcol)
            gm = P.sbuf("gm", [16, n], F32)
            P.checkpoint("F1" + tag)
            with P.scope():
                wr = LW["wr"]
                lg = ps[0]
                for t in range(nt):
                    for k in range(8):
                        P.mm(lg[:, t * 16:(t + 1) * 16], h2[:, k, t * 128:(t + 1) * 128], wr[:, k, :],
                             start=(k == 0), stop=(k == 7))
                aff = P.sbuf("aff", [128, nt, 16], F32)
                mx = P.sbuf("affmx", [128, nt], F32)
                lv = lg[:, 0:nt * 16].re("p (t e) -> p t e", e=16)
                P.reduce(mx[:], lv, ALU.max)
                P.tt(aff[:], lv, mx[:].re("p (t o) -> p t o", o=1).bc([128, nt, 16]), ALU.subtract)
                P.act(aff[:], aff[:], AF.Exp)
                P.reduce(mx[:], aff[:], ALU.add)
                P.recip(mx[:], mx[:])
                P.tt(aff[:], aff[:], mx[:].re("p (t o) -> p t o", o=1).bc([128, nt, 16]), ALU.mult)
                affT = P.sbuf("affT", [16, n], F32)
                for t in range(nt):
                    pt = ps[1 + (t // 4) % 2]
                    P.transpose(pt[0:16, (t % 4) * 128:(t % 4 + 1) * 128], aff[:, t, :], IDF)
                    if t % 4 == 3 or t == nt - 1:
                        t0 = (t // 4) * 4
                        wdt = (t - t0 + 1) * 128
                        evac(affT[:, t0 * 128:t0 * 128 + wdt], pt[0:16, 0:wdt])
                work = P.sbuf("tkwork", [16, n], F32)
                mx8 = P.sbuf("tkmx8", [16, 8], F32)
                cap = n // 8
                cur = affT
                for it in range(cap // 8):
                    P.op("dve", (lambda c_: (lambda e: e.max(out=mx8.h[:], in_=c_.h[:])))(cur),
                         reads=[cur[:]], writes=[mx8[:]])
                    P.op("dve", (lambda c_: (lambda e: e.match_replace(out=work.h[:], in_to_replace=mx8.h[:],
                                                                        in_values=c_.h[:], imm_value=-1.0)))(cur),
                         reads=[cur[:], mx8[:]], writes=[work[:]])
                    cur = work
                P.ts(work[:], work[:], 0.0, ALU.is_lt)
                P.tt(gm[:], work[:], affT[:], ALU.mult)
            if debug:
                P.dma(dbg_gm[tag][:, :], gm[:, :])
            P.checkpoint("F2" + tag)
            gmb = P.sbuf("gmb", [16, n], BF16)
            P.copy(gmb[:], gm[:])
            P.checkpoint("F2b" + tag)
            with P.scope():
                yacc = P.sbuf("yacc", [128, 8, n], F32)
                with P.scope():
                    sel = P.sbuf("sel", [16, 16, 128], BF16)
                    P.dma(sel[:], sel_in[:, :, :])
                    wgu = [P.sbuf("wgu%d" % i, [128, 8, 1024], BF16) for i in range(2)]
                    wdn = [P.sbuf("wdn%d" % i, [128, 4, D], BF16) for i in range(2)]
                    gbc = [P.sbuf("gbc%d" % i, [128, cw], F32) for i in range(2)]
                    sa = P.sbuf("sa", [128, cw], F32)
                    hid = [P.sbuf("hid%d" % i, [128, 4, cw], BF16) for i in range(2)]
                    for e in range(nexp):
                        wg_ = wgu[e % 2]
                        wd_ = wdn[e % 2]
                        P.dma(wg_[:], wgu_bf[e % 2][e // 2, :, :, :], eng=q2(e))
                        P.dma(wd_[:], wdn_bf[e % 2][e // 2, :, :, :], eng=q2(e + 1))
                        for c in range(ncw):
                            sl = slice(c * cw, (c + 1) * cw)
                            gb = gbc[c % 2]
                            P.mm(ps[6][:, 0:cw], sel[:, e, :], gmb[:, sl])
                            P.copy(gb[:], ps[6][:, 0:cw], eng="act")
                            hd = hid[c % 2]
                            for j in range(4):
                                pa = ps[(j % 2) * 2]
                                pu = ps[(j % 2) * 2 + 1]
                                for k in range(8):
                                    P.mm(pa[:, 0:cw], wg_[:, k, j * 128:(j + 1) * 128], h2[:, k, sl],
                                         start=(k == 0), stop=(k == 7))
                                for k in range(8):
                                    P.mm(pu[:, 0:cw], wg_[:, k, 512 + j * 128:512 + (j + 1) * 128], h2[:, k, sl],
                                         start=(k == 0), stop=(k == 7))
                                P.act(sa[:], pa[:, 0:cw], AF.Silu)
                                P.tt(sa[:], sa[:], gb[:], ALU.mult)
                                P.tt(hd[:, j, :], pu[:, 0:cw], sa[:], ALU.mult)
                            for i in range(8):
                                py = ps[4 + i % 2]
                                for j in range(4):
                                    P.mm(py[:, 0:cw], wd_[:, j, i * 128:(i + 1) * 128], hd[:, j, :],
                                         start=(j == 0), stop=(j == 3))
                                if e == 0:
                                    evac(yacc[:, i, sl], py[:, 0:cw])
                                else:
                                    P.tt(yacc[:, i, sl], yacc[:, i, sl], py[:, 0:cw], ALU.add)
                xin = [P.sbuf("fxin%d" % i, [128, n], F32) for i in range(2)]
                P.dma(xin[0][:, :], S["xmid"][0:128, :], eng=q2(0))
                for i in range(8):
                    xi = xin[i % 2]
                    if i + 1 < 8:
                        P.dma(xin[(i + 1) % 2][:, :], S["xmid"][(i + 1) * 128:(i + 2) * 128, :], eng=q2(i + 1))
                    P.stt(xi[:, :], yacc[:, i, :], modT[:, 40 + i, bcol:bcol + 1], xi[:, :], ALU.mult, ALU.add)
                    r = P.dma(x_dst[b, i * 128:(i + 1) * 128, :], xi[:, :], eng=q2(i))
                    if last and not is_ctx:
                        final.append(r)

    final = []
    try:
        for l in range(depth):
            last = (l == DEPTH - 1)
            precast_dense(l)
            precast_experts(l)
            layer_prologue(l)
            P.checkpoint("prologue")
            filter_gen(l, SEQ, "L")
            P.checkpoint("filtL")
            if not last:
                filter_gen(l, NCTX, "C")
                P.checkpoint("filtC")
            for b in range(nb):
                with P.scope():
                    KV["kg"] = P.sbuf("kg_all", [128, SEQ + NCTX], BF16)
                    KV["vg"] = P.sbuf("vg_all", [128, 18, 2, 65], BF16)
                    KV["km"] = P.sbuf("km_all", [96, 4, SEQ + NCTX], BF16)
                    KV["vm"] = P.sbuf("vm_all", [128, 18, 4, 65], BF16)
                    P.memset(KV["vg"][:, :, :, 64:65], 1.0)
                    P.memset(KV["vm"][:, :, :, 64:65], 1.0)
                    seq_front(l, b, NCTX, True, last)
                    P.checkpoint("frontC")
                    seq_front(l, b, SEQ, False, last)
                    P.checkpoint("frontL")
                seq_back(l, b, SEQ, False, last)
                P.checkpoint("backL")
                if not last:
                    seq_back(l, b, NCTX, True, last)
                    P.checkpoint("backC")
    except StopBuild:
        lastrec = {}
        for r in P.recs:
            if r.is_dma:
                lastrec[("d", r.sem_idx)] = r
            else:
                lastrec[r.eng] = r
        final = list(lastrec.values())
    P.finish(final_waits=final)
    return nc, P


def prep_shared(inp):
    C = get_consts()
    sh = dict(C)
    sh["vecs"] = np.stack([pack_vecs(inp, l) for l in range(DEPTH)], 0)
    sh["w_mod"] = np.ascontiguousarray(inp["w_mod"], np.float32)
    w_in = np.asarray(inp["w_in"], np.float32)
    perm = np.arange(1952)
    q0 = 416
    perm[q0:q0 + 256] = np.concatenate([q0 + np.arange(0, 64), q0 + np.arange(128, 192),
                                        q0 + np.arange(64, 128), q0 + np.arange(192, 256)])
    w_in = w_in[:, :, perm]
    wflat = np.zeros((DEPTH, 128, 8 * 1952), np.float32)
    for l in range(DEPTH):
        wp = w_in[l].reshape(8, 128, 1952).transpose(1, 0, 2)
        for c0, m in W_IN_GROUPS:
            wflat[l, :, 8 * c0:8 * (c0 + m)] = wp[:, :, c0:c0 + m].reshape(128, 8 * m)
    sh["w_in"] = wflat
    w_out = np.asarray(inp["w_out"], np.float32)
    sh["w_out"] = np.ascontiguousarray(w_out.reshape(DEPTH, 8, 128, D).transpose(0, 2, 1, 3))
    wo64 = np.concatenate([w_out[:, 256:512, :], w_out[:, 768:1024, :]], 1)
    sh["w_out64"] = np.ascontiguousarray(wo64.reshape(DEPTH, 8, 64, D).transpose(0, 2, 1, 3))
    pw = np.asarray(inp["pool_w"], np.float32)
    bd = np.zeros((DEPTH, 2, 128, 128), np.float32)
    for l in range(DEPTH):
        for g in range(4):
            o = (g % 2) * 64
            bd[l, g // 2, o:o + 64, o:o + 64] = pw[l, g]
    sh["poolbd"] = bd
    for k in ("hy_f_w1", "hy_f_w2", "hy_f_w3", "mla_w_uq", "router_w"):
        sh[k] = np.ascontiguousarray(inp[k], np.float32)
    wg = np.asarray(inp["exp_w_gate"], np.float32).reshape(DEPTH, NEXP, 8, 128, 512)
    wu = np.asarray(inp["exp_w_up"], np.float32).reshape(DEPTH, NEXP, 8, 128, 512)
    sh["exp_w_gu"] = np.ascontiguousarray(np.concatenate([wg, wu], -1).transpose(0, 1, 3, 2, 4))
    wd = np.asarray(inp["exp_w_down"], np.float32).reshape(DEPTH, NEXP, 4, 128, D)
    sh["exp_w_dn"] = np.ascontiguousarray(wd.transpose(0, 1, 3, 2, 4))
    wukv = np.asarray(inp["mla_w_ukv"], np.float32).reshape(DEPTH, 128, 4, 128)
    wk = np.zeros((DEPTH, 128, 4, 96), np.float32)
    wk[:, :, :, 0:64] = wukv[:, :, :, 0:64]
    sh["wukv_k"] = wk
    sh["wukv_v"] = np.ascontiguousarray(wukv[:, :, :, 64:128].reshape(DEPTH, 128, 256))
    return sh


def make_in_maps(inp, nb, n_cores=8):
    sh = prep_shared(inp)
    x = np.asarray(inp["x"], np.float32)
    ctx = np.asarray(inp["ctx"], np.float32)
    c = np.asarray(inp["c"], np.float32)
    c_ctx = np.asarray(inp["c_ctx"], np.float32)
    in_maps = []
    for core in range(n_cores):
        bs = slice(core * nb, (core + 1) * nb)
        m = dict(sh)
        m["xT"] = np.ascontiguousarray(x[bs].transpose(0, 2, 1))
        m["ctxT"] = np.ascontiguousarray(ctx[bs].transpose(0, 2, 1))
        cc = np.concatenate([c[bs], c_ctx[None, :]], 0)
        m["cT"] = np.ascontiguousarray(cc.reshape(nb + 1, 8, 128).transpose(2, 1, 0))
        in_maps.append(m)
    return in_maps


def kernel(**inp):
    inp = {k: np.asarray(v) for k, v in inp.items()}
    n_cores = 8
    in_maps = make_in_maps(inp, NB, n_cores)
    nc, _ = build_program()
    res = run_bass_kernel_spmd(nc, in_maps, core_ids=list(range(n_cores)))
    out = np.empty((32, SEQ, D), np.float32)
    for core in range(n_cores):
        oT = np.asarray(res.results[core]["outT"], np.float32)
        out[core * NB:(core + 1) * NB] = oT.transpose(0, 2, 1)
    return out
```

```python
import math
import numpy as np
import ml_dtypes
import concourse.bass as bass
import concourse.mybir as mybir
from concourse.bass_utils import run_bass_kernel_spmd

F32 = mybir.dt.float32
BF16 = mybir.dt.bfloat16
ALU = mybir.AluOpType
AF = mybir.ActivationFunctionType
AX = mybir.AxisListType
NPBF = ml_dtypes.bfloat16

ENGS = ("pe", "act", "dve", "pool", "sp")
N_DMA_SEMS = 48

D = 1024
SEQ = 2048
NCTX = 256
DEPTH = 2
NB = 4
NEXP = 16
EPS = 1e-6
MAGIC = 12582912.0
TWO_PI = 2.0 * math.pi


class Tile:
    def __init__(self, name, handle, space, const=False):
        self.name = name
        self.h = handle
        self.space = space
        self.const = const
        self.last_write = None
        self.reads = []
        self.sem_idx = None

    def __getitem__(self, idx):
        return V(self, self.h[idx])


class V:
    def __init__(self, tile, ap):
        self.tile = tile
        self.ap = ap

    def __getitem__(self, idx):
        return V(self.tile, self.ap[idx])

    def re(self, pat, **kw):
        return V(self.tile, self.ap.rearrange(pat, **kw))

    def bc(self, shape):
        return V(self.tile, self.ap.to_broadcast(list(shape)))


class Rec:
    __slots__ = ("eng", "fn", "deps", "signaled", "count", "is_dma", "sem_idx")

    def __init__(self, eng, fn, deps, is_dma=False):
        self.eng = eng
        self.fn = fn
        self.deps = deps
        self.signaled = False
        self.count = None
        self.is_dma = is_dma
        self.sem_idx = None


class StopBuild(Exception):
    pass


class Scope:
    def __init__(self, P):
        self.P = P

    def __enter__(self):
        self.P.scope_stack.append([])
        return self

    def __exit__(self, *a):
        self.P.barrier()
        for cm in reversed(self.P.scope_stack.pop()):
            cm.__exit__(None, None, None)
        return False


class Prog:
    def __init__(self, nc, same_engine_sync=True):
        self.nc = nc
        self.recs = []
        self.same_engine_sync = same_engine_sync
        self.scope_stack = [[]]
        self.bar_deps = {e: [] for e in ENGS}
        self.bar_start = 0
        self.next_sem = 0
        self.uid = 0
        self.stop_at = None
        self.ckpts = []

    def checkpoint(self, name):
        self.ckpts.append((name, len(self.recs)))
        if self.stop_at is not None and name == self.stop_at:
            raise StopBuild()

    def scope(self):
        return Scope(self)

    def _enter(self, cm):
        v = cm.__enter__()
        self.scope_stack[-1].append(cm)
        return v

    def _name(self, name):
        self.uid += 1
        return "%s_%d" % (name, self.uid)

    def sbuf(self, name, shape, dt):
        h = self._enter(self.nc.sbuf_tensor(self._name(name), list(shape), dt))
        return Tile(name, h, "sbuf")

    def psum(self, name, shape, dt=F32):
        h = self._enter(self.nc.psum_tensor(self._name(name), list(shape), dt))
        return Tile(name, h, "psum")

    def dram(self, name, shape, dt, kind="Internal"):
        h = self.nc.dram_tensor(name, list(shape), dt, kind=kind)
        return Tile(name, h, "dram", const=(kind == "ExternalInput"))

    def barrier(self):
        last = {}
        dmas = []
        for r in self.recs[self.bar_start:]:
            if r.is_dma:
                dmas.append(r)
            else:
                last[r.eng] = r
        self.bar_start = len(self.recs)
        new = list(last.values()) + dmas
        for e in ENGS:
            self.bar_deps[e] = self.bar_deps[e] + new

    def _deps(self, eng, reads, writes):
        deps = list(self.bar_deps[eng])
        self.bar_deps[eng] = []
        for t in reads:
            if t.const:
                continue
            if t.last_write is not None:
                deps.append(t.last_write)
        for t in writes:
            if t.last_write is not None:
                deps.append(t.last_write)
            deps.extend(t.reads)
        return deps

    def _commit(self, rec, reads, writes):
        for t in writes:
            t.last_write = rec
            t.reads = []
        for t in reads:
            if t.const or t in writes:
                continue
            t.reads.append(rec)
            if len(t.reads) > 48:
                keep = {}
                rest = {}
                for r in t.reads:
                    if r.is_dma:
                        rest[r.sem_idx] = r
                    else:
                        keep[r.eng] = r
                t.reads = list(rest.values()) + list(keep.values())
        self.recs.append(rec)

    def op(self, eng, fn, reads=(), writes=()):
        reads = [v.tile for v in reads]
        writes = [v.tile for v in writes]
        rec = Rec(eng, fn, self._deps(eng, reads, writes))
        self._commit(rec, reads, writes)
        return rec

    def dma(self, out, in_, eng="sp", **kw):
        reads = [in_.tile]
        writes = [out.tile]
        o_ap, i_ap = out.ap, in_.ap
        rec = Rec(eng, lambda e: e.dma_start(out=o_ap, in_=i_ap, **kw), self._deps(eng, reads, writes), is_dma=True)
        t = out.tile
        if t.sem_idx is None:
            t.sem_idx = self.next_sem % N_DMA_SEMS
            self.next_sem += 1
        rec.sem_idx = t.sem_idx
        self._commit(rec, reads, writes)
        return rec

    def mm(self, out, lhsT, rhs, start=True, stop=True):
        o, l, r = out.ap, lhsT.ap, rhs.ap
        return self.op("pe", lambda e: e.matmul(o, l, r, start=start, stop=stop),
                       reads=[lhsT, rhs], writes=[out])

    def transpose(self, out, in_, ident):
        o, i, d = out.ap, in_.ap, ident.ap
        return self.op("pe", lambda e: e.transpose(o, i, d), reads=[in_, ident], writes=[out])

    def act(self, out, in_, func, bias=None, scale=1.0):
        o, i = out.ap, in_.ap
        reads = [in_]
        kw = {}
        if bias is not None:
            if isinstance(bias, V):
                reads.append(bias)
                kw["bias"] = bias.ap
            else:
                kw["bias"] = bias
        if isinstance(scale, V):
            reads.append(scale)
            kw["scale"] = scale.ap
        else:
            kw["scale"] = scale
        return self.op("act", lambda e: e.activation(o, i, func, **kw), reads=reads, writes=[out])

    def tt(self, out, in0, in1, op, eng="dve"):
        o, a, b = out.ap, in0.ap, in1.ap
        return self.op(eng, lambda e: e.tensor_tensor(o, a, b, op), reads=[in0, in1], writes=[out])

    def ts(self, out, in0, s1, op0, s2=None, op1=None, eng="dve"):
        o, a = out.ap, in0.ap
        reads = [in0]
        if isinstance(s1, V):
            reads.append(s1)
            s1 = s1.ap
        if isinstance(s2, V):
            reads.append(s2)
            s2 = s2.ap
        kw = {}
        if op1 is not None:
            kw["op1"] = op1
        return self.op(eng, lambda e: e.tensor_scalar(o, a, s1, s2, op0, **kw), reads=reads, writes=[out])

    def stt(self, out, in0, scalar, in1, op0, op1, eng="dve"):
        o, a, b = out.ap, in0.ap, in1.ap
        reads = [in0, in1]
        if isinstance(scalar, V):
            reads.append(scalar)
            scalar = scalar.ap
        return self.op(eng, lambda e: e.scalar_tensor_tensor(o, a, scalar, b, op0, op1), reads=reads, writes=[out])

    def copy(self, out, in_, eng="dve"):
        o, i = out.ap, in_.ap
        if eng == "act":
            return self.op("act", lambda e: e.copy(o, i), reads=[in_], writes=[out])
        return self.op(eng, lambda e: e.tensor_copy(o, i), reads=[in_], writes=[out])

    def memset(self, out, val, eng="dve"):
        o = out.ap
        return self.op(eng, lambda e: e.memset(o, val), reads=[], writes=[out])

    def reduce(self, out, in_, op, axis=AX.X):
        o, i = out.ap, in_.ap
        return self.op("dve", lambda e: e.tensor_reduce(o, i, axis, op), reads=[in_], writes=[out])

    def recip(self, out, in_):
        o, i = out.ap, in_.ap
        return self.op("dve", lambda e: e.reciprocal(o, i), reads=[in_], writes=[out])

    def _skip(self, d, r):
        return (not r.is_dma) and d.eng == r.eng and (d.eng == "pe" or not self.same_engine_sync)

    def finish(self, final_waits=()):
        nc = self.nc
        for r in self.recs:
            for d in r.deps:
                if d.is_dma or self._skip(d, r):
                    continue
                d.signaled = True
        for r in final_waits:
            if not r.is_dma:
                r.signaled = True
        cnt = {e: 0 for e in ENGS}
        for r in self.recs:
            if r.is_dma or not r.signaled:
                continue
            cnt[r.eng] += 1
            r.count = cnt[r.eng]
        cms = []
        sems = {}
        for e in ENGS:
            cm = nc.semaphore("s_" + e)
            sems[e] = cm.__enter__()
            cms.append(cm)
        dsems = []
        for i in range(min(N_DMA_SEMS, max(1, self.next_sem))):
            cm = nc.semaphore("d_%d" % i)
            dsems.append(cm.__enter__())
            cms.append(cm)
        streams = {e: [] for e in ENGS}
        seen = {e: {} for e in ENGS}
        dma_emitted = {}
        for r in self.recs:
            waits = {}
            for d in r.deps:
                if d.is_dma:
                    key = ("d", d.sem_idx)
                    val = 16 * dma_emitted[d.sem_idx]
                    sem = dsems[d.sem_idx]
                else:
                    if self._skip(d, r):
                        continue
                    key = ("c", d.eng)
                    val = d.count
                    sem = sems[d.eng]
                if seen[r.eng].get(key, 0) >= val:
                    continue
                if key not in waits or waits[key][1] < val:
                    waits[key] = (sem, val)
            for key, (sem, val) in waits.items():
                seen[r.eng][key] = val
            if r.is_dma:
                dma_emitted[r.sem_idx] = dma_emitted.get(r.sem_idx, 0) + 1
            streams[r.eng].append((list(waits.values()), r))
            r.deps = None
        fw = {}
        for r in final_waits:
            if r.is_dma:
                fw[("d", r.sem_idx)] = (dsems[r.sem_idx], 16 * dma_emitted[r.sem_idx])
            else:
                key = ("c", r.eng)
                if key not in fw or fw[key][1] < r.count:
                    fw[key] = (sems[r.eng], r.count)
        self.stats = {e: len(streams[e]) for e in ENGS}
        self.stats["sem_counts"] = dict(cnt)

        def make(ename):
            def body(e):
                for waits, r in streams[ename]:
                    for sem, val in waits:
                        e.wait_ge(sem, val)
                    ins = r.fn(e)
                    if r.is_dma:
                        ins.then_inc(dsems[r.sem_idx], 16)
                    elif r.signaled:
                        ins.then_inc(sems[r.eng], 1)
                if ename == "sp":
                    for sem, val in fw.values():
                        e.wait_ge(sem, val)
            return body

        with nc.Block() as block:
            block.tensor(make("pe"))
            block.scalar(make("act"))
            block.vector(make("dve"))
            block.gpsimd(make("pool"))
            block.sync(make("sp"))
        for cm in reversed(cms):
            cm.__exit__(None, None, None)
        while self.scope_stack:
            for cm in reversed(self.scope_stack.pop()):
                cm.__exit__(None, None, None)


VEC_LAYOUT = [("n1g", 8), ("n2g", 8), ("bmod", 48), ("pscale", 2), ("gq", 1), ("gk", 1), ("hcw", 18),
              ("hcb", 6), ("fb1", 1), ("fb2", 1), ("ffr", 1), ("hbias", 4), ("cqg", 2), ("ckvg", 1),
              ("mqg", 1), ("mkg", 1)]
W_IN_GROUPS = [(0, 256), (256, 160), (416, 256), (672, 256)] + [(928 + 128 * j, 128) for j in range(8)]
VEC_OFF = {}
_o = 0
for _n, _k in VEC_LAYOUT:
    VEC_OFF[_n] = (_o, _k)
    _o += _k
NV = _o


def _rot_tables(m, p):
    inv = (10000.0 ** (-np.arange(m, dtype=np.float32) / m)).astype(np.float32)
    ang = (p[None, :].astype(np.float32) * inv[:, None]).astype(np.float32)
    c = np.concatenate([np.cos(ang), np.cos(ang)], 0)
    s = np.concatenate([np.sin(ang), np.sin(ang)], 0)
    return c.astype(np.float32), s.astype(np.float32)


def _rot_matrix(dim, blocks):
    R = np.zeros((dim, dim), np.float32)
    for st, m in blocks:
        for d in range(m):
            R[st + d, st + d + m] = -1.0
            R[st + d + m, st + d] = 1.0
    return np.ascontiguousarray(R.T)


def build_consts():
    C = {}
    pos = np.arange(SEQ)
    row = (pos // 64).astype(np.float32)
    col = (pos % 64).astype(np.float32)
    c1, s1 = _rot_tables(16, row)
    c2, s2 = _rot_tables(16, col)
    c64 = np.concatenate([c1, c2], 0)
    s64 = np.concatenate([s1, s2], 0)
    C["ropeg"] = np.ascontiguousarray(np.stack([np.concatenate([c64, c64], 0), np.concatenate([s64, s64], 0)], 1)).astype(NPBF)
    c1, s1 = _rot_tables(8, row)
    c2, s2 = _rot_tables(8, col)
    c96 = np.concatenate([np.ones((64, SEQ), np.float32), c1, c2], 0)
    s96 = np.concatenate([np.zeros((64, SEQ), np.float32), s1, s2], 0)
    rm = np.zeros((2, 128, SEQ), np.float32)
    rm[0, :96] = c96
    rm[1, :96] = s96
    C["ropem"] = np.ascontiguousarray(rm.transpose(1, 0, 2)).astype(NPBF)
    rg = _rot_matrix(128, [(0, 16), (32, 16), (64, 16), (96, 16)])
    rmm = np.zeros((128, 128), np.float32)
    rmm[:96, :96] = _rot_matrix(96, [(64, 8), (80, 8)])
    mats = np.zeros((6, 128, 128), np.float32)
    mats[0] = np.eye(128)
    mats[1] = 1.0
    mats[2, :64, :64] = 1.0
    mats[2, 64:, 64:] = 1.0
    mats[3] = rg
    mats[4] = rmm
    mats[5, :32, 64:96] = np.eye(32)
    C["mats"] = mats
    sel = np.zeros((16, 16, 128), np.float32)
    for e in range(16):
        sel[e, e, :] = 1.0
    C["sel"] = sel.astype(NPBF)
    for n, tag in ((SEQ, "L"), (NCTX, "C")):
        N = 2 * n
        th = 2.0 * np.pi / N
        nt = n // 128
        cw = min(512, n)
        ncw = n // cw
        t = np.linspace(0.0, 1.0, n, dtype=np.float32)[:, None]
        lag = np.arange(n, dtype=np.float32)[:, None]
        bands = np.linspace(1e-4, 7.0, 8, dtype=np.float32)[None, :]
        ang = (np.float32(2.0 * math.pi / n) * lag * bands).astype(np.float32)
        feats = np.concatenate([t, np.cos(ang), -np.sin(ang)], -1).astype(np.float32)
        C["feats" + tag] = np.ascontiguousarray(feats.T)
        deltas = np.abs(np.linspace(math.log(1e-2) / 1.5, math.log(1e-2) / 0.3, 256, dtype=np.float32))
        dec = np.exp(-t * deltas[None, :]).astype(np.float32)
        C["decay" + tag] = np.ascontiguousarray(dec.T.reshape(2, 128, n))
        ic = np.zeros((2, 128, n), np.float32)
        tt = np.arange(n)
        for gi, w in enumerate((2, 4, 8, 16)):
            left = w // 2
            right = w - 1 - left
            lo = np.clip(tt - left, 0, n)
            hi = np.clip(tt + right + 1, 0, n)
            ic[gi // 2, (gi % 2) * 64:(gi % 2) * 64 + 64, :] = (1.0 / (hi - lo).astype(np.float32))[None, :]
        C["invcnt" + tag] = ic
        f = np.arange(n, dtype=np.float64)
        p = np.arange(n, dtype=np.float64)
        A = th * np.outer(p, f)
        Fc = np.cos(A)
        Fs = -np.sin(A)
        Fh = np.zeros((nt, 128, nt, 256), np.float32)
        Fc4 = Fc.reshape(nt, 128, nt, 128)
        Fs4 = Fs.reshape(nt, 128, nt, 128)
        Fh[:, :, :, 0:128] = Fc4.transpose(2, 1, 0, 3)
        Fh[:, :, :, 128:256] = Fs4.transpose(2, 1, 0, 3)
        C["F" + tag] = Fh.astype(NPBF)
        wf = np.full(n, 2.0)
        wf[0] = 1.0
        Gr = (wf[:, None] / N) * np.cos(A.T)
        Gi = -(2.0 / N) * np.sin(A.T)
        G = np.concatenate([Gr, Gi], 0)
        cwi = 128
        G4 = G.reshape(2 * nt, 128, n // cwi, cwi)
        C["G" + tag] = np.ascontiguousarray(G4.transpose(2, 1, 0, 3)).astype(NPBF)
        ny = np.zeros((128, 1 + n), np.float32)
        ny[:, 0] = (-1.0) ** np.arange(128)
        ny[0, 1:] = ((-1.0) ** np.arange(n)) / N
        C["ny" + tag] = ny.astype(NPBF)
    return C


_CONSTS = None


def get_consts():
    global _CONSTS
    if _CONSTS is None:
        _CONSTS = build_consts()
    return _CONSTS


def fm(v, rows=128):
    v = np.asarray(v, np.float32)
    return np.ascontiguousarray(v.reshape(-1, rows).T)


def pack_vecs(inp, l):
    out = np.zeros((128, NV), np.float32)

    def put(name, arr):
        o, k = VEC_OFF[name]
        arr = np.asarray(arr, np.float32)
        out[:arr.shape[0], o:o + k] = arr.reshape(arr.shape[0], k)

    put("n1g", fm(inp["norm1_g"][l]))
    put("n2g", fm(inp["norm2_g"][l]))
    put("bmod", fm(inp["b_mod"][l]))
    put("pscale", fm(inp["pool_scale"][l]))
    put("gq", np.tile(inp["gqa_qnorm_g"][l], 2)[:, None])
    put("gk", np.tile(inp["gqa_knorm_g"][l], 2)[:, None])
    hw = np.asarray(inp["hy_conv_w"][l], np.float32)
    put("hcw", hw.reshape(3, 6, 128).transpose(2, 1, 0).reshape(128, 18))
    put("hcb", fm(inp["hy_conv_b"][l]))
    put("fb1", np.asarray(inp["hy_f_b1"][l])[:, None])
    put("fb2", np.asarray(inp["hy_f_b2"][l])[:, None])
    put("ffr", np.asarray(inp["hy_freq"][l])[:, None])
    hb = np.asarray(inp["hy_bias"][l], np.float32)
    put("hbias", hb.reshape(2, 2, 128).transpose(2, 0, 1).reshape(128, 4))
    put("cqg", fm(inp["mla_cq_g"][l]))
    put("ckvg", fm(inp["mla_ckv_g"][l]))
    put("mqg", np.asarray(inp["mla_qnorm_g"][l])[:, None])
    put("mkg", np.asarray(inp["mla_knorm_g"][l])[:, None])
    return out


def build_program(nb=NB, depth=DEPTH, debug=False, stop_at=None, nexp=NEXP):
    nc = bass.Bass("TRN2", target_bir_lowering=False)
    P = Prog(nc)
    P.stop_at = stop_at
    P.used_inputs = []

    class LazyIn:
        def __init__(self, name, shape, dt):
            self.name, self.shape, self.dt, self.t = name, shape, dt, None

        def __getitem__(self, idx):
            if self.t is None:
                self.t = P.dram(self.name, self.shape, self.dt, kind="ExternalInput")
                P.used_inputs.append(self.name)
            return self.t[idx]

    def inp(name, shape, dt=F32):
        return LazyIn(name, shape, dt)

    xT_in = inp("xT", [nb, D, SEQ])
    cxT_in = inp("ctxT", [nb, D, NCTX])
    cT_in = inp("cT", [128, 8, nb + 1])
    vecs_in = inp("vecs", [DEPTH, 128, NV])
    wmod_in = inp("w_mod", [DEPTH, D, 6 * D])
    win_in = inp("w_in", [DEPTH, 128, 8 * 1952])
    wout_in = inp("w_out", [DEPTH, 128, 8, D])
    wout64_in = inp("w_out64", [DEPTH, 64, 8, D])
    poolbd_in = inp("poolbd", [DEPTH, 2, 128, 128])
    hw1_in = inp("hy_f_w1", [DEPTH, 17, 64])
    hw2_in = inp("hy_f_w2", [DEPTH, 64, 64])
    hw3_in = inp("hy_f_w3", [DEPTH, 64, 1024])
    wuq_in = inp("mla_w_uq", [DEPTH, 256, 384])
    wukvk_in = inp("wukv_k", [DEPTH, 128, 4, 96])
    wukvv_in = inp("wukv_v", [DEPTH, 128, 256])
    wr_in = inp("router_w", [DEPTH, D, 16])
    wgu_in = inp("exp_w_gu", [DEPTH, nexp, 128, 8, 1024])
    wd_in = inp("exp_w_dn", [DEPTH, nexp, 128, 4, D])
    ropeg_in = inp("ropeg", [128, 2, SEQ], BF16)
    ropem_in = inp("ropem", [128, 2, SEQ], BF16)
    mats_in = inp("mats", [6, 128, 128])
    sel_in = inp("sel", [16, 16, 128], BF16)
    CI = {}
    for n, tag in ((SEQ, "L"), (NCTX, "C")):
        nt = n // 128
        cw = min(512, n)
        CI["feats" + tag] = inp("feats" + tag, [17, n])
        CI["decay" + tag] = inp("decay" + tag, [2, 128, n])
        CI["invcnt" + tag] = inp("invcnt" + tag, [2, 128, n])
        CI["F" + tag] = inp("F" + tag, [nt, 128, nt, 256], BF16)
        cwi = 128
        CI["G" + tag] = inp("G" + tag, [n // cwi, 128, 2 * nt, cwi], BF16)
        CI["ny" + tag] = inp("ny" + tag, [128, 1 + n], BF16)

    if debug == "force_decl":
        for t_ in (wgu_in, wd_in):
            t_[0, 0, 0:1, :, :]
    outT = P.dram("outT", [nb, D, SEQ], F32, kind="ExternalOutput")
    dbg = {}

    dk = "ExternalOutput" if debug else "Internal"
    xs_d = P.dram("xs_d", [nb, D, SEQ], F32, kind=dk)
    xc_d = P.dram("xc_d", [nb, D, NCTX], F32, kind=dk)
    SCR = {}
    for tag, n in (("L", SEQ), ("C", NCTX)):
        SCR[tag] = dict(
            xmid=P.dram("xmid_" + tag, [D, n], F32, kind=dk),
            mix=P.dram("mix_" + tag, [D, n], BF16, kind=dk),
            pu=P.dram("pu_" + tag, [256, n], F32, kind=dk),
            hu=P.dram("hu_" + tag, [768, n], F32, kind=dk),
        )
    hf_d = {"L": P.dram("hf_L", [2, SEQ // 128, 128, 2, 256], F32, kind=dk),
            "C": P.dram("hf_C", [2, NCTX // 128, 128, 2, 256], F32, kind=dk)}

    wgu_bf = [P.dram("wgu_bf%d" % i, [NEXP // 2, 128, 8, 1024], BF16) for i in range(2)]
    wdn_bf = [P.dram("wdn_bf%d" % i, [NEXP // 2, 128, 4, D], BF16) for i in range(2)]

    win_bf = P.dram("win_bf", [128, 8 * 1952], BF16)
    wout_bf = P.dram("wout_bf", [128, 8, D], BF16)
    wout64_bf = P.dram("wout64_bf", [64, 8, D], BF16)
    LW = {}

    def precast_dense(l):
        P.dma(win_bf[:, :], win_in[l, :, :], eng="pool")
        P.dma(wout_bf[:, :, :], wout_in[l, :, :, :], eng="pool")
        P.dma(wout64_bf[:, :, :], wout64_in[l, :, :, :], eng="pool")
        P.dma(LW["wk"][:], wukvk_in[l, :, :, :], eng="pool")
        P.dma(LW["wv"][:], wukvv_in[l, :, :], eng="pool")
        P.dma(LW["wuq"][:], wuq_in[l, :, :].re("(k p) n -> p k n", p=128), eng="pool")
        P.dma(LW["wbd"][:], poolbd_in[l, :, :, :].re("c p e -> p c e"), eng="pool")
        P.dma(LW["wr"][:], wr_in[l, :, :].re("(k p) e -> p k e", p=128), eng="pool")

    def precast_experts(l):
        for e in range(nexp):
            P.dma(wgu_bf[e % 2][e // 2, :, :, :], wgu_in[l, e, :, :, :], eng="pool")
            P.dma(wdn_bf[e % 2][e // 2, :, :, :], wd_in[l, e, :, :, :], eng="pool")

    dbg_gm = {}
    if debug:
        dbg_gm = {"L": P.dram("gm_L", [16, SEQ], F32, kind="ExternalOutput"),
                  "C": P.dram("gm_C", [16, NCTX], F32, kind="ExternalOutput")}

    mats = P.sbuf("mats", [128, 6, 128], F32)
    matsb = P.sbuf("matsb", [128, 6, 128], BF16)
    P.dma(mats[:], mats_in[:, :, :].re("m p c -> p m c"))
    P.dma(matsb[:], mats_in[:, :, :].re("m p c -> p m c"), eng="pool")
    IDF, ONF, BLKF, RG, RM, SELKR = [mats[:, i, :] for i in range(6)]
    IDB, ONB, BLKB = [matsb[:, i, :] for i in range(3)]
    SELKRB = matsb[:, 5, :]
    vecs = P.sbuf("vecs", [128, NV], F32)
    modT = P.sbuf("modT", [128, 48, nb + 1], F32)
    epsb = P.sbuf("epsb", [128, 1], F32)
    P.memset(epsb[:], EPS)
    hn = {"L": P.sbuf("hnL", [1, 512], F32), "C": P.sbuf("hnC", [1, 512], F32)}
    LW["wk"] = P.sbuf("wk", [128, 4, 96], BF16)
    LW["wv"] = P.sbuf("wv", [128, 256], BF16)
    LW["wuq"] = P.sbuf("wuq", [128, 2, 384], BF16)
    LW["wbd"] = P.sbuf("wbd", [128, 2, 128], BF16)
    LW["wr"] = P.sbuf("wr", [128, 8, 16], BF16)
    ps = [P.psum("ps%d" % i, [128, 512], F32) for i in range(7)]
    psb = P.psum("psb", [128, 1024], BF16)
    KV = {}

    def vec(name, j=0, rows=128):
        o, k = VEC_OFF[name]
        return vecs[0:rows, o + j:o + j + 1]

    evac_flip = [0]

    def evac(out, in_):
        evac_flip[0] ^= 1
        if evac_flip[0]:
            P.copy(out, in_, eng="act")
        else:
            P.copy(out, in_, eng="dve")

    def q2(i):
        return "sp"

    def layer_prologue(l):
        P.dma(vecs[:], vecs_in[l, :, :])
        with P.scope():
            sil = P.sbuf("sil", [128, 8, nb + 1], F32)
            P.dma(sil[:], cT_in[:, :, :])
            P.act(sil[:], sil[:], AF.Silu)
            wbuf = [P.sbuf("wmod%d" % i, [128, 8, 512], F32) for i in range(2)]
            for piece in range(12):
                wb = wbuf[piece % 2]
                P.dma(wb[:], wmod_in[l, :, piece * 512:(piece + 1) * 512].re("(k p) n -> p k n", p=128),
                      eng=q2(piece))
                for jj in range(4):
                    j = piece * 4 + jj
                    pt = ps[j % 2]
                    for k in range(8):
                        P.mm(pt[:, 0:nb + 1], wb[:, k, jj * 128:(jj + 1) * 128], sil[:, k, :],
                             start=(k == 0), stop=(k == 7))
                    P.ts(modT[:, j, :], pt[:, 0:nb + 1], vec("bmod", j), ALU.add)

    def filter_gen(l, n, tag):
        nt = n // 128
        cw = min(512, n)
        ncw = n // cw
        with P.scope():
            edT = P.sbuf("fedT", [128, 2, nt, 512], BF16)
            with P.scope():
                ed = P.sbuf("fed", [128, 2, 4, n], BF16)
                with P.scope():
                    hid2 = P.sbuf("hid2", [64, n], F32)
                    w3 = P.sbuf("fw3", [64, 1024], F32)
                    P.dma(w3[:], hw3_in[l, :, :])
                    with P.scope():
                        feats = P.sbuf("feats", [17, n], F32)
                        P.dma(feats[:], CI["feats" + tag][:, :])
                        w1 = P.sbuf("fw1", [17, 64], F32)
                        w2 = P.sbuf("fw2", [64, 64], F32)
                        P.dma(w1[:], hw1_in[l, :, :])
                        P.dma(w2[:], hw2_in[l, :, :])
                        frb = P.sbuf("frb", [64, 2], F32)
                        P.tt(frb[:, 0:1], vec("ffr", 0, 64), vec("fb1", 0, 64), ALU.mult)
                        P.tt(frb[:, 1:2], vec("ffr", 0, 64), vec("fb2", 0, 64), ALU.mult)
                        hid1 = P.sbuf("hid1", [64, n], F32)
                        arg = P.sbuf("farg", [64, cw], F32)
                        kk = P.sbuf("fkk", [64, cw], F32)

                        def sin_layer(dst, w, src, bcol):
                            for c in range(ncw):
                                sl = slice(c * cw, (c + 1) * cw)
                                P.mm(ps[0][0:64, 0:cw], w, src[:, sl])
                                P.ts(arg[:], ps[0][0:64, 0:cw], vec("ffr", 0, 64), ALU.mult,
                                     frb[:, bcol:bcol + 1], ALU.add)
                                P.ts(kk[:], arg[:], 1.0 / TWO_PI, ALU.mult, MAGIC, ALU.add)
                                P.ts(kk[:], kk[:], -MAGIC, ALU.add, TWO_PI, ALU.mult)
                                P.tt(arg[:], arg[:], kk[:], ALU.subtract)
                                P.ts(arg[:], arg[:], -3.141592, ALU.max, 3.141592, ALU.min)
                                P.act(dst[:, sl], arg[:], AF.Sin)

                        sin_layer(hid1, w1[:], feats, 0)
                        sin_layer(hid2, w2[:], hid1, 1)
                    decay = P.sbuf("decay", [128, 2, n], F32)
                    P.dma(decay[:], CI["decay" + tag][:, :, :].re("c p n -> p c n"))
                    hh = P.sbuf("hh", [128, 2, n], F32)
                    sq = P.sbuf("fsq", [128, n], F32)
                    tmp = P.sbuf("ftmp", [128, n], F32)
                    ssq = P.sbuf("ssq", [128, 2], F32)
                    rn = P.sbuf("frn", [128, 1], F32)
                    for q in range(4):
                        for d_ in range(2):
                            j = d_ * 4 + q
                            for c in range(ncw):
                                sl = slice(c * cw, (c + 1) * cw)
                                pt = ps[c % 2]
                                P.mm(pt[:, 0:cw], w3[:, j * 128:(j + 1) * 128], hid2[:, sl])
                                P.tt(hh[:, d_, sl], pt[:, 0:cw], decay[:, q % 2, sl], ALU.mult)
                        P.memset(hh[:, 1, 0:1], 0.0)
                        for d_ in range(2):
                            P.tt(sq[:], hh[:, d_, :], hh[:, d_, :], ALU.mult)
                            P.reduce(ssq[:, d_:d_ + 1], sq[:], ALU.add)
                        P.tt(rn[:], ssq[:, 0:1], ssq[:, 1:2], ALU.add)
                        P.act(rn[:], rn[:], AF.Sqrt, bias=epsb[:, 0:1], scale=1.0)
                        P.recip(rn[:], rn[:])
                        P.tt(tmp[:], hh[:, 0, :], hh[:, 1, :], ALU.add)
                        P.ts(ed[:, 0, q, :], tmp[:], rn[:, 0:1], ALU.mult)
                        P.tt(tmp[:], hh[:, 0, :], hh[:, 1, :], ALU.subtract)
                        P.ts(ed[:, 1, q, :], tmp[:], rn[:, 0:1], ALU.mult)
                for w_ in range(2):
                    for s in range(nt):
                        for q in range(4):
                            P.transpose(psb[:, q * 128:(q + 1) * 128], ed[:, w_, q, s * 128:(s + 1) * 128], IDB)
                        evac(edT[:, w_, s, :], psb[:, 0:512])
            ny = P.sbuf("fny", [128, 1 + n], BF16)
            P.dma(ny[:], CI["ny" + tag][:, :])
            for s in range(nt):
                P.mm(ps[2][0:1, :], ny[:, 0:1], edT[:, 0, s, :], start=(s == 0), stop=(s == nt - 1))
            P.copy(hn[tag][:], ps[2][0:1, :])
            fbuf = [P.sbuf("fF%d" % i, [128, nt, 256], BF16) for i in range(2)]
            hfs = [P.sbuf("fhfs%d" % i, [128, 2, 2, 256], F32) for i in range(2)]
            for fk in range(nt):
                fb = fbuf[fk % 2]
                P.dma(fb[:], CI["F" + tag][fk, :, :, :], eng=q2(fk))
                for half in range(2):
                    pt = ps[half]
                    for s in range(nt):
                        P.mm(pt[:, :], fb[:, s, half * 128:(half + 1) * 128], edT[:, half, s, :],
                             start=(s == 0), stop=(s == nt - 1))
                hs = hfs[fk % 2]
                evac(hs[:, :, 0, :], ps[0][:, :].re("p (o c) -> p o c", o=2))
                evac(hs[:, :, 1, :], ps[1][:, :].re("p (o c) -> p o c", o=2))
                for o in range(2):
                    P.dma(hf_d[tag][o, fk, :, :, :], hs[:, o, :, :], eng=q2(o))

    def norm_mod(dst_bf, x_view_fn, n, gname, sh_j, sc_j, bcol):
        cw = min(512, n)
        ncw = n // cw
        with P.scope():
            A = P.sbuf("nmA", [128, 8], F32)
            P.ts(A[:], modT[:, sc_j:sc_j + 8, bcol], 1.0, ALU.add)
            o, k = VEC_OFF[gname]
            P.tt(A[:], A[:], vecs[:, o:o + 8], ALU.mult)
            xfs = [P.sbuf("nmx%d" % i, [128, 8, cw], F32) for i in range(2)]
            sq = P.sbuf("nmsq", [128, 8, cw], BF16)
            rstd = P.sbuf("nmrstd", [128, cw], F32)
            tmp = P.sbuf("nmtmp", [128, cw], F32)
            for c in range(ncw):
                sl = slice(c * cw, (c + 1) * cw)
                xf = xfs[c % 2]
                P.dma(xf[:], x_view_fn(sl), eng=q2(c))
                P.act(sq[:, 0:4, :], xf[:, 0:4, :], AF.Square)
                P.tt(sq[:, 4:8, :], xf[:, 4:8, :], xf[:, 4:8, :], ALU.mult)
                pt = ps[c % 2]
                for k in range(8):
                    P.mm(pt[:, 0:cw], ONB, sq[:, k, :], start=(k == 0), stop=(k == 7))
                P.act(rstd[:], pt[:, 0:cw], AF.Sqrt, bias=epsb[:, 0:1], scale=1.0 / D)
                P.recip(rstd[:], rstd[:])
                for k in range(8):
                    P.tt(tmp[:], xf[:, k, :], rstd[:], ALU.mult)
                    P.ts(dst_bf[:, k, sl], tmp[:], A[:, k:k + 1], ALU.mult,
                         modT[:, sh_j + k, bcol:bcol + 1], ALU.add)

    HS = {}

    def headnorm_alloc(n):
        cw = min(512, n)
        HS["g"] = [P.sbuf("hg%d" % i, [128, 1], F32) for i in range(2)]
        HS["sq"] = [P.sbuf("hsq%d" % i, [128, cw], BF16) for i in range(2)]
        HS["rstd"] = [P.sbuf("hrstd%d" % i, [128, cw], F32) for i in range(2)]
        HS["xn"] = P.sbuf("hxn", [128, n], F32)
        HS["t1"] = [P.sbuf("ht1%d" % i, [128, cw], F32) for i in range(2)]
        HS["i"] = 0

    def headnorm_rope(dst, src, rows, n, ones_blk, gcol, gscale, rmat, rope, inv_d):
        cw = min(512, n)
        ncw = n // cw
        HS["i"] += 1
        g = HS["g"][HS["i"] % 2]
        sq, rstd, xn, t1 = HS["sq"], HS["rstd"], HS["xn"], HS["t1"]
        P.ts(g[0:rows, :], gcol, gscale, ALU.mult)
        for c in range(ncw):
            sl = slice(c * cw, (c + 1) * cw)
            sq_, rs_ = sq[c % 2], rstd[c % 2]
            pt = ps[c % 4]
            P.act(sq_[0:rows, :], src[0:rows, sl], AF.Square)
            P.mm(pt[0:rows, 0:cw], ones_blk, sq_[0:rows, :])
            P.act(rs_[0:rows, :], pt[0:rows, 0:cw], AF.Sqrt, bias=epsb[0:rows, 0:1], scale=inv_d)
            P.recip(rs_[0:rows, :], rs_[0:rows, :])
            if rope is None:
                P.stt(dst[0:rows, sl], src[0:rows, sl], g[0:rows, 0:1], rs_[0:rows, :], ALU.mult, ALU.mult)
            else:
                P.stt(xn[0:rows, sl], src[0:rows, sl], g[0:rows, 0:1], rs_[0:rows, :], ALU.mult, ALU.mult)
        if rope is not None:
            for c in range(ncw):
                sl = slice(c * cw, (c + 1) * cw)
                pt = ps[4 + c % 2]
                t1_ = t1[c % 2]
                P.mm(pt[0:rows, 0:cw], rmat, xn[0:rows, sl])
                P.tt(t1_[0:rows, :], pt[0:rows, 0:cw], rope[0:rows, 1, sl], ALU.mult)
                P.tt(xn[0:rows, sl], xn[0:rows, sl], rope[0:rows, 0, sl], ALU.mult)
                P.tt(dst[0:rows, sl], xn[0:rows, sl], t1_[0:rows, :], ALU.add)

    def seq_params(l, b, n, is_ctx, last):
        tag = "C" if is_ctx else "L"
        if l == 0:
            x_src = (cxT_in if is_ctx else xT_in)
        else:
            x_src = (xc_d if is_ctx else xs_d)
        if is_ctx:
            x_dst = xc_d
        else:
            x_dst = outT if last else xs_d
        return tag, n // 128, min(512, n), n // min(512, n), (nb if is_ctx else b), x_src, x_dst

    def seq_front(l, b, n, is_ctx, last):
        tag, nt, cw, ncw, bcol, x_src, x_dst = seq_params(l, b, n, is_ctx, last)
        S = SCR[tag]
        kg_all, vg_all, km_all, vm_all = KV["kg"], KV["vg"], KV["km"], KV["vm"]
        koff = SEQ if is_ctx else 0
        ktoff = koff // 128
        kv_only = is_ctx and last
        with P.scope():
            qg = P.sbuf("qg", [128, 2, n], BF16)
            qm = P.sbuf("qm", [96, 4, n], BF16)
            with P.scope():
                hT = P.sbuf("hT", [128, 8, n], BF16)
                headnorm_alloc(n)
                norm_mod(hT, lambda sl: x_src[b, :, sl].re("(k p) n -> p k n", p=128), n, "n1g", 0, 8, bcol)

                lw_i = [0]

                def load_w(c0, m):
                    w = P.sbuf("wsub", [128, 8, m], BF16)
                    lw_i[0] += 1
                    P.dma(w[:], win_bf[:, 8 * c0:8 * (c0 + m)].re("p (k m) -> p k m", k=8), eng=q2(lw_i[0]))
                    return w

                def proj_fm(w, w0, m, dst_fn):
                    for c in range(ncw):
                        sl = slice(c * cw, (c + 1) * cw)
                        pt = ps[2 + c % 2]
                        for k in range(8):
                            P.mm(pt[0:m, 0:cw], w[:, k, w0:w0 + m], hT[:, k, sl], start=(k == 0), stop=(k == 7))
                        dst_fn(sl, pt[0:m, 0:cw])

                with P.scope():
                    rope_g = None
                    if not is_ctx:
                        rope_g = P.sbuf("ropeg", [128, 2, SEQ], BF16)
                        P.dma(rope_g[:], ropeg_in[:, :, :], eng="sp")
                    w = load_w(0, 256)
                    src = P.sbuf("pj", [128, n], F32)
                    proj_fm(w, 0, 128, lambda sl, pv: evac(src[:, sl], pv))
                    headnorm_rope(kg_all[:, koff:koff + n], src, 128, n, BLKB, vec("gk"), 1.0, RG, rope_g, 1.0 / 64)
                    for t in range(nt):
                        pt = ps[4 + t % 2]
                        for k in range(8):
                            P.mm(pt[:, 0:128], hT[:, k, t * 128:(t + 1) * 128], w[:, k, 128:256],
                                 start=(k == 0), stop=(k == 7))
                        evac(vg_all[:, ktoff + t, :, 0:64], pt[:, 0:128].re("p (h d) -> p h d", h=2))
                    if not kv_only:
                        wq = load_w(416, 256)
                        for j in range(2):
                            proj_fm(wq, j * 128, 128, lambda sl, pv: evac(src[:, sl], pv))
                            headnorm_rope(qg[:, j, :], src, 128, n, BLKB, vec("gq"), 64 ** -0.5, RG, rope_g, 1.0 / 64)
                rope_m = None
                with P.scope():
                    if not is_ctx:
                        rope_m = P.sbuf("ropem", [128, 2, SEQ], BF16)
                        P.dma(rope_m[:], ropem_in[:, :, :], eng="sp")
                    w = load_w(256, 160)
                    src = P.sbuf("pj", [128, n], F32)
                    proj_fm(w, 0, 128, lambda sl, pv: evac(src[:, sl], pv))
                    ckvn = P.sbuf("ckvn", [128, n], BF16)
                    headnorm_rope(ckvn[:, :], src, 128, n, ONB, vec("ckvg"), 1.0, None, None, 1.0 / 128)
                    kr = P.sbuf("kr", [32, n], BF16)
                    proj_fm(w, 128, 32, lambda sl, pv: evac(kr[:, sl], pv))
                    wk, wv = LW["wk"], LW["wv"]
                    for h in range(4):
                        for c in range(ncw):
                            sl = slice(c * cw, (c + 1) * cw)
                            pt = ps[2 + c % 2]
                            P.mm(pt[0:96, 0:cw], wk[:, h, :], ckvn[:, sl], start=True, stop=False)
                            P.mm(pt[0:96, 0:cw], SELKRB[0:32, 0:96], kr[:, sl], start=False, stop=True)
                            evac(src[0:96, sl], pt[0:96, 0:cw])
                        headnorm_rope(km_all[:, h, koff:koff + n], src, 96, n, ONB[0:96, 0:96], vec("mkg", 0, 96),
                                      1.0, RM[0:96, 0:96], rope_m, 1.0 / 96)
                    for t in range(nt):
                        pt = ps[4 + t % 2]
                        P.mm(pt[:, 0:256], ckvn[:, t * 128:(t + 1) * 128], wv[:, :])
                        evac(vm_all[:, ktoff + t, :, 0:64], pt[:, 0:256].re("p (h d) -> p h d", h=4))
                    if not kv_only:
                        w = load_w(672, 256)
                        cqn = P.sbuf("cqn", [128, 2, n], BF16)
                        wuq = LW["wuq"]
                        if True:
                            cq = P.sbuf("cq", [128, 2, cw], F32)
                            sqc = P.sbuf("sqc", [128, 2, cw], BF16)
                            rstd = P.sbuf("cqrstd", [128, cw], F32)
                            for c in range(ncw):
                                sl = slice(c * cw, (c + 1) * cw)
                                for j in range(2):
                                    pt = ps[2 + j]
                                    for k in range(8):
                                        P.mm(pt[:, 0:cw], w[:, k, j * 128:(j + 1) * 128], hT[:, k, sl],
                                             start=(k == 0), stop=(k == 7))
                                    evac(cq[:, j, :], pt[:, 0:cw])
                                P.act(sqc[:, :, :], cq[:, :, :], AF.Square)
                                for j in range(2):
                                    P.mm(ps[0][:, 0:cw], ONB, sqc[:, j, :], start=(j == 0), stop=(j == 1))
                                P.act(rstd[:], ps[0][:, 0:cw], AF.Sqrt, bias=epsb[:, 0:1], scale=1.0 / 256)
                                P.recip(rstd[:], rstd[:])
                                for j in range(2):
                                    P.stt(cqn[:, j, sl], cq[:, j, :], vec("cqg", j), rstd[:], ALU.mult, ALU.mult)
                        for h in range(4):
                            for c in range(ncw):
                                sl = slice(c * cw, (c + 1) * cw)
                                pt = ps[2 + c % 2]
                                for j in range(2):
                                    P.mm(pt[0:96, 0:cw], wuq[:, j, h * 96:(h + 1) * 96], cqn[:, j, sl],
                                         start=(j == 0), stop=(j == 1))
                                evac(src[0:96, sl], pt[0:96, 0:cw])
                            headnorm_rope(qm[:, h, :], src, 96, n, ONB[0:96, 0:96], vec("mqg", 0, 96),
                                          96 ** -0.5, RM[0:96, 0:96], rope_m, 1.0 / 96)
                if not kv_only:
                    with P.scope():
                        stg = [P.sbuf("stg%d" % i, [128, n], F32) for i in range(2)]
                        wph = [load_w(928 + j * 128, 128) for j in range(8)]
                        for j in range(8):
                            st = stg[j % 2]
                            w = wph[j]
                            proj_fm(w, 0, 128, lambda sl, pv: evac(st[:, sl], pv))
                            if j < 2:
                                P.dma(S["pu"][j * 128:(j + 1) * 128, :], st[:, :], eng=q2(j))
                            else:
                                P.dma(S["hu"][(j - 2) * 128:(j - 1) * 128, :], st[:, :], eng=q2(j))
            P.checkpoint("A" + tag)
            if kv_only:
                return
            nk = n if is_ctx else SEQ + NCTX
            k0 = SEQ if is_ctx else 0
            nkt = nk // 128
            kt0 = k0 // 128
            with P.scope():
                pT = [P.sbuf("pT%d" % i, [128, cw], BF16) for i in range(3)]
                rs = P.sbuf("rs", [128, cw], F32)
                bcs = P.sbuf("bcs", [64, cw], F32)
                ob = [P.sbuf("ob%d" % i, [64, n], BF16) for i in range(2)]
                pend = [None]

                def flush():
                    if pend[0] is not None:
                        pend[0]()
                        pend[0] = None

                for hh_ in range(8):
                    o_t = ob[hh_ % 2]
                    if hh_ < 4:
                        row0 = 256 + hh_ * 64
                    else:
                        row0 = 768 + (hh_ - 4) * 64
                    for c in range(ncw):
                        sl = slice(c * cw, (c + 1) * cw)
                        acc = ps[4 + (hh_ * ncw + c) % 2]

                        def mm1(kt):
                            st_ = ps[kt % 3]
                            ks = slice(k0 + kt * 128, k0 + (kt + 1) * 128)
                            if hh_ < 4:
                                chunk, half = hh_ % 2, hh_ // 2
                                P.mm(st_[:, 0:cw], kg_all[half * 64:(half + 1) * 64, ks],
                                     qg[half * 64:(half + 1) * 64, chunk, sl])
                            else:
                                P.mm(st_[:, 0:cw], km_all[:, hh_ - 4, ks], qm[:, hh_ - 4, sl])

                        def fin(acc=acc, o_t=o_t, sl=sl, last_c=(c == ncw - 1), row0=row0, hh_=hh_):
                            P.recip(rs[64:65, :], acc[64:65, 0:cw])
                            P.mm(ps[3][0:64, 0:cw], ONF[64:65, 0:64], rs[64:65, :])
                            P.copy(bcs[:], ps[3][0:64, 0:cw], eng="act")
                            P.tt(o_t[:, sl], acc[0:64, 0:cw], bcs[:], ALU.mult)
                            if last_c:
                                P.dma(S["mix"][row0:row0 + 64, :], o_t[:, :], eng=q2(hh_))

                        mm1(0)
                        if nkt > 1:
                            mm1(1)
                        for kt in range(nkt):
                            if kt + 2 < nkt:
                                mm1(kt + 2)
                            if hh_ < 4:
                                vv = vg_all[:, kt0 + kt, hh_ // 2, :]
                            else:
                                vv = vm_all[:, kt0 + kt, hh_ - 4, :]
                            p_t = pT[kt % 3]
                            P.act(p_t[:], ps[kt % 3][:, 0:cw], AF.Exp)
                            P.mm(acc[0:65, 0:cw], vv, p_t[:], start=(kt == 0), stop=(kt == nkt - 1))
                            if kt == 1:
                                flush()
                        pend[0] = fin
                flush()

    def seq_back(l, b, n, is_ctx, last):
        tag, nt, cw, ncw, bcol, x_src, x_dst = seq_params(l, b, n, is_ctx, last)
        S = SCR[tag]
        with P.scope():
            rights = (0, 1, 3, 7)
            wbd = LW["wbd"]
            U = P.sbuf("pU", [128, n + 32], F32)
            A = [P.sbuf("pA%d" % i, [128, n + 32], F32) for i in range(2)]
            ic = P.sbuf("pic", [128, n], F32)
            dd = P.sbuf("pdd", [128, n], F32)
            db = P.sbuf("pdb", [128, n], BF16)
            ob = P.sbuf("pob", [128, n], BF16)
            P.memset(U[:], 0.0)
            P.memset(A[0][:], 0.0)
            P.memset(A[1][:], 0.0)
            ext = n + 8
            for ch in range(2):
                P.dma(U[:, 16:16 + n], S["pu"][ch * 128:(ch + 1) * 128, :])
                P.dma(ic[:], CI["invcnt" + tag][ch, :, :], eng="sp")
                cur = U
                for wi, w in enumerate((2, 4, 8, 16)):
                    nxt = A[wi % 2]
                    sh = w // 2
                    P.tt(nxt[:, 16:16 + ext], cur[:, 16:16 + ext], cur[:, 16 - sh:16 - sh + ext], ALU.add)
                    cur = nxt
                    g = wi - 2 * ch
                    if g in (0, 1):
                        r_ = rights[wi]
                        psl = slice(g * 64, g * 64 + 64)
                        P.tt(dd[psl, :], cur[psl, 16 + r_:16 + r_ + n], ic[psl, :], ALU.mult)
                        P.tt(db[psl, :], dd[psl, :], U[psl, 16:16 + n], ALU.subtract)
                for c in range(ncw):
                    sl = slice(c * cw, (c + 1) * cw)
                    pt = ps[c % 2]
                    P.mm(pt[:, 0:cw], wbd[:, ch, :], db[:, sl])
                    P.ts(ob[:, sl], pt[:, 0:cw], vec("pscale", ch), ALU.mult)
                P.dma(S["mix"][ch * 128:(ch + 1) * 128, :], ob[:, :])

        P.checkpoint("C" + tag)
        with P.scope():
            NF = nt
            cwi = 128
            ncwi = n // cwi
            vx = P.sbuf("hvx", [128, 6, n], F32)
            with P.scope():
                up = P.sbuf("hup", [128, n + 2], F32)
                P.memset(up[:], 0.0)
                o_, _k = VEC_OFF["hcw"]
                for j in range(6):
                    P.dma(up[:, 1:n + 1], S["hu"][j * 128:(j + 1) * 128, :], eng=q2(j))
                    w0 = vecs[:, o_ + j * 3:o_ + j * 3 + 1]
                    w1_ = vecs[:, o_ + j * 3 + 1:o_ + j * 3 + 2]
                    w2_ = vecs[:, o_ + j * 3 + 2:o_ + j * 3 + 3]
                    P.ts(vx[:, j, :], up[:, 0:n], w0, ALU.mult, vec("hcb", j), ALU.add)
                    P.stt(vx[:, j, :], up[:, 1:n + 1], w1_, vx[:, j, :], ALU.mult, ALU.add)
                    P.stt(vx[:, j, :], up[:, 2:n + 2], w2_, vx[:, j, :], ALU.mult, ALU.add)
            ny = P.sbuf("hny", [128, 1 + n], BF16)
            P.dma(ny[:], CI["ny" + tag][:, :])
            zin = P.sbuf("hzin", [128, 2, n], F32)
            zb = P.sbuf("hzb", [128, 2, n], BF16)
            ztok = P.sbuf("hztok", [128, nt, 256], BF16)
            Y = P.sbuf("hY", [128, 2 * NF, 256], BF16)
            yn = P.sbuf("hyn", [1, 256], BF16)
            NFB = 3 if n > 256 else 2
            fbuf = [P.sbuf("hF%d" % i, [128, nt, 256], BF16) for i in range(NFB)]
            hfb = [P.sbuf("hhf%d" % i, [128, 2, 256], F32) for i in range(NFB)]
            gbuf = [P.sbuf("hG%d" % i, [128, 2 * NF, cwi], BF16) for i in range(2)]
            ta = P.sbuf("hta", [128, 2, 256], F32)
            tb = P.sbuf("htb", [128, 2, 256], F32)
            oh = P.sbuf("hoh", [128, 2, n], BF16)
            taf = ta[:, :, :].re("p h c -> p (h c)")
            for order in range(2):
                src = vx[:, 0:2, :] if order == 0 else zin[:, :, :]
                P.copy(zb[:, 0, :], src[:, 0, :], eng="act")
                P.copy(zb[:, 1, :], src[:, 1, :], eng="dve")
                for s in range(nt):
                    for ch in range(2):
                        P.transpose(psb[:, ch * 128:(ch + 1) * 128], zb[:, ch, s * 128:(s + 1) * 128], IDB)
                    evac(ztok[:, s, :], psb[:, 0:256])
                for s in range(nt):
                    P.mm(ps[2][0:1, 0:256], ny[:, 0:1], ztok[:, s, :], start=(s == 0), stop=(s == nt - 1))
                P.tt(yn[:, :], ps[2][0:1, 0:256], hn[tag][:, order * 256:(order + 1) * 256], ALU.mult)
                for fk in range(NF):
                    fb = fbuf[fk % NFB]
                    P.dma(fb[:], CI["F" + tag][fk, :, :, :], eng=q2(fk))
                    hb = hfb[fk % NFB]
                    P.dma(hb[:], hf_d[tag][order, fk, :, :, :], eng=q2(fk + 1))
                    pz = ps[fk % 2]
                    for half in range(2):
                        for s in range(nt):
                            P.mm(pz[:, half * 256:(half + 1) * 256], fb[:, s, half * 128:(half + 1) * 128],
                                 ztok[:, s, :], start=(s == 0), stop=(s == nt - 1))
                    zv = pz[:, :].re("p (h c) -> p h c", h=2)
                    P.tt(ta[:], zv, hb[:, 0:1, :].bc([128, 2, 256]), ALU.mult)
                    P.tt(tb[:], zv, hb[:, 1:2, :].bc([128, 2, 256]), ALU.mult)
                    P.tt(Y[:, fk, :], ta[:, 0, :], tb[:, 1, :], ALU.subtract)
                    P.tt(Y[:, NF + fk, :], tb[:, 0, :], ta[:, 1, :], ALU.add)
                for c in range(ncwi):
                    sl = slice(c * cwi, (c + 1) * cwi)
                    gb = gbuf[c % 2]
                    P.dma(gb[:], CI["G" + tag][c, :, :, :], eng=q2(c))
                    for ch in range(2):
                        pt = ps[2 + (2 * c + ch) % 4]
                        for r in range(2 * NF):
                            P.mm(pt[:, 0:cwi], Y[:, r, ch * 128:(ch + 1) * 128], gb[:, r, :],
                                 start=(r == 0), stop=False)
                        P.mm(pt[:, 0:cwi], yn[0:1, ch * 128:(ch + 1) * 128], ny[0:1, 1 + c * cwi:1 + (c + 1) * cwi],
                             start=False, stop=True)
                        bias = vec("hbias", order * 2 + ch)
                        if order == 0:
                            P.stt(zin[:, ch, sl], vx[:, ch, sl], bias, pt[:, 0:cwi], ALU.mult, ALU.add)
                            P.tt(zin[:, ch, sl], zin[:, ch, sl], vx[:, 2 + ch, sl], ALU.mult)
                        else:
                            P.stt(taf[:, 0:cwi], zin[:, ch, sl], bias, pt[:, 0:cwi], ALU.mult, ALU.add)
                            P.tt(oh[:, ch, sl], taf[:, 0:cwi], vx[:, 4 + ch, sl], ALU.mult)
            for ch in range(2):
                P.dma(S["mix"][512 + ch * 128:512 + (ch + 1) * 128, :], oh[:, ch, :], eng=q2(ch))

        P.checkpoint("D" + tag)
        with P.scope():
            wo = P.sbuf("wo", [128, 8, D], BF16)
            wo64 = P.sbuf("wo64", [64, 8, D], BF16)
            P.dma(wo[:], wout_bf[:, :, :], eng="sp")
            P.dma(wo64[:], wout64_bf[:, :, :], eng="sp")
            m128 = P.sbuf("m128", [128, 4, n], BF16)
            m64 = P.sbuf("m64", [64, 8, n], BF16)
            for j, r0 in enumerate((0, 128, 512, 640)):
                P.dma(m128[:, j, :], S["mix"][r0:r0 + 128, :], eng=q2(j))
            for j in range(8):
                r0 = (256 + j * 64) if j < 4 else (768 + (j - 4) * 64)
                P.dma(m64[:, j, :], S["mix"][r0:r0 + 64, :], eng=q2(j))
            xin = [P.sbuf("exin%d" % i, [128, n], F32) for i in range(2)]
            P.dma(xin[0][:, :], x_src[b, 0:128, :], eng=q2(0))
            for i in range(8):
                xi = xin[i % 2]
                if i + 1 < 8:
                    P.dma(xin[(i + 1) % 2][:, :], x_src[b, (i + 1) * 128:(i + 2) * 128, :], eng=q2(i + 1))
                for c in range(ncw):
                    sl = slice(c * cw, (c + 1) * cw)
                    pt = ps[(i * ncw + c) % 2]
                    osl = slice(i * 128, (i + 1) * 128)
                    mlist = []
                    for j, kc in enumerate((0, 1, 4, 5)):
                        mlist.append((wo[:, kc, osl], m128[:, j, sl]))
                    for j in range(8):
                        mlist.append((wo64[:, j, osl], m64[:, j, sl]))
                    for mi, (lh, rh) in enumerate(mlist):
                        P.mm(pt[:, 0:cw], lh, rh, start=(mi == 0), stop=(mi == len(mlist) - 1))
                    P.stt(xi[:, sl], pt[:, 0:cw], modT[:, 16 + i, bcol:bcol + 1], xi[:, sl], ALU.mult, ALU.add)
                P.dma(S["xmid"][i * 128:(i + 1) * 128, :], xi[:, :], eng=q2(i))

        P.checkpoint("E" + tag)
        with P.scope():
            h2 = P.sbuf("h2", [128, 8, n], BF16)
            norm_mod(h2, lambda sl: S["xmid"][:, sl].re("(k p) n -> p k n", p=128), n, "n2g", 24, 32, bcol)
            gm = P.sbuf("gm", [16, n], F32)
            P.checkpoint("F1" + tag)
            with P.scope():
                wr = LW["wr"]
                lg = ps[0]
                for t in range(nt):
                    for k in range(8):
                        P.mm(lg[:, t * 16:(t + 1) * 16], h2[:, k, t * 128:(t + 1) * 128], wr[:, k, :],
                             start=(k == 0), stop=(k == 7))
                aff = P.sbuf("aff", [128, nt, 16], F32)
                mx = P.sbuf("affmx", [128, nt], F32)
                lv = lg[:, 0:nt * 16].re("p (t e) -> p t e", e=16)
                P.reduce(mx[:], lv, ALU.max)
                P.tt(aff[:], lv, mx[:].re("p (t o) -> p t o", o=1).bc([128, nt, 16]), ALU.subtract)
                P.act(aff[:], aff[:], AF.Exp)
                P.reduce(mx[:], aff[:], ALU.add)
                P.recip(mx[:], mx[:])
                P.tt(aff[:], aff[:], mx[:].re("p (t o) -> p t o", o=1).bc([128, nt, 16]), ALU.mult)
                affT = P.sbuf("affT", [16, n], F32)
                for t in range(nt):
                    pt = ps[1 + (t // 4) % 2]
                    P.transpose(pt[0:16, (t % 4) * 128:(t % 4 + 1) * 128], aff[:, t, :], IDF)
                    if t % 4 == 3 or t == nt - 1:
                        t0 = (t // 4) * 4
                        wdt = (t - t0 + 1) * 128
                        evac(affT[:, t0 * 128:t0 * 128 + wdt], pt[0:16, 0:wdt])
                work = P.sbuf("tkwork", [16, n], F32)
                mx8 = P.sbuf("tkmx8", [16, 8], F32)
                cap = n // 8
                cur = affT
                for it in range(cap // 8):
                    P.op("dve", (lambda c_: (lambda e: e.max(out=mx8.h[:], in_=c_.h[:])))(cur),
                         reads=[cur[:]], writes=[mx8[:]])
                    P.op("dve", (lambda c_: (lambda e: e.match_replace(out=work.h[:], in_to_replace=mx8.h[:],
                                                                        in_values=c_.h[:], imm_value=-1.0)))(cur),
                         reads=[cur[:], mx8[:]], writes=[work[:]])
                    cur = work
                P.ts(work[:], work[:], 0.0, ALU.is_lt)
                P.tt(gm[:], work[:], affT[:], ALU.mult)
            if debug:
                P.dma(dbg_gm[tag][:, :], gm[:, :])
            P.checkpoint("F2" + tag)
            gmb = P.sbuf("gmb", [16, n], BF16)
            P.copy(gmb[:], gm[:])
            P.checkpoint("F2b" + tag)
            with P.scope():
                yacc = P.sbuf("yacc", [128, 8, n], F32)
                with P.scope():
                    sel = P.sbuf("sel", [16, 16, 128], BF16)
                    P.dma(sel[:], sel_in[:, :, :])
                    wgu = [P.sbuf("wgu%d" % i, [128, 8, 1024], BF16) for i in range(2)]
                    wdn = [P.sbuf("wdn%d" % i, [128, 4, D], BF16) for i in range(2)]
                    gbc = [P.sbuf("gbc%d" % i, [128, cw], F32) for i in range(2)]
                    sa = P.sbuf("sa", [128, cw], F32)
                    hid = [P.sbuf("hid%d" % i, [128, 4, cw], BF16) for i in range(2)]
                    for e in range(nexp):
                        wg_ = wgu[e % 2]
                        wd_ = wdn[e % 2]
                        P.dma(wg_[:], wgu_bf[e % 2][e // 2, :, :, :], eng=q2(e))
                        P.dma(wd_[:], wdn_bf[e % 2][e // 2, :, :, :], eng=q2(e + 1))
                        for c in range(ncw):
                            sl = slice(c * cw, (c + 1) * cw)
                            gb = gbc[c % 2]
                            P.mm(ps[6][:, 0:cw], sel[:, e, :], gmb[:, sl])
                            P.copy(gb[:], ps[6][:, 0:cw], eng="act")
                            hd = hid[c % 2]
                            for j in range(4):
                                pa = ps[(j % 2) * 2]
                                pu = ps[(j % 2) * 2 + 1]
                                for k in range(8):
                                    P.mm(pa[:, 0:cw], wg_[:, k, j * 128:(j + 1) * 128], h2[:, k, sl],
                                         start=(k == 0), stop=(k == 7))
                                for k in range(8):
                                    P.mm(pu[:, 0:cw], wg_[:, k, 512 + j * 128:512 + (j + 1) * 128], h2[:, k, sl],
                                         start=(k == 0), stop=(k == 7))
                                P.act(sa[:], pa[:, 0:cw], AF.Silu)
                                P.tt(sa[:], sa[:], gb[:], ALU.mult)
                                P.tt(hd[:, j, :], pu[:, 0:cw], sa[:], ALU.mult)
                            for i in range(8):
                                py = ps[4 + i % 2]
                                for j in range(4):
                                    P.mm(py[:, 0:cw], wd_[:, j, i * 128:(i + 1) * 128], hd[:, j, :],
                                         start=(j == 0), stop=(j == 3))
                                if e == 0:
                                    evac(yacc[:, i, sl], py[:, 0:cw])
                                else:
                                    P.tt(yacc[:, i, sl], yacc[:, i, sl], py[:, 0:cw], ALU.add)
                xin = [P.sbuf("fxin%d" % i, [128, n], F32) for i in range(2)]
                P.dma(xin[0][:, :], S["xmid"][0:128, :], eng=q2(0))
                for i in range(8):
                    xi = xin[i % 2]
                    if i + 1 < 8:
                        P.dma(xin[(i + 1) % 2][:, :], S["xmid"][(i + 1) * 128:(i + 2) * 128, :], eng=q2(i + 1))
                    P.stt(xi[:, :], yacc[:, i, :], modT[:, 40 + i, bcol:bcol + 1], xi[:, :], ALU.mult, ALU.add)
                    r = P.dma(x_dst[b, i * 128:(i + 1) * 128, :], xi[:, :], eng=q2(i))
                    if last and not is_ctx:
                        final.append(r)

    final = []
    try:
        for l in range(depth):
            last = (l == DEPTH - 1)
            precast_dense(l)
            precast_experts(l)
            layer_prologue(l)
            P.checkpoint("prologue")
            filter_gen(l, SEQ, "L")
            P.checkpoint("filtL")
            if not last:
                filter_gen(l, NCTX, "C")
                P.checkpoint("filtC")
            for b in range(nb):
                with P.scope():
                    KV["kg"] = P.sbuf("kg_all", [128, SEQ + NCTX], BF16)
                    KV["vg"] = P.sbuf("vg_all", [128, 18, 2, 65], BF16)
                    KV["km"] = P.sbuf("km_all", [96, 4, SEQ + NCTX], BF16)
                    KV["vm"] = P.sbuf("vm_all", [128, 18, 4, 65], BF16)
                    P.memset(KV["vg"][:, :, :, 64:65], 1.0)
                    P.memset(KV["vm"][:, :, :, 64:65], 1.0)
                    seq_front(l, b, NCTX, True, last)
                    P.checkpoint("frontC")
                    seq_front(l, b, SEQ, False, last)
                    P.checkpoint("frontL")
                seq_back(l, b, SEQ, False, last)
                P.checkpoint("backL")
                if not last:
                    seq_back(l, b, NCTX, True, last)
                    P.checkpoint("backC")
    except StopBuild:
        lastrec = {}
        for r in P.recs:
            if r.is_dma:
                lastrec[("d", r.sem_idx)] = r
            else:
                lastrec[r.eng] = r
        final = list(lastrec.values())
    P.finish(final_waits=final)
    return nc, P


def prep_shared(inp):
    C = get_consts()
    sh = dict(C)
    sh["vecs"] = np.stack([pack_vecs(inp, l) for l in range(DEPTH)], 0)
    sh["w_mod"] = np.ascontiguousarray(inp["w_mod"], np.float32)
    w_in = np.asarray(inp["w_in"], np.float32)
    perm = np.arange(1952)
    q0 = 416
    perm[q0:q0 + 256] = np.concatenate([q0 + np.arange(0, 64), q0 + np.arange(128, 192),
                                        q0 + np.arange(64, 128), q0 + np.arange(192, 256)])
    w_in = w_in[:, :, perm]
    wflat = np.zeros((DEPTH, 128, 8 * 1952), np.float32)
    for l in range(DEPTH):
        wp = w_in[l].reshape(8, 128, 1952).transpose(1, 0, 2)
        for c0, m in W_IN_GROUPS:
            wflat[l, :, 8 * c0:8 * (c0 + m)] = wp[:, :, c0:c0 + m].reshape(128, 8 * m)
    sh["w_in"] = wflat
    w_out = np.asarray(inp["w_out"], np.float32)
    sh["w_out"] = np.ascontiguousarray(w_out.reshape(DEPTH, 8, 128, D).transpose(0, 2, 1, 3))
    wo64 = np.concatenate([w_out[:, 256:512, :], w_out[:, 768:1024, :]], 1)
    sh["w_out64"] = np.ascontiguousarray(wo64.reshape(DEPTH, 8, 64, D).transpose(0, 2, 1, 3))
    pw = np.asarray(inp["pool_w"], np.float32)
    bd = np.zeros((DEPTH, 2, 128, 128), np.float32)
    for l in range(DEPTH):
        for g in range(4):
            o = (g % 2) * 64
            bd[l, g // 2, o:o + 64, o:o + 64] = pw[l, g]
    sh["poolbd"] = bd
    for k in ("hy_f_w1", "hy_f_w2", "hy_f_w3", "mla_w_uq", "router_w"):
        sh[k] = np.ascontiguousarray(inp[k], np.float32)
    wg = np.asarray(inp["exp_w_gate"], np.float32).reshape(DEPTH, NEXP, 8, 128, 512)
    wu = np.asarray(inp["exp_w_up"], np.float32).reshape(DEPTH, NEXP, 8, 128, 512)
    sh["exp_w_gu"] = np.ascontiguousarray(np.concatenate([wg, wu], -1).transpose(0, 1, 3, 2, 4))
    wd = np.asarray(inp["exp_w_down"], np.float32).reshape(DEPTH, NEXP, 4, 128, D)
    sh["exp_w_dn"] = np.ascontiguousarray(wd.transpose(0, 1, 3, 2, 4))
    wukv = np.asarray(inp["mla_w_ukv"], np.float32).reshape(DEPTH, 128, 4, 128)
    wk = np.zeros((DEPTH, 128, 4, 96), np.float32)
    wk[:, :, :, 0:64] = wukv[:, :, :, 0:64]
    sh["wukv_k"] = wk
    sh["wukv_v"] = np.ascontiguousarray(wukv[:, :, :, 64:128].reshape(DEPTH, 128, 256))
    return sh


def make_in_maps(inp, nb, n_cores=8):
    sh = prep_shared(inp)
    x = np.asarray(inp["x"], np.float32)
    ctx = np.asarray(inp["ctx"], np.float32)
    c = np.asarray(inp["c"], np.float32)
    c_ctx = np.asarray(inp["c_ctx"], np.float32)
    in_maps = []
    for core in range(n_cores):
        bs = slice(core * nb, (core + 1) * nb)
        m = dict(sh)
        m["xT"] = np.ascontiguousarray(x[bs].transpose(0, 2, 1))
        m["ctxT"] = np.ascontiguousarray(ctx[bs].transpose(0, 2, 1))
        cc = np.concatenate([c[bs], c_ctx[None, :]], 0)
        m["cT"] = np.ascontiguousarray(cc.reshape(nb + 1, 8, 128).transpose(2, 1, 0))
        in_maps.append(m)
    return in_maps


def kernel(**inp):
    inp = {k: np.asarray(v) for k, v in inp.items()}
    n_cores = 8
    in_maps = make_in_maps(inp, NB, n_cores)
    nc, _ = build_program()
    res = run_bass_kernel_spmd(nc, in_maps, core_ids=list(range(n_cores)))
    out = np.empty((32, SEQ, D), np.float32)
    for core in range(n_cores):
        oT = np.asarray(res.results[core]["outT"], np.float32)
        out[core * NB:(core + 1) * NB] = oT.transpose(0, 2, 1)
    return out
```
